# Optimizing a Trainium2 kernel written in Bass

```python
import math
import jax, jax.numpy as jnp
from jax import lax
import numpy as np

D_MODEL = 1024
BATCH = 8
SEQ = 2048
DEPTH = 2

EPS = 1e-6
D_FF = 2816
Q_BLOCK = 128
NUM_BUCKETS = 32
MAX_DISTANCE = 1024
ROPE_THETA = 10000.0
NEG_BIG = -1e30

DIFF_HEADS = 4
DIFF_QK_DIM = 32
DIFF_V_DIM = 64
DIL_HEADS = 6
DIL_HEAD_DIM = 64
DIL_PATTERNS = ((128, 1), (512, 4), (2048, 16))
MLA_HEADS = 6
MLA_Q_RANK = 256
MLA_KV_RANK = 128
MLA_NOPE_DIM = 64
MLA_ROPE_DIM = 32
MLA_V_DIM = 64

DIFF_WIDTH = DIFF_HEADS * DIFF_V_DIM
DIL_WIDTH = DIL_HEADS * DIL_HEAD_DIM
MLA_WIDTH = MLA_HEADS * MLA_V_DIM
MIX_WIDTH = DIFF_WIDTH + DIL_WIDTH + MLA_WIDTH
BIAS_HEADS = DIFF_HEADS + DIL_HEADS
IN_SIZES = (DIFF_HEADS * 2 * DIFF_QK_DIM, DIFF_HEADS * 2 * DIFF_QK_DIM, DIFF_WIDTH,
            DIL_WIDTH, DIL_WIDTH, DIL_WIDTH,
            MLA_Q_RANK, MLA_KV_RANK + MLA_ROPE_DIM)
IN_WIDTH = sum(IN_SIZES)

kernel_name = 'hybrid_parallel_diff_dilated_mla_encoder'


def rms_norm(x, g):
    xf = x.astype(jnp.float32)
    y = xf * lax.rsqrt(jnp.mean(xf * xf, axis=-1, keepdims=True) + EPS)
    return (y * g.astype(jnp.float32)).astype(x.dtype)


def swiglu(x, wg, wu, wd):
    return (jax.nn.silu(x @ wg) * (x @ wu)) @ wd


def t5_bucket(rel):
    half = NUM_BUCKETS // 2
    max_exact = half // 2
    n = jnp.abs(rel)
    nf = jnp.maximum(n, 1).astype(jnp.float32)
    large = max_exact + (jnp.log(nf / max_exact) / math.log(MAX_DISTANCE / max_exact)
                         * (half - max_exact)).astype(jnp.int32)
    large = jnp.minimum(large, half - 1)
    return jnp.where(rel > 0, half, 0) + jnp.where(n < max_exact, n, large)


def rope(x, pos):
    half = x.shape[-1] // 2
    inv = ROPE_THETA ** (-jnp.arange(half, dtype=jnp.float32) / half)
    ang = pos.astype(jnp.float32)[:, None] * inv[None, :]
    cos, sin = jnp.cos(ang), jnp.sin(ang)
    x1, x2 = x[..., :half], x[..., half:]
    return jnp.concatenate([x1 * cos - x2 * sin, x1 * sin + x2 * cos], axis=-1).astype(x.dtype)


def diff_attention(q, k, v, lam, lambda_init, subln_g, bias_table):
    B, H, S, _, dk = q.shape
    dv = v.shape[-1]
    nblk = S // Q_BLOCK
    qb = q.reshape(B, H, nblk, Q_BLOCK, 2, dk).transpose(2, 0, 1, 3, 4, 5)
    kpos = jnp.arange(S)
    scale = dk ** -0.5
    table = bias_table.astype(jnp.float32)

    def block(args):
        qi, start = args
        rel = kpos[None, :] - (start + jnp.arange(Q_BLOCK))[:, None]
        bias = jnp.moveaxis(table[t5_bucket(rel)], -1, 0)
        logits = jnp.einsum('bhqcd,bhkcd->cbhqk', qi, k).astype(jnp.float32) * scale + bias
        p = jax.nn.softmax(logits, axis=-1)
        a = p[0] - lam * p[1]
        return jnp.einsum('bhqk,bhkd->bhqd', a.astype(v.dtype), v)

    starts = jnp.arange(nblk, dtype=jnp.int32) * Q_BLOCK
    out = lax.map(block, (qb, starts))
    out = out.transpose(1, 2, 0, 3, 4).reshape(B, H, S, dv)
    return rms_norm(out, subln_g) * (1.0 - lambda_init)


def dilated_branch(q, k, v, window, dil, bias_table):
    B, H, S, d = q.shape
    R = window // (2 * dil)
    L = S // dil
    nb = -(-L // R)
    Lp = nb * R

    def to_sub(t):
        return t.reshape(B, H, L, dil, d).transpose(0, 1, 3, 2, 4)

    pad_q = ((0, 0), (0, 0), (0, 0), (0, Lp - L), (0, 0))
    pad_kv = ((0, 0), (0, 0), (0, 0), (R, Lp - L + R), (0, 0))
    qb = jnp.pad(to_sub(q), pad_q).reshape(B, H, dil, nb, R, d)

    def bands(t):
        tb = jnp.pad(to_sub(t), pad_kv).reshape(B, H, dil, nb + 2, R, d)
        return jnp.concatenate([tb[:, :, :, :-2], tb[:, :, :, 1:-1], tb[:, :, :, 2:]], axis=4)

    kb, vb = bands(k), bands(v)
    qi = jnp.arange(nb)[:, None] * R + jnp.arange(R)[None, :]
    kj = jnp.arange(nb)[:, None] * R - R + jnp.arange(3 * R)[None, :]
    rel = kj[:, None, :] - qi[:, :, None]
    valid = (jnp.abs(rel) <= R) & (kj[:, None, :] >= 0) & (kj[:, None, :] < L)
    bias = jnp.moveaxis(bias_table.astype(jnp.float32)[t5_bucket(rel * dil)], -1, 0)
    logits = jnp.einsum('bhrnqd,bhrnkd->bhrnqk', qb, kb).astype(jnp.float32) * (d ** -0.5)
    logits = jnp.where(valid, logits + bias[None, :, None], NEG_BIG)
    m = jnp.max(logits, axis=-1, keepdims=True)
    e = jnp.exp(logits - m)
    s = jnp.sum(e, axis=-1, keepdims=True)
    out = jnp.einsum('bhrnqk,bhrnkd->bhrnqd', (e / s).astype(v.dtype), vb)
    lse = (m + jnp.log(s))[..., 0]
    out = out.reshape(B, H, dil, Lp, d)[:, :, :, :L].transpose(0, 1, 3, 2, 4).reshape(B, H, S, d)
    lse = lse.reshape(B, H, dil, Lp)[:, :, :, :L].transpose(0, 1, 3, 2).reshape(B, H, S)
    return out, lse


def dilated_attention(q, k, v, bias_table):
    outs, lses = [], []
    for window, dil in DIL_PATTERNS:
        o, l = dilated_branch(q, k, v, window, dil, bias_table)
        outs.append(o)
        lses.append(l)
    w = jax.nn.softmax(jnp.stack(lses, axis=0), axis=0)
    return jnp.sum(w[..., None] * jnp.stack(outs, axis=0).astype(jnp.float32), axis=0).astype(q.dtype)


def dense_attention(q, k, v):
    B, H, S, dq = q.shape
    nblk = S // Q_BLOCK
    qb = q.reshape(B, H, nblk, Q_BLOCK, dq).transpose(2, 0, 1, 3, 4)
    scale = dq ** -0.5

    def block(qi):
        logits = jnp.einsum('bhqd,bhkd->bhqk', qi, k).astype(jnp.float32) * scale
        p = jax.nn.softmax(logits, axis=-1)
        return jnp.einsum('bhqk,bhkd->bhqd', p.astype(v.dtype), v)

    out = lax.map(block, qb)
    return out.transpose(1, 2, 0, 3, 4).reshape(B, H, S, v.shape[-1])


def mla_mixer(q_down, kv_down, q_norm_g, q_up, kv_norm_g, kv_up, qn_g, kn_g):
    B, S, _ = q_down.shape
    qk_dim = MLA_NOPE_DIM + MLA_ROPE_DIM
    q = (rms_norm(q_down, q_norm_g) @ q_up).reshape(B, S, MLA_HEADS, qk_dim)
    c_kv = rms_norm(kv_down[..., :MLA_KV_RANK], kv_norm_g)
    k_rope = kv_down[..., MLA_KV_RANK:]
    kv = (c_kv @ kv_up).reshape(B, S, MLA_HEADS, MLA_NOPE_DIM + MLA_V_DIM)
    k = jnp.concatenate([kv[..., :MLA_NOPE_DIM],
                         jnp.broadcast_to(k_rope[:, :, None, :], (B, S, MLA_HEADS, MLA_ROPE_DIM))], axis=-1)
    v = kv[..., MLA_NOPE_DIM:].transpose(0, 2, 1, 3)
    q = rms_norm(q, qn_g).transpose(0, 2, 1, 3)
    k = rms_norm(k, kn_g).transpose(0, 2, 1, 3)
    pos = jnp.arange(S)
    q = jnp.concatenate([q[..., :MLA_NOPE_DIM], rope(q[..., MLA_NOPE_DIM:], pos)], axis=-1)
    k = jnp.concatenate([k[..., :MLA_NOPE_DIM], rope(k[..., MLA_NOPE_DIM:], pos)], axis=-1)
    out = dense_attention(q, k, v)
    return out.transpose(0, 2, 1, 3).reshape(B, S, MLA_WIDTH)


def setup_inputs(seed: int = 0) -> dict:
    key = jax.random.key(seed)
    keys = iter(jax.random.split(key, 32))
    L = DEPTH

    def w(shape, fan_in):
        return jax.random.normal(next(keys), shape, jnp.float32) * fan_in ** -0.5

    def g(shape):
        return 1.0 + 0.05 * jax.random.normal(next(keys), shape, jnp.float32)

    return {
        'x': jax.random.normal(next(keys), (BATCH, SEQ, D_MODEL), jnp.float32),
        'rel_bias': 0.5 * jax.random.normal(next(keys), (NUM_BUCKETS, BIAS_HEADS), jnp.float32),
        'ffn1_norm': g((L, D_MODEL)),
        'ffn1_wg': w((L, D_MODEL, D_FF), D_MODEL),
        'ffn1_wu': w((L, D_MODEL, D_FF), D_MODEL),
        'ffn1_wd': w((L, D_FF, D_MODEL), D_FF),
        'mix_norm': g((L, D_MODEL)),
        'w_in': w((L, D_MODEL, IN_WIDTH), D_MODEL),
        'diff_q_norm': g((L, DIFF_QK_DIM)),
        'diff_k_norm': g((L, DIFF_QK_DIM)),
        'diff_lambda': 0.1 * jax.random.normal(next(keys), (L, 4, DIFF_QK_DIM), jnp.float32),
        'diff_subln': g((L, DIFF_V_DIM)),
        'dil_q_norm': g((L, DIL_HEAD_DIM)),
        'dil_k_norm': g((L, DIL_HEAD_DIM)),
        'mla_q_norm': g((L, MLA_Q_RANK)),
        'mla_q_up': w((L, MLA_Q_RANK, MLA_HEADS * (MLA_NOPE_DIM + MLA_ROPE_DIM)), MLA_Q_RANK),
        'mla_kv_norm': g((L, MLA_KV_RANK)),
        'mla_kv_up': w((L, MLA_KV_RANK, MLA_HEADS * (MLA_NOPE_DIM + MLA_V_DIM)), MLA_KV_RANK),
        'mla_qn': g((L, MLA_NOPE_DIM + MLA_ROPE_DIM)),
        'mla_kn': g((L, MLA_NOPE_DIM + MLA_ROPE_DIM)),
        'w_o': w((L, MIX_WIDTH, D_MODEL), MIX_WIDTH),
        'ffn2_norm': g((L, D_MODEL)),
        'ffn2_wg': w((L, D_MODEL, D_FF), D_MODEL),
        'ffn2_wu': w((L, D_MODEL, D_FF), D_MODEL),
        'ffn2_wd': w((L, D_FF, D_MODEL), D_FF),
    }


def reference(x, rel_bias, ffn1_norm, ffn1_wg, ffn1_wu, ffn1_wd, mix_norm, w_in,
              diff_q_norm, diff_k_norm, diff_lambda, diff_subln, dil_q_norm, dil_k_norm,
              mla_q_norm, mla_q_up, mla_kv_norm, mla_kv_up, mla_qn, mla_kn, w_o,
              ffn2_norm, ffn2_wg, ffn2_wu, ffn2_wd):
    B, S, _ = x.shape
    split_points = [int(p) for p in np.cumsum(IN_SIZES)[:-1]]
    diff_bias = rel_bias[:, :DIFF_HEADS]
    dil_bias = rel_bias[:, DIFF_HEADS:]
    for l in range(DEPTH):
        lambda_init = 0.8 - 0.6 * math.exp(-0.3 * (l + 1))
        x = x + 0.5 * swiglu(rms_norm(x, ffn1_norm[l]), ffn1_wg[l], ffn1_wu[l], ffn1_wd[l])
        h = rms_norm(x, mix_norm[l])
        dq, dk, dv, lq, lk, lv, mq, mkv = jnp.split(h @ w_in[l], split_points, axis=-1)
        dq = rms_norm(dq.reshape(B, S, DIFF_HEADS, 2, DIFF_QK_DIM), diff_q_norm[l]).transpose(0, 2, 1, 3, 4)
        dk = rms_norm(dk.reshape(B, S, DIFF_HEADS, 2, DIFF_QK_DIM), diff_k_norm[l]).transpose(0, 2, 1, 3, 4)
        dv = dv.reshape(B, S, DIFF_HEADS, DIFF_V_DIM).transpose(0, 2, 1, 3)
        lp = diff_lambda[l].astype(jnp.float32)
        lam = jnp.exp(jnp.sum(lp[0] * lp[1])) - jnp.exp(jnp.sum(lp[2] * lp[3])) + lambda_init
        out_a = diff_attention(dq, dk, dv, lam, lambda_init, diff_subln[l], diff_bias)
        out_a = out_a.transpose(0, 2, 1, 3).reshape(B, S, DIFF_WIDTH)
        lq = rms_norm(lq.reshape(B, S, DIL_HEADS, DIL_HEAD_DIM), dil_q_norm[l]).transpose(0, 2, 1, 3)
        lk = rms_norm(lk.reshape(B, S, DIL_HEADS, DIL_HEAD_DIM), dil_k_norm[l]).transpose(0, 2, 1, 3)
        lv = lv.reshape(B, S, DIL_HEADS, DIL_HEAD_DIM).transpose(0, 2, 1, 3)
        out_b = dilated_attention(lq, lk, lv, dil_bias).transpose(0, 2, 1, 3).reshape(B, S, DIL_WIDTH)
        out_c = mla_mixer(mq, mkv, mla_q_norm[l], mla_q_up[l], mla_kv_norm[l], mla_kv_up[l], mla_qn[l], mla_kn[l])
        mix = jnp.concatenate([out_a, out_b, out_c], axis=-1)
        x = x + mix @ w_o[l]
        x = x + 0.5 * swiglu(rms_norm(x, ffn2_norm[l]), ffn2_wg[l], ffn2_wu[l], ffn2_wd[l])
    return x
```

```python
import math
import contextlib
import numpy as np
import concourse.bass as bass
import concourse.mybir as mybir
from concourse.bass_utils import run_bass_kernel_spmd

F32 = mybir.dt.float32
BF16 = mybir.dt.bfloat16
AF = mybir.ActivationFunctionType
ALU = mybir.AluOpType

D = 1024; S = 2048; DFF = 2816; L = 2; NC8 = 8; TB = 512; NTB = 4
EPS = 1e-6
INW = 2336
NEG = -30000.0
GW = 3968
WIN = 2432
C_ID = 0; C_ONES = 128; C_G32 = 256; C_G64 = 384; C_M65 = 512; C_RT = 640; C_SEL = 768; NCST = 896
NPK = 40


class Res:
    __slots__ = ("name", "w", "r")

    def __init__(self, name):
        self.name = name; self.w = None; self.r = {}


class K:
    def __init__(self, nc, es, n_dma_sems=8):
        self.nc = nc
        self.eng = {"pe": nc.tensor, "dve": nc.vector, "act": nc.scalar, "pool": nc.gpsimd, "sp": nc.sync}
        self.sem = {}; self.cnt = {}
        for e in self.eng:
            self.sem[e] = es.enter_context(nc.semaphore("s_" + e)); self.cnt[e] = 0
        self.seen = {e: {} for e in self.eng}
        self.pend = {e: ([], []) for e in self.eng}
        self.dq = {}
        for q in ("sp", "pool"):
            lst = []
            for i in range(n_dma_sems):
                nm = "d_%s%d" % (q, i)
                self.sem[nm] = es.enter_context(nc.semaphore(nm)); self.cnt[nm] = 0
                lst.append(nm)
            self.dq[q] = [lst, 0]
        self.nwait = 0; self.ninst = 0
        self.es = es; self.slot_sem = {}

    def _sem_for(self, q, writes):
        if not writes:
            lst, i = self.dq[q]
            self.dq[q][1] = i + 1
            return lst[i % len(lst)]
        key = writes[0].name
        if key not in self.slot_sem:
            nm = "ds_%d" % len(self.slot_sem)
            self.sem[nm] = self.es.enter_context(self.nc.semaphore(nm)); self.cnt[nm] = 0
            self.slot_sem[key] = nm
        return self.slot_sem[key]

    def _wait(self, e, f, c):
        if c <= 0 or self.seen[e].get(f, 0) >= c:
            return
        self.eng[e].wait_ge(self.sem[f], c)
        self.seen[e][f] = c; self.nwait += 1

    def op(self, e, fn, reads=(), writes=(), inc=True):
        need = {}
        for r in reads:
            if r.w is not None:
                f, c = r.w
                if f == e and e == "pe":
                    continue
                if c > need.get(f, 0): need[f] = c
        for w in writes:
            if w.w is not None:
                f, c = w.w
                if f != e and c > need.get(f, 0): need[f] = c
            for f, c in w.r.items():
                if f != e and c > need.get(f, 0): need[f] = c
        for f, c in need.items():
            self._wait(e, f, c)
        ins = fn()
        self.ninst += 1
        pr, pw = self.pend[e]
        pr.extend(reads); pw.extend(writes)
        if inc:
            ins.then_inc(self.sem[e], 1)
            self.cnt[e] += 1
            c = self.cnt[e]
            for r in pr: r.r[e] = c
            for w in pw:
                w.w = (e, c); w.r = {}
            self.pend[e] = ([], [])
        return ins

    def dma(self, q, out, in_, reads=(), writes=()):
        s = self._sem_for(q, list(writes))
        need = {}
        for r in reads:
            if r.w is not None:
                f, c = r.w
                if c > need.get(f, 0): need[f] = c
        for w in writes:
            if w.w is not None:
                f, c = w.w
                if c > need.get(f, 0): need[f] = c
            for f, c in w.r.items():
                if c > need.get(f, 0): need[f] = c
        for f, c in need.items():
            self._wait(q, f, c)
        self._wait(q, s, self.cnt[s])
        self.eng[q].dma_start(out=out, in_=in_).then_inc(self.sem[s], 16)
        self.cnt[s] += 16
        c = self.cnt[s]
        for r in reads: r.r[s] = c
        for w in writes:
            w.w = (s, c); w.r = {}
        self.ninst += 1

    def barrier(self):
        for e in self.eng:
            for f in self.cnt:
                if f != e: self._wait(e, f, self.cnt[f])


class Arena:
    def __init__(self, nc, lo, hi):
        self.nc = nc; self.free = [(lo, hi)]; self.used = {}; self.n = 0; self.peak = 0; self.hi = hi

    def alloc(self, name, shape, dt):
        esz = 4 if dt == F32 else 2
        size = esz
        for d in shape[1:]: size *= d
        size = (size + 63) // 64 * 64
        for i, (a, b) in enumerate(self.free):
            if b - a >= size:
                self.free[i] = (a + size, b)
                if self.free[i][0] == self.free[i][1]: self.free.pop(i)
                self.n += 1
                t = self.nc.alloc_sbuf_tensor_at("%s_%d" % (name, self.n), list(shape), dt, offset=a)
                self.used[id(t)] = (a, a + size, t)
                self.peak = max(self.peak, a + size)
                return t
        raise RuntimeError("SBUF arena full allocating %s %s (free=%s)" % (name, shape, self.free))

    def release(self, ts):
        for t in ts:
            a, b, _ = self.used.pop(id(t))
            self.free.append((a, b))
        self.free.sort()
        m = []
        for a, b in self.free:
            if m and m[-1][1] == a: m[-1] = (m[-1][0], b)
            else: m.append((a, b))
        self.free = m


class Scope:
    def __init__(self, ar): self.ar = ar; self.ts = []
    def __call__(self, name, shape, dt):
        t = self.ar.alloc(name, shape, dt); self.ts.append(t); return t
    def close(self):
        self.ar.release(self.ts); self.ts = []


def _t5_bucket(rel):
    rel = np.asarray(rel, dtype=np.int64)
    n = np.abs(rel)
    nf = np.maximum(n, 1).astype(np.float64)
    large = 8 + np.floor(np.log(nf / 8.0) / math.log(128.0) * 8.0 + 1e-9).astype(np.int64)
    large = np.where(n < 8, 0, large)
    large = np.minimum(large, 15)
    return np.where(rel > 0, 16, 0) + np.where(n < 8, n, large)


def _rel_grid():
    kp = np.arange(128)[:, None]; j = np.arange(GW)[None, :]
    return kp - j + 1920


def _host_consts():
    cst = np.zeros((128, NCST), np.float32)
    cst[:, C_ID:C_ID + 128] = np.eye(128, dtype=np.float32)
    cst[:, C_ONES:C_ONES + 128] = 1.0
    for g in range(4):
        cst[g * 32:(g + 1) * 32, C_G32 + g * 32:C_G32 + (g + 1) * 32] = 1.0
    for g in range(2):
        cst[g * 64:(g + 1) * 64, C_G64 + g * 64:C_G64 + (g + 1) * 64] = 1.0
    cst[1:65, C_M65:C_M65 + 65] = 1.0
    for m in range(64, 80):
        cst[m + 16, C_RT + m] = -1.0
    for m in range(80, 96):
        cst[m - 16, C_RT + m] = 1.0
    for j in range(32):
        cst[j, C_SEL + 64 + j] = 1.0
    inv = (10000.0 ** (-(np.arange(16, dtype=np.float32) / np.float32(16)))).astype(np.float32)
    ang = (np.arange(S, dtype=np.float32)[None, :] * inv[:, None]).astype(np.float32)
    ropeC = np.ones((96, S), np.float32); ropeS = np.zeros((96, S), np.float32)
    ropeC[64:80] = np.cos(ang.astype(np.float64)); ropeC[80:96] = ropeC[64:80]
    ropeS[64:80] = np.sin(ang.astype(np.float64)); ropeS[80:96] = ropeS[64:80]
    rel = _rel_grid()
    cnt = (np.abs(rel) <= 64).astype(np.int64) + ((rel % 4 == 0) & (np.abs(rel) <= 256)) + ((rel % 16 == 0) & (np.abs(rel) <= 1024))
    logc = np.where(cnt > 0, np.log(np.maximum(cnt, 1).astype(np.float64)), NEG).astype(np.float32)
    return cst, ropeC, ropeS, logc


def _pack_small(inp):
    pk = np.zeros((128, L, NPK), np.float32)
    for l in range(L):
        for i, nm in enumerate(("ffn1_norm", "mix_norm", "ffn2_norm")):
            pk[:, l, i * 8:(i + 1) * 8] = inp[nm][l].reshape(8, 128).T
        pk[0:64, l, 24] = np.tile(inp["diff_q_norm"][l], 2)
        pk[0:64, l, 25] = np.tile(inp["diff_k_norm"][l], 2)
        pk[0:128, l, 26] = np.tile(inp["dil_q_norm"][l], 2)
        pk[0:128, l, 27] = np.tile(inp["dil_k_norm"][l], 2)
        pk[:, l, 28:30] = inp["mla_q_norm"][l].reshape(2, 128).T
        pk[:, l, 30] = inp["mla_kv_norm"][l]
        pk[0:96, l, 31] = inp["mla_qn"][l]
        pk[0:96, l, 32] = inp["mla_kn"][l]
        pk[1:65, l, 33] = inp["diff_subln"][l]
        pk[0:32, l, 34:38] = inp["diff_lambda"][l].T
    return pk


WEIGHT_SPECS = [
    ("ffn1_wg", [L, D, DFF]), ("ffn1_wu", [L, D, DFF]), ("ffn1_wd", [L, DFF, D]),
    ("ffn2_wg", [L, D, DFF]), ("ffn2_wu", [L, D, DFF]), ("ffn2_wd", [L, DFF, D]),
    ("w_in", [L, D, INW]), ("w_o", [L, D, D]),
    ("mla_q_up", [L, 256, 576]), ("mla_kv_up", [L, 128, 768]),
]


def build(stages, dbg=None):
    nc = bass.Bass("TRN2", target_bir_lowering=False)
    P = {}
    P["xT"] = nc.dram_tensor("xT", [D, S], F32, kind="ExternalInput").ap()
    for nm, shp in WEIGHT_SPECS:
        P[nm] = nc.dram_tensor(nm, shp, F32, kind="ExternalInput").ap()
    P["pk"] = nc.dram_tensor("pk", [128, L, NPK], F32, kind="ExternalInput").ap()
    P["cst"] = nc.dram_tensor("cst", [128, NCST], F32, kind="ExternalInput").ap()
    P["ropeC"] = nc.dram_tensor("ropeC", [96, S], F32, kind="ExternalInput").ap()
    P["ropeS"] = nc.dram_tensor("ropeS", [96, S], F32, kind="ExternalInput").ap()
    P["logc"] = nc.dram_tensor("logc", [128, GW], F32, kind="ExternalInput").ap()
    P["gbias"] = nc.dram_tensor("gbias", [10, 128, GW], F32, kind="ExternalInput").ap()
    yT = nc.dram_tensor("yT", [D, S], F32, kind="ExternalOutput").ap()

    with contextlib.ExitStack() as es:
        k = K(nc, es)
        ar = Arena(nc, 16512, 229376)
        glob = Scope(ar)

        x_sb = glob("x_sb", [128, NC8, S], F32)
        RX = [[Res("x%d_%d" % (c, t)) for t in range(NTB)] for c in range(NC8)]
        cb = glob("cb", [128, NCST], BF16); Rcb = Res("cb")
        cf = glob("cf", [128, 128], F32); Rcf = Res("cf")
        pk = glob("pk_sb", [128, L, NPK], F32); Rpk = Res("pk")
        epsb = glob("epsb", [128, 1], F32); Reps = Res("eps")
        ps = [es.enter_context(nc.psum_tensor("ps%d" % i, [128, 512], F32)) for i in range(8)]
        RP = [Res("ps%d" % i) for i in range(8)]

        def mm(out, lhsT, rhs, start, stop, reads, writes, inc=True):
            return k.op("pe", lambda: nc.tensor.matmul(out, lhsT=lhsT, rhs=rhs, start=start, stop=stop), reads, writes, inc)

        for c in range(NC8):
            k.dma("sp", x_sb[:, c, :], P["xT"][c * 128:(c + 1) * 128, :], writes=RX[c])
        k.dma("pool", cb[:], P["cst"][:, :], writes=[Rcb])
        k.dma("sp", cf[:], P["cst"][:, C_ONES:C_ONES + 128], writes=[Rcf])
        k.dma("sp", pk[:], P["pk"][:, :, :], writes=[Rpk])
        k.op("dve", lambda: nc.vector.memset(epsb[:], EPS), writes=[Reps])
        for l in range(L):
            lam_init = 0.8 - 0.6 * math.exp(-0.3 * (l + 1))
            for col, sc in ((24, 32 ** -0.5), (26, 64 ** -0.5), (31, 96 ** -0.5), (33, 1.0 - lam_init)):
                k.op("dve", lambda: nc.vector.tensor_scalar(
                    out=pk[:, l, col:col + 1], in0=pk[:, l, col:col + 1], scalar1=float(sc), scalar2=None,
                    op0=ALU.mult), reads=[Rpk], writes=[Rpk])

        ones_bf = cb[:, C_ONES:C_ONES + 128]
        ident_bf = cb[:, C_ID:C_ID + 128]

        def rmsnorm_h(h, RH, gcol, stat_bank):
            sc = Scope(ar)
            sq = [sc("nsq%d" % i, [128, TB], BF16) for i in range(4)]
            Rsq = [Res("nsq%d" % i) for i in range(4)]
            lnv = sc("nlnv", [128, TB], F32); Rln = Res("nlnv")
            rstd = [sc("nrstd%d" % i, [128, TB], F32) for i in range(2)]
            Rrs = [Res("nrstd%d" % i) for i in range(2)]
            n = 0
            for tb in range(NTB):
                tsl = slice(tb * TB, (tb + 1) * TB)
                for c in range(NC8):
                    i = n % 4; n += 1
                    k.op("act", lambda: nc.scalar.activation(out=sq[i][:], in_=x_sb[:, c, tsl], func=AF.Square),
                         reads=[RX[c][tb]], writes=[Rsq[i]])
                    mm(ps[stat_bank][:], ones_bf, sq[i][:], c == 0, c == NC8 - 1, [Rsq[i], Rcb], [RP[stat_bank]])
                r = tb % 2
                k.op("act", lambda: nc.scalar.activation(out=lnv[:], in_=ps[stat_bank][:], func=AF.Ln, scale=1.0 / D, bias=epsb[:, 0:1]),
                     reads=[RP[stat_bank], Reps], writes=[Rln])
                k.op("act", lambda: nc.scalar.activation(out=rstd[r][:], in_=lnv[:], func=AF.Exp, scale=-0.5),
                     reads=[Rln], writes=[Rrs[r]])
                for c in range(NC8):
                    k.op("dve", lambda: nc.vector.scalar_tensor_tensor(
                        out=h[:, c, tsl], in0=x_sb[:, c, tsl], scalar=gcol[:, c:c + 1], in1=rstd[r][:],
                        op0=ALU.mult, op1=ALU.mult), reads=[RX[c][tb], Rrs[r], Rpk], writes=[RH[c][tb]])
            k.barrier()
            sc.close()

        def ffn(l, which):
            wg = P["ffn%d_wg" % which]; wu = P["ffn%d_wu" % which]; wd = P["ffn%d_wd" % which]
            gcol = pk[:, l, (0 if which == 1 else 16):(8 if which == 1 else 24)]
            st = Scope(ar)
            h = st("f_h", [128, NC8, S], BF16)
            RH = [[Res("h%d_%d" % (c, t)) for t in range(NTB)] for c in range(NC8)]
            act = st("f_act", [128, 12, S], BF16)
            RA = [[Res("a%d_%d" % (c, t)) for t in range(NTB)] for c in range(12)]
            wgu = [st("f_wgu%d" % i, [128, 2, NC8, 256], BF16) for i in range(2)]
            Rwgu = [Res("wgu%d" % i) for i in range(2)]
            wdt = [st("f_wd%d" % i, [128, 12, 256], BF16) for i in range(2)]
            Rwd = [Res("wd%d" % i) for i in range(2)]
            sg = [st("f_sg%d" % i, [128, TB], F32) for i in range(2)]
            Rsg = [Res("sg%d" % i) for i in range(2)]
            halves = [(0, 6), (6, 11)]
            seq = []
            for hi, (g0, g1) in enumerate(halves):
                for g in range(g0, g1): seq.append(("up", hi, g))
                for dg in range(4): seq.append(("dn", hi, dg))
            cnt = {"up": 0, "dn": 0, "gu": 0, "d": 0}
            slot_of = {}

            def load(item):
                kind, hi, g = item
                g0, g1 = halves[hi]
                if kind == "up":
                    s = cnt["up"] % 2; cnt["up"] += 1
                    slot_of[item] = s
                    for j, w in enumerate((wg, wu)):
                        k.dma("pool", wgu[s][:, j, :, :], w[l, :, g * 256:(g + 1) * 256].rearrange("(c p) f -> p c f", p=128),
                              writes=[Rwgu[s]])
                else:
                    s = cnt["dn"] % 2; cnt["dn"] += 1
                    slot_of[item] = s
                    nf = (g1 - g0) * 2
                    k.dma("pool", wdt[s][:, 0:nf, :], wd[l, g0 * 256:g1 * 256, g * 256:(g + 1) * 256].rearrange("(c p) d -> p c d", p=128),
                          writes=[Rwd[s]])

            def compute(item):
                kind, hi, g = item
                g0, g1 = halves[hi]
                s = slot_of[item]
                if kind == "up":
                    for j in range(2):
                        fi = (g - g0) * 2 + j
                        for tb in range(NTB):
                            tsl = slice(tb * TB, (tb + 1) * TB)
                            n = cnt["gu"] % 2; cnt["gu"] += 1
                            for wi, bank in ((0, n), (1, 2 + n)):
                                for c in range(NC8):
                                    mm(ps[bank][:], wgu[s][:, wi, c, j * 128:(j + 1) * 128], h[:, c, tsl], c == 0, c == NC8 - 1,
                                       [Rwgu[s], RH[c][tb]], [RP[bank]], inc=(c == NC8 - 1))
                            k.op("act", lambda: nc.scalar.activation(out=sg[n][:], in_=ps[n][:], func=AF.Silu),
                                 reads=[RP[n]], writes=[Rsg[n]])
                            k.op("dve", lambda: nc.vector.tensor_tensor(
                                out=act[:, fi, tsl], in0=sg[n][:], in1=ps[2 + n][:], op=ALU.mult),
                                reads=[Rsg[n], RP[2 + n]], writes=[RA[fi][tb]])
                else:
                    nf = (g1 - g0) * 2
                    for dj in range(2):
                        dc = g * 2 + dj
                        for tb in range(NTB):
                            tsl = slice(tb * TB, (tb + 1) * TB)
                            bank = 4 + cnt["d"] % 2; cnt["d"] += 1
                            for fi in range(nf):
                                mm(ps[bank][:], wdt[s][:, fi, dj * 128:(dj + 1) * 128], act[:, fi, tsl], fi == 0, fi == nf - 1,
                                   [Rwd[s], RA[fi][tb]], [RP[bank]], inc=(fi == nf - 1))
                            k.op("dve", lambda: nc.vector.scalar_tensor_tensor(
                                out=x_sb[:, dc, tsl], in0=ps[bank][:], scalar=0.5, in1=x_sb[:, dc, tsl],
                                op0=ALU.mult, op1=ALU.add), reads=[RP[bank], RX[dc][tb]], writes=[RX[dc][tb]])

            load(seq[0])
            rmsnorm_h(h, RH, gcol, 7)
            for i, item in enumerate(seq):
                if i + 1 < len(seq): load(seq[i + 1])
                compute(item)
            k.barrier()
            st.close()

        def mixer(l):
            w_in = P["w_in"]; w_o = P["w_o"]
            lam_init = 0.8 - 0.6 * math.exp(-0.3 * (l + 1))
            com = Scope(ar)
            hs = Scope(ar)
            h = hs("m_h", [128, NC8, S], BF16)
            RH = [[Res("mh%d_%d" % (c, t)) for t in range(NTB)] for c in range(NC8)]
            rmsnorm_h(h, RH, pk[:, l, 8:16], 7)

            Et = [com("Et%d" % i, [128, TB], BF16) for i in range(4)]; REt = [Res("Et%d" % i) for i in range(4)]
            NUF = 3
            Uf = [com("Uf%d" % i, [65, TB], F32) for i in range(NUF)]; RU = [Res("Uf%d" % i) for i in range(NUF)]
            deferred = []

            def flush_deferred():
                while deferred:
                    deferred.pop(0)()
            rec = [com("rec%d" % i, [1, TB], F32) for i in range(2)]; Rrec = [Res("rec%d" % i) for i in range(2)]
            sqb = [com("sqb%d" % i, [128, TB], BF16) for i in range(2)]; Rsqb = [Res("sqb%d" % i) for i in range(2)]
            rsd = [com("rsd%d" % i, [128, TB], F32) for i in range(2)]; Rrsd = [Res("rsd%d" % i) for i in range(2)]
            wsl = [com("wsl%d" % i, [128, NC8, 384], BF16) for i in range(2)]; Rwsl = [Res("wsl%d" % i) for i in range(2)]
            vaug = com("vaug", [128, 16, 6, 65], BF16); RV = [Res("vaug%d" % i) for i in range(16)]
            woa = com("woa", [65, 6, D], BF16); Rwo = Res("woa")
            lamt = com("lamt", [65, 8], F32); Rlam = Res("lamt")
            ctr = {"w": 0, "pj": 0, "st": 0, "s": 0, "e": 0, "v": 0, "o": 0, "u": 0, "b": 0, "wo": 0, "r": 0}

            k.op("dve", lambda: nc.vector.tensor_tensor(out=lamt[0:32, 0:2], in0=pk[0:32, l, 34:38:2], in1=pk[0:32, l, 35:39:2], op=ALU.mult),
                 reads=[Rpk], writes=[Rlam])
            mm(ps[7][0:65, 0:2], cf[0:32, 0:65], lamt[0:32, 0:2], True, True, [Rcf, Rlam], [RP[7]])
            k.op("act", lambda: nc.scalar.activation(out=lamt[0:65, 2:4], in_=ps[7][0:65, 0:2], func=AF.Exp), reads=[RP[7]], writes=[Rlam])
            k.op("dve", lambda: nc.vector.tensor_tensor(out=lamt[0:65, 4:5], in0=lamt[0:65, 3:4], in1=lamt[0:65, 2:3], op=ALU.subtract),
                 reads=[Rlam], writes=[Rlam])
            k.op("dve", lambda: nc.vector.tensor_scalar(out=lamt[0:65, 5:6], in0=lamt[0:65, 4:5], scalar1=float(-lam_init), scalar2=None, op0=ALU.add),
                 reads=[Rlam], writes=[Rlam])
            neglam = lamt[0:65, 5:6]

            def load_win(c0, ncol):
                s = ctr["w"] % 2; ctr["w"] += 1
                k.dma("pool", wsl[s][:, :, 0:ncol], w_in[l, :, c0:c0 + ncol].rearrange("(c p) f -> p c f", p=128), writes=[Rwsl[s]])
                return s

            def stat_rstd(src_ap, M, gmat, inv_n, src_res):
                i = ctr["st"] % 2; ctr["st"] += 1
                if src_res[0] in RP:
                    k.op("act", lambda: nc.scalar.activation(out=sqb[i][0:M, :], in_=src_ap, func=AF.Square), reads=src_res, writes=[Rsqb[i]])
                else:
                    k.op("dve", lambda: nc.vector.tensor_tensor(out=sqb[i][0:M, :], in0=src_ap, in1=src_ap, op=ALU.mult),
                         reads=src_res, writes=[Rsqb[i]])
                mm(ps[7][0:M, :], gmat, sqb[i][0:M, :], True, True, [Rcb, Rsqb[i]], [RP[7]])
                k.op("act", lambda: nc.scalar.activation(out=rsd[i][0:M, :], in_=ps[7][0:M, :], func=AF.Ln, scale=float(inv_n), bias=epsb[0:M, 0:1]),
                     reads=[RP[7], Reps], writes=[Rrsd[i]])
                k.op("act", lambda: nc.scalar.activation(out=rsd[i][0:M, :], in_=rsd[i][0:M, :], func=AF.Exp, scale=-0.5),
                     reads=[Rrsd[i]], writes=[Rrsd[i]])
                return i

            def proj_qk(slot, col0, M, tb, gmat, inv_n, gcol, dst_ap, dst_res, src_h=None):
                tsl = slice(tb * TB, (tb + 1) * TB)
                bank = 5 + ctr["pj"] % 2; ctr["pj"] += 1
                for c in range(NC8):
                    mm(ps[bank][0:M, :], wsl[slot][:, c, col0:col0 + M], h[:, c, tsl], c == 0, c == NC8 - 1,
                       [Rwsl[slot], RH[c][tb]], [RP[bank]], inc=(c == NC8 - 1))
                i = stat_rstd(ps[bank][0:M, :], M, gmat, inv_n, [RP[bank]])
                k.op("dve", lambda: nc.vector.scalar_tensor_tensor(out=dst_ap, in0=ps[bank][0:M, :], scalar=gcol, in1=rsd[i][0:M, :],
                                                                   op0=ALU.mult, op1=ALU.mult),
                     reads=[RP[bank], Rrsd[i], Rpk], writes=dst_res)

            def proj_v(lhs_fn, lhs_res_fn, rhs_ap, rhs_res, nh):
                for tcn in range(16):
                    bank = ctr["v"] % 3; ctr["v"] += 1
                    lst = lhs_fn(tcn)
                    for ci, (lap, lres) in enumerate(lst):
                        mm(ps[bank][:, 0:nh * 64], lap, rhs_ap(ci), ci == 0, ci == len(lst) - 1, [lres] + rhs_res, [RP[bank]],
                           inc=(ci == len(lst) - 1))
                    k.op("dve", lambda: nc.vector.tensor_copy(out=vaug[:, tcn, 0:nh, 1:65],
                                                              in_=ps[bank][:, 0:nh * 64].rearrange("p (h d) -> p h d", h=nh)),
                         reads=[RP[bank]], writes=[RV[tcn]])

            def init_vaug():
                k.op("dve", lambda: nc.vector.memset(vaug[:, :, :, 0:1], 1.0), writes=RV)

            def load_wo(h0, nh):
                k.op("dve", lambda: nc.vector.memset(woa[0:1, :, :], 0.0), writes=[Rwo])
                k.dma("pool", woa[1:65, 0:nh, :], w_o[l, h0 * 64:(h0 + nh) * 64, :].rearrange("(h d) o -> d h o", d=64), writes=[Rwo])

            def attn_pass(q_ap, q_res, k_fn, k_res, hl, bias_fn, bias_res, kcs, obank):
                n = len(kcs)
                sb_of = {}; e_of = {}

                def qk(i):
                    kc = kcs[i]
                    sbk = ctr["s"] % 3; ctr["s"] += 1
                    e = ctr["e"] % 4; ctr["e"] += 1
                    sb_of[i] = sbk; e_of[i] = e
                    mm(ps[sbk][:, :], k_fn(kc), q_ap, True, bias_fn is None, k_res + q_res, [RP[sbk]], inc=(bias_fn is None))
                    if bias_fn is not None:
                        mm(ps[sbk][:, :], ident_bf, bias_fn(kc), False, True, [Rcb] + bias_res, [RP[sbk]])
                    k.op("act", lambda: nc.scalar.activation(out=Et[e][:], in_=ps[sbk][:, :], func=AF.Exp), reads=[RP[sbk]], writes=[REt[e]])

                def av(i):
                    kc = kcs[i]; e = e_of[i]
                    mm(ps[obank][0:65, :], vaug[:, kc, hl, 0:65], Et[e][:], i == 0, i == n - 1, [RV[kc], REt[e]], [RP[obank]], inc=(i == n - 1))

                for i in range(min(2, n)): qk(i)
                for i in range(n):
                    if i + 2 < n: qk(i + 2)
                    av(i)
                    if i == min(3, n - 1): flush_deferred()

            def epi_head(obank):
                u = ctr["u"] % NUF; ctr["u"] += 1
                r = ctr["r"] % 2; ctr["r"] += 1
                k.op("dve", lambda: nc.vector.tensor_copy(out=Uf[u][:], in_=ps[obank][0:65, :]), reads=[RP[obank]], writes=[RU[u]])
                k.op("act", lambda: nc.scalar.activation(out=rec[r][:], in_=Uf[u][0:1, :], func=AF.Ln), reads=[RU[u]], writes=[Rrec[r]])
                k.op("act", lambda: nc.scalar.activation(out=rec[r][:], in_=rec[r][:], func=AF.Exp, scale=-1.0), reads=[Rrec[r]], writes=[Rrec[r]])
                return u, r

            def finish_plain(obank, dst_ap, dst_res, after=None):
                u, r = epi_head(obank)

                def tail(u=u, r=r, dst_ap=dst_ap, dst_res=dst_res, after=after):
                    mm(ps[7][0:65, :], cf[0:1, 0:65], rec[r][:], True, True, [Rcf, Rrec[r]], [RP[7]])
                    k.op("dve", lambda: nc.vector.tensor_tensor(out=dst_ap, in0=Uf[u][:], in1=ps[7][0:65, :], op=ALU.mult),
                         reads=[RU[u], RP[7]], writes=dst_res)
                    if after is not None: after()
                deferred.append(tail)
                return u

            def wo_apply(mix_ap_fn, mix_res, nh, qb):
                qsl = slice(qb * TB, (qb + 1) * TB)
                hl = list(range(nh)) if isinstance(nh, int) else nh
                for dc in range(NC8):
                    bank = 5 + ctr["wo"] % 2; ctr["wo"] += 1
                    for hi_, hh in enumerate(hl):
                        mm(ps[bank][:, :], woa[0:65, hh, dc * 128:(dc + 1) * 128], mix_ap_fn(hh), hi_ == 0, hi_ == len(hl) - 1,
                           [Rwo] + mix_res, [RP[bank]], inc=(hi_ == len(hl) - 1))
                    k.op("dve", lambda: nc.vector.tensor_tensor(out=x_sb[:, dc, qsl], in0=ps[bank][:, :], in1=x_sb[:, dc, qsl], op=ALU.add),
                         reads=[RP[bank], RX[dc][qb]], writes=[RX[dc][qb]])

            def mixer_diff():
                sc = Scope(ar)
                dq = [sc("dq%d" % i, [64, S], BF16) for i in range(4)]; Rdq = [[Res("dq%d_%d" % (i, t)) for t in range(NTB)] for i in range(4)]
                dk = [sc("dk%d" % i, [64, S], BF16) for i in range(4)]; Rdk = [[Res("dk%d_%d" % (i, t)) for t in range(NTB)] for i in range(4)]
                bwin = [sc("bwin%d" % i, [128, WIN], BF16) for i in range(2)]; Rbw = [Res("bwin%d" % i) for i in range(2)]
                mixT = [sc("mixT%d" % i, [65, 4, TB], BF16) for i in range(2)]; Rmx = [Res("mixT%d" % i) for i in range(2)]
                qm = [sc("qm%d" % i, [64, TB], BF16) for i in range(2)]; Rqm = [Res("qm%d" % i) for i in range(2)]
                for i in range(2):
                    k.op("dve", lambda: nc.vector.memset(qm[i][:, :], 0.0), writes=[Rqm[i]])
                init_vaug()
                load_wo(0, 4)
                g32 = cb[0:64, C_G32:C_G32 + 64]
                s0 = load_win(0, 256)
                s1 = load_win(256, 256)
                for hd in range(4):
                    for tb in range(NTB):
                        tsl = slice(tb * TB, (tb + 1) * TB)
                        proj_qk(s0, hd * 64, 64, tb, g32, 1.0 / 32, pk[0:64, l, 24:25], dq[hd][:, tsl], [Rdq[hd][tb]])
                s2 = load_win(512, 256)
                for hd in range(4):
                    for tb in range(NTB):
                        tsl = slice(tb * TB, (tb + 1) * TB)
                        proj_qk(s1, hd * 64, 64, tb, g32, 1.0 / 32, pk[0:64, l, 25:26], dk[hd][:, tsl], [Rdk[hd][tb]])
                proj_v(lambda tcn: [(h[:, c, tcn * 128:(tcn + 1) * 128], RH[c][tcn // 4]) for c in range(NC8)], None,
                       lambda ci: wsl[s2][:, ci, 0:256], [Rwsl[s2]], 4)
                kcs = list(range(16))
                m65 = cb[0:65, C_M65:C_M65 + 65]

                def load_bias(hd, qb):
                    s = ctr["b"] % 2; ctr["b"] += 1
                    k.dma("pool", bwin[s][:, :], P["gbias"][hd, :, qb * TB:qb * TB + WIN], writes=[Rbw[s]])
                    return s
                order = [(qb, hd) for qb in range(NTB) for hd in range(4)]
                bs = {order[0]: load_bias(order[0][1], order[0][0])}
                for oi, (qb, hd) in enumerate(order):
                    if oi + 1 < len(order):
                        nq, nh_ = order[oi + 1]
                        bs[order[oi + 1]] = load_bias(nh_, nq)
                    s = bs[(qb, hd)]
                    qsl = slice(qb * TB, (qb + 1) * TB)
                    m = qb % 2
                    kres = [Rdk[hd][t] for t in range(NTB)]
                    us = []
                    for c2 in range(2):
                        rows = slice(c2 * 32, (c2 + 1) * 32)
                        ob = 3 + c2
                        k.op("dve", lambda: nc.vector.tensor_copy(out=qm[c2][rows, :], in_=dq[hd][rows, qsl]), reads=[Rdq[hd][qb]], writes=[Rqm[c2]])
                        attn_pass(qm[c2][0:64, :], [Rqm[c2]], lambda kc: dk[hd][0:64, kc * 128:(kc + 1) * 128], kres, hd,
                                  lambda kc: bwin[s][:, 1920 - kc * 128:1920 - kc * 128 + TB], [Rbw[s]], kcs, ob)
                        u, r = epi_head(ob)
                        us.append((u, r))
                    (u0, r0), (u1, r1) = us

                    def tail(u0=u0, r0=r0, u1=u1, r1=r1, m=m, hd=hd, qb=qb):
                        mm(ps[7][0:65, :], cf[0:1, 0:65], rec[r0][:], True, True, [Rcf, Rrec[r0]], [RP[7]])
                        k.op("dve", lambda: nc.vector.tensor_tensor(out=Uf[u0][:], in0=Uf[u0][:], in1=ps[7][0:65, :], op=ALU.mult),
                             reads=[RU[u0], RP[7]], writes=[RU[u0]])
                        mm(ps[7][0:65, :], cf[0:1, 0:65], rec[r1][:], True, True, [Rcf, Rrec[r1]], [RP[7]])
                        k.op("dve", lambda: nc.vector.scalar_tensor_tensor(out=Uf[u1][:], in0=Uf[u1][:], scalar=neglam, in1=ps[7][0:65, :],
                                                                           op0=ALU.mult, op1=ALU.mult),
                             reads=[RU[u1], RP[7], Rlam], writes=[RU[u1]])
                        k.op("dve", lambda: nc.vector.tensor_tensor(out=Uf[u1][:], in0=Uf[u1][:], in1=Uf[u0][:], op=ALU.add),
                             reads=[RU[u0], RU[u1]], writes=[RU[u1]])
                        i = stat_rstd(Uf[u1][:], 65, m65, 1.0 / 64, [RU[u1]])
                        k.op("dve", lambda: nc.vector.scalar_tensor_tensor(out=mixT[m][:, hd, :], in0=Uf[u1][:], scalar=pk[0:65, l, 33:34], in1=rsd[i][0:65, :],
                                                                           op0=ALU.mult, op1=ALU.mult),
                             reads=[RU[u1], Rrsd[i], Rpk], writes=[Rmx[m]])
                        if hd == 3:
                            wo_apply(lambda hh: mixT[m][:, hh, :], [Rmx[m]], 4, qb)
                    deferred.append(tail)
                flush_deferred()
                k.barrier()
                sc.close()

            def mixer_dil_proj():
                sc = Scope(ar)
                lq = [sc("lq%d" % i, [128, S], BF16) for i in range(3)]; Rlq = [[Res("lq%d_%d" % (i, t)) for t in range(NTB)] for i in range(3)]
                lk = [sc("lk%d" % i, [128, S], BF16) for i in range(3)]; Rlk = [[Res("lk%d_%d" % (i, t)) for t in range(NTB)] for i in range(3)]
                g64 = cb[:, C_G64:C_G64 + 128]
                init_vaug()
                s0 = load_win(768, 384)
                s1 = load_win(1152, 384)
                for pr in range(3):
                    for tb in range(NTB):
                        tsl = slice(tb * TB, (tb + 1) * TB)
                        proj_qk(s0, pr * 128, 128, tb, g64, 1.0 / 64, pk[:, l, 26:27], lq[pr][:, tsl], [Rlq[pr][tb]])
                s2 = load_win(1536, 384)
                for pr in range(3):
                    for tb in range(NTB):
                        tsl = slice(tb * TB, (tb + 1) * TB)
                        proj_qk(s1, pr * 128, 128, tb, g64, 1.0 / 64, pk[:, l, 27:28], lk[pr][:, tsl], [Rlk[pr][tb]])
                proj_v(lambda tcn: [(h[:, c, tcn * 128:(tcn + 1) * 128], RH[c][tcn // 4]) for c in range(NC8)], None,
                       lambda ci: wsl[s2][:, ci, 0:384], [Rwsl[s2]], 6)
                return sc, lq, Rlq, lk, Rlk

            def mixer_dil_attn(sc, lq, Rlq, lk, Rlk):
                bwin = [sc("bwin%d" % i, [128, WIN], BF16) for i in range(2)]; Rbw = [Res("bwin%d" % i) for i in range(2)]
                lcw = sc("lcw", [128, WIN], BF16); Rlc = Res("lcw")
                mixT = [sc("mixT%d" % i, [65, 6, TB], BF16) for i in range(2)]; Rmx = [Res("mixT%d" % i) for i in range(2)]
                load_wo(4, 6)

                def load_bias(hd, qb):
                    s = ctr["b"] % 2; ctr["b"] += 1
                    k.dma("pool", bwin[s][:, :], P["gbias"][4 + hd, :, qb * TB:qb * TB + WIN], writes=[Rbw[s]])
                    return s
                order = [(qb, hd) for qb in range(NTB) for hd in range(6)]
                bs = {order[0]: load_bias(order[0][1], order[0][0])}
                for oi, (qb, hd) in enumerate(order):
                    if hd == 0:
                        k.dma("pool", lcw[:, :], P["logc"][:, qb * TB:qb * TB + WIN], writes=[Rlc])
                    if oi + 1 < len(order):
                        nq, nh_ = order[oi + 1]
                        bs[order[oi + 1]] = load_bias(nh_, nq)
                    s = bs[(qb, hd)]
                    k.op("dve", lambda: nc.vector.tensor_tensor(out=bwin[s][:, :], in0=bwin[s][:, :], in1=lcw[:, :], op=ALU.add),
                         reads=[Rbw[s], Rlc], writes=[Rbw[s]])
                    qsl = slice(qb * TB, (qb + 1) * TB)
                    m = qb % 2
                    pr = hd // 2; rows = slice((hd % 2) * 64, (hd % 2) * 64 + 64)
                    kcs = [kc for kc in range(16) if -1151 <= kc * 128 - qb * TB <= 1535]
                    ob = 3 + ctr["o"] % 2; ctr["o"] += 1
                    attn_pass(lq[pr][rows, qsl], [Rlq[pr][qb]], lambda kc: lk[pr][rows, kc * 128:(kc + 1) * 128], [Rlk[pr][t] for t in range(NTB)], hd,
                              lambda kc: bwin[s][:, 1920 - kc * 128:1920 - kc * 128 + TB], [Rbw[s]], kcs, ob)
                    aft = (lambda m=m, qb=qb: wo_apply(lambda hh: mixT[m][:, hh, :], [Rmx[m]], 6, qb)) if hd == 5 else None
                    finish_plain(ob, mixT[m][:, hd, :], [Rmx[m]], after=aft)
                flush_deferred()
                k.barrier()
                sc.close()

            def mixer_mla_latents():
                sc = Scope(ar)
                qd = sc("qd", [128, 2, S], BF16); Rqd = [Res("qd%d" % t) for t in range(NTB)]
                ckv = sc("ckv", [128, S], BF16); Rckv = [Res("ckv%d" % t) for t in range(NTB)]
                krp = sc("krp", [32, S], BF16); Rkrp = [Res("krp%d" % t) for t in range(NTB)]
                s0 = load_win(1920, 256)
                s1 = load_win(2176, 160)
                for tb in range(NTB):
                    tsl = slice(tb * TB, (tb + 1) * TB)
                    for j in range(2):
                        for c in range(NC8):
                            mm(ps[5 + j][:, :], wsl[s0][:, c, j * 128:(j + 1) * 128], h[:, c, tsl], c == 0, c == NC8 - 1,
                               [Rwsl[s0], RH[c][tb]], [RP[5 + j]], inc=(c == NC8 - 1))
                    for j in range(2):
                        k.op("act", lambda: nc.scalar.activation(out=sqb[j][:, :], in_=ps[5 + j][:, :], func=AF.Square),
                             reads=[RP[5 + j]], writes=[Rsqb[j]])
                        mm(ps[7][:, :], ones_bf, sqb[j][:, :], j == 0, j == 1, [Rcb, Rsqb[j]], [RP[7]])
                    k.op("act", lambda: nc.scalar.activation(out=rsd[0][:, :], in_=ps[7][:, :], func=AF.Ln, scale=1.0 / 256, bias=epsb[:, 0:1]),
                         reads=[RP[7], Reps], writes=[Rrsd[0]])
                    k.op("act", lambda: nc.scalar.activation(out=rsd[0][:, :], in_=rsd[0][:, :], func=AF.Exp, scale=-0.5), reads=[Rrsd[0]], writes=[Rrsd[0]])
                    for j in range(2):
                        k.op("dve", lambda: nc.vector.scalar_tensor_tensor(out=qd[:, j, tsl], in0=ps[5 + j][:, :], scalar=pk[:, l, 28 + j:29 + j], in1=rsd[0][:, :],
                                                                           op0=ALU.mult, op1=ALU.mult),
                             reads=[RP[5 + j], Rrsd[0], Rpk], writes=[Rqd[tb]])
                    for c in range(NC8):
                        mm(ps[5][:, :], wsl[s1][:, c, 0:128], h[:, c, tsl], c == 0, c == NC8 - 1, [Rwsl[s1], RH[c][tb]], [RP[5]], inc=(c == NC8 - 1))
                    for c in range(NC8):
                        mm(ps[6][0:32, :], wsl[s1][:, c, 128:160], h[:, c, tsl], c == 0, c == NC8 - 1, [Rwsl[s1], RH[c][tb]], [RP[6]], inc=(c == NC8 - 1))
                    i = stat_rstd(ps[5][:, :], 128, ones_bf, 1.0 / 128, [RP[5]])
                    k.op("dve", lambda: nc.vector.scalar_tensor_tensor(out=ckv[:, tsl], in0=ps[5][:, :], scalar=pk[:, l, 30:31], in1=rsd[i][:, :],
                                                                       op0=ALU.mult, op1=ALU.mult),
                         reads=[RP[5], Rrsd[i], Rpk], writes=[Rckv[tb]])
                    k.op("dve", lambda: nc.vector.tensor_copy(out=krp[:, tsl], in_=ps[6][0:32, :]), reads=[RP[6]], writes=[Rkrp[tb]])
                return sc, qd, Rqd, ckv, Rckv, krp, Rkrp

            def mixer_mla_attn(sc, qd, Rqd, ckv, Rckv, krp, Rkrp):
                qup = sc("qup", [128, 2, 576], BF16); Rqup = Res("qup")
                knw = sc("knw", [128, 6, 96], BF16); Rknw = Res("knw")
                kvw = sc("kvw", [128, 6, 64], BF16); Rkvw = Res("kvw")
                rC = sc("ropeC", [96, S], F32); rS = sc("ropeS", [96, S], F32); Rrope = Res("rope")
                mqh = [sc("mqh%d" % i, [96, S], BF16) for i in range(2)]; Rmq = [[Res("mq%d_%d" % (i, t)) for t in range(NTB)] for i in range(2)]
                mkh = [sc("mkh%d" % i, [96, S], BF16) for i in range(2)]; Rmk = [[Res("mk%d_%d" % (i, t)) for t in range(NTB)] for i in range(2)]
                qn = [sc("qn%d" % i, [96, TB], BF16) for i in range(2)]; Rqn = [Res("qn%d" % i) for i in range(2)]
                t1 = sc("t1", [96, TB], F32); Rt1 = Res("t1")
                t2 = sc("t2", [96, TB], F32); Rt2 = Res("t2")
                mixT = [sc("mixTc%d" % i, [65, TB], BF16) for i in range(2)]; Rmx = [Res("mxc%d" % i) for i in range(2)]
                k.dma("pool", qup[:, :, :], P["mla_q_up"][l].rearrange("(c p) n -> p c n", p=128), writes=[Rqup])
                k.op("dve", lambda: nc.vector.memset(knw[:, :, :], 0.0), writes=[Rknw])
                kvv = P["mla_kv_up"][l].rearrange("p (h t d) -> p h t d", h=6, t=2)
                k.dma("pool", knw[:, :, 0:64], kvv[:, :, 0, :], writes=[Rknw])
                k.dma("pool", kvw[:, :, :], kvv[:, :, 1, :], writes=[Rkvw])
                k.dma("sp", rC[:, :], P["ropeC"][:, :], writes=[Rrope])
                k.dma("sp", rS[:, :], P["ropeS"][:, :], writes=[Rrope])
                init_vaug()
                load_wo(10, 6)
                proj_v(lambda tcn: [(ckv[:, tcn * 128:(tcn + 1) * 128], Rckv[tcn // 4])], None,
                       lambda ci: kvw[:, :, :].rearrange("p h d -> p (h d)"), [Rkvw], 6)
                o96 = cb[0:96, C_ONES:C_ONES + 96]
                rt = cb[0:96, C_RT:C_RT + 96]
                sel = cb[0:32, C_SEL:C_SEL + 96]
                nn = {"q": 0}

                def norm_rope(bank, gcol, dst_ap, dst_res, tsl):
                    i = stat_rstd(ps[bank][0:96, :], 96, o96, 1.0 / 96, [RP[bank]])
                    j = nn["q"] % 2; nn["q"] += 1
                    k.op("dve", lambda: nc.vector.scalar_tensor_tensor(out=qn[j][:, :], in0=ps[bank][0:96, :], scalar=gcol, in1=rsd[i][0:96, :],
                                                                       op0=ALU.mult, op1=ALU.mult),
                         reads=[RP[bank], Rrsd[i], Rpk], writes=[Rqn[j]])
                    rb = 3 + nn["q"] % 2
                    mm(ps[rb][0:96, :], rt, qn[j][:, :], True, True, [Rcb, Rqn[j]], [RP[rb]])
                    k.op("dve", lambda: nc.vector.tensor_tensor(out=t1[:, :], in0=qn[j][:, :], in1=rC[:, tsl], op=ALU.mult),
                         reads=[Rqn[j], Rrope], writes=[Rt1])
                    k.op("dve", lambda: nc.vector.tensor_tensor(out=t2[:, :], in0=ps[rb][0:96, :], in1=rS[:, tsl], op=ALU.mult),
                         reads=[RP[rb], Rrope], writes=[Rt2])
                    k.op("dve", lambda: nc.vector.tensor_tensor(out=dst_ap, in0=t1[:, :], in1=t2[:, :], op=ALU.add),
                         reads=[Rt1, Rt2], writes=dst_res)

                kcs = list(range(16))
                for hd in range(6):
                    sl = hd % 2
                    for tb in range(NTB):
                        tsl = slice(tb * TB, (tb + 1) * TB)
                        bank = 5 + ctr["pj"] % 2; ctr["pj"] += 1
                        for j in range(2):
                            mm(ps[bank][0:96, :], qup[:, j, hd * 96:(hd + 1) * 96], qd[:, j, tsl], j == 0, j == 1, [Rqup, Rqd[tb]], [RP[bank]], inc=(j == 1))
                        norm_rope(bank, pk[0:96, l, 31:32], mqh[sl][:, tsl], [Rmq[sl][tb]], tsl)
                        bank = 5 + ctr["pj"] % 2; ctr["pj"] += 1
                        mm(ps[bank][0:96, :], knw[:, hd, :], ckv[:, tsl], True, False, [Rknw, Rckv[tb]], [RP[bank]], inc=False)
                        mm(ps[bank][0:96, :], sel, krp[:, tsl], False, True, [Rcb, Rkrp[tb]], [RP[bank]])
                        norm_rope(bank, pk[0:96, l, 32:33], mkh[sl][:, tsl], [Rmk[sl][tb]], tsl)
                    for qb in range(NTB):
                        qsl = slice(qb * TB, (qb + 1) * TB)
                        ob = 3 + ctr["o"] % 2; ctr["o"] += 1
                        attn_pass(mqh[sl][:, qsl], [Rmq[sl][qb]], lambda kc: mkh[sl][:, kc * 128:(kc + 1) * 128], [Rmk[sl][t] for t in range(NTB)], hd,
                                  None, [], kcs, ob)
                        m = ctr["o"] % 2
                        finish_plain(ob, mixT[m][:, :], [Rmx[m]],
                                     after=(lambda m=m, hd=hd, qb=qb: wo_apply(lambda hh: mixT[m][:, :], [Rmx[m]], [hd], qb)))
                flush_deferred()
                k.barrier()
                sc.close()

            mixer_diff()
            dsc = mixer_dil_proj()
            msc = mixer_mla_latents()
            k.barrier()
            hs.close()
            mixer_dil_attn(*dsc)
            mixer_mla_attn(*msc)
            k.barrier()
            com.close()

        for stg in stages:
            kind, l = stg
            if kind == "ffn1": ffn(l, 1)
            elif kind == "ffn2": ffn(l, 2)
            elif kind == "mix": mixer(l)

        for c in range(NC8):
            k.dma("sp", yT[c * 128:(c + 1) * 128, :], x_sb[:, c, :], reads=RX[c])
        k.barrier()
        build.stats = (k.ninst, k.nwait, ar.peak)
    return nc


ALL_STAGES = [("ffn1", 0), ("mix", 0), ("ffn2", 0), ("ffn1", 1), ("mix", 1), ("ffn2", 1)]


def _gather_bias(rel_bias):
    idx = _t5_bucket(_rel_grid())
    return np.ascontiguousarray(np.transpose(rel_bias[idx], (2, 0, 1))).astype(np.float32)


def make_in_maps(inp, n_cores=8):
    cst, ropeC, ropeS, logc = _host_consts()
    pk = _pack_small(inp)
    gb = _gather_bias(np.asarray(inp["rel_bias"], np.float32))
    shared = {nm: np.ascontiguousarray(np.asarray(inp[nm], np.float32)) for nm, _ in WEIGHT_SPECS}
    shared.update({"pk": pk, "cst": cst, "ropeC": ropeC, "ropeS": ropeS, "logc": logc, "gbias": gb})
    maps = []
    for b in range(n_cores):
        m = dict(shared)
        m["xT"] = np.ascontiguousarray(np.asarray(inp["x"][b], np.float32).T)
        maps.append(m)
    return maps


def kernel(**inp):
    nc = build(ALL_STAGES)
    maps = make_in_maps(inp, 8)
    res = run_bass_kernel_spmd(nc, maps, core_ids=list(range(8)))
    out = np.stack([np.ascontiguousarray(r["yT"].T) for r in res.results], axis=0)
    return out.astype(np.float32)
```

```python
import math
import contextlib
import numpy as np
import concourse.bass as bass
import concourse.mybir as mybir
from concourse.bass_utils import run_bass_kernel_spmd

F32 = mybir.dt.float32
BF16 = mybir.dt.bfloat16
AF = mybir.ActivationFunctionType
ALU = mybir.AluOpType

D = 1024; S = 2048; DFF = 2816; L = 2; NC8 = 8; TB = 512; NTB = 4
EPS = 1e-6
INW = 2336
NEG = -30000.0
GW = 3968
WIN = 2432
C_ID = 0; C_ONES = 128; C_G32 = 256; C_G64 = 384; C_M65 = 512; C_RT = 640; C_SEL = 768; NCST = 896
NPK = 40


class Res:
    __slots__ = ("name", "w", "r")

    def __init__(self, name):
        self.name = name; self.w = None; self.r = {}


class K:
    def __init__(self, nc, es, n_dma_sems=8):
        self.nc = nc
        self.eng = {"pe": nc.tensor, "dve": nc.vector, "act": nc.scalar, "pool": nc.gpsimd, "sp": nc.sync}
        self.sem = {}; self.cnt = {}
        for e in self.eng:
            self.sem[e] = es.enter_context(nc.semaphore("s_" + e)); self.cnt[e] = 0
        self.seen = {e: {} for e in self.eng}
        self.pend = {e: ([], []) for e in self.eng}
        self.dq = {}
        for q in ("sp", "pool"):
            lst = []
            for i in range(n_dma_sems):
                nm = "d_%s%d" % (q, i)
                self.sem[nm] = es.enter_context(nc.semaphore(nm)); self.cnt[nm] = 0
                lst.append(nm)
            self.dq[q] = [lst, 0]
        self.nwait = 0; self.ninst = 0
        self.es = es; self.slot_sem = {}

    def _sem_for(self, q, writes):
        if not writes:
            lst, i = self.dq[q]
            self.dq[q][1] = i + 1
            return lst[i % len(lst)]
        key = writes[0].name
        if key not in self.slot_sem:
            nm = "ds_%d" % len(self.slot_sem)
            self.sem[nm] = self.es.enter_context(self.nc.semaphore(nm)); self.cnt[nm] = 0
            self.slot_sem[key] = nm
        return self.slot_sem[key]

    def _wait(self, e, f, c):
        if c <= 0 or self.seen[e].get(f, 0) >= c:
            return
        self.eng[e].wait_ge(self.sem[f], c)
        self.seen[e][f] = c; self.nwait += 1

    def op(self, e, fn, reads=(), writes=(), inc=True):
        need = {}
        for r in reads:
            if r.w is not None:
                f, c = r.w
                if f == e and e == "pe":
                    continue
                if c > need.get(f, 0): need[f] = c
        for w in writes:
            if w.w is not None:
                f, c = w.w
                if f != e and c > need.get(f, 0): need[f] = c
            for f, c in w.r.items():
                if f != e and c > need.get(f, 0): need[f] = c
        for f, c in need.items():
            self._wait(e, f, c)
        ins = fn()
        self.ninst += 1
        pr, pw = self.pend[e]
        pr.extend(reads); pw.extend(writes)
        if inc:
            ins.then_inc(self.sem[e], 1)
            self.cnt[e] += 1
            c = self.cnt[e]
            for r in pr: r.r[e] = c
            for w in pw:
                w.w = (e, c); w.r = {}
            self.pend[e] = ([], [])
        return ins

    def dma(self, q, out, in_, reads=(), writes=()):
        s = self._sem_for(q, list(writes))
        need = {}
        for r in reads:
            if r.w is not None:
                f, c = r.w
                if c > need.get(f, 0): need[f] = c
        for w in writes:
            if w.w is not None:
                f, c = w.w
                if c > need.get(f, 0): need[f] = c
            for f, c in w.r.items():
                if c > need.get(f, 0): need[f] = c
        for f, c in need.items():
            self._wait(q, f, c)
        self._wait(q, s, self.cnt[s])
        self.eng[q].dma_start(out=out, in_=in_).then_inc(self.sem[s], 16)
        self.cnt[s] += 16
        c = self.cnt[s]
        for r in reads: r.r[s] = c
        for w in writes:
            w.w = (s, c); w.r = {}
        self.ninst += 1

    def barrier(self):
        for e in self.eng:
            for f in self.cnt:
                if f != e: self._wait(e, f, self.cnt[f])


class Arena:
    def __init__(self, nc, lo, hi):
        self.nc = nc; self.free = [(lo, hi)]; self.used = {}; self.n = 0; self.peak = 0; self.hi = hi

    def alloc(self, name, shape, dt):
        esz = 4 if dt == F32 else 2
        size = esz
        for d in shape[1:]: size *= d
        size = (size + 63) // 64 * 64
        for i, (a, b) in enumerate(self.free):
            if b - a >= size:
                self.free[i] = (a + size, b)
                if self.free[i][0] == self.free[i][1]: self.free.pop(i)
                self.n += 1
                t = self.nc.alloc_sbuf_tensor_at("%s_%d" % (name, self.n), list(shape), dt, offset=a)
                self.used[id(t)] = (a, a + size, t)
                self.peak = max(self.peak, a + size)
                return t
        raise RuntimeError("SBUF arena full allocating %s %s (free=%s)" % (name, shape, self.free))

    def release(self, ts):
        for t in ts:
            a, b, _ = self.used.pop(id(t))
            self.free.append((a, b))
        self.free.sort()
        m = []
        for a, b in self.free:
            if m and m[-1][1] == a: m[-1] = (m[-1][0], b)
            else: m.append((a, b))
        self.free = m


class Scope:
    def __init__(self, ar): self.ar = ar; self.ts = []
    def __call__(self, name, shape, dt):
        t = self.ar.alloc(name, shape, dt); self.ts.append(t); return t
    def close(self):
        self.ar.release(self.ts); self.ts = []


def _t5_bucket(rel):
    rel = np.asarray(rel, dtype=np.int64)
    n = np.abs(rel)
    nf = np.maximum(n, 1).astype(np.float64)
    large = 8 + np.floor(np.log(nf / 8.0) / math.log(128.0) * 8.0 + 1e-9).astype(np.int64)
    large = np.where(n < 8, 0, large)
    large = np.minimum(large, 15)
    return np.where(rel > 0, 16, 0) + np.where(n < 8, n, large)


def _rel_grid():
    kp = np.arange(128)[:, None]; j = np.arange(GW)[None, :]
    return kp - j + 1920


def _host_consts():
    cst = np.zeros((128, NCST), np.float32)
    cst[:, C_ID:C_ID + 128] = np.eye(128, dtype=np.float32)
    cst[:, C_ONES:C_ONES + 128] = 1.0
    for g in range(4):
        cst[g * 32:(g + 1) * 32, C_G32 + g * 32:C_G32 + (g + 1) * 32] = 1.0
    for g in range(2):
        cst[g * 64:(g + 1) * 64, C_G64 + g * 64:C_G64 + (g + 1) * 64] = 1.0
    cst[1:65, C_M65:C_M65 + 65] = 1.0
    for m in range(64, 80):
        cst[m + 16, C_RT + m] = -1.0
    for m in range(80, 96):
        cst[m - 16, C_RT + m] = 1.0
    for j in range(32):
        cst[j, C_SEL + 64 + j] = 1.0
    inv = (10000.0 ** (-(np.arange(16, dtype=np.float32) / np.float32(16)))).astype(np.float32)
    ang = (np.arange(S, dtype=np.float32)[None, :] * inv[:, None]).astype(np.float32)
    ropeC = np.ones((96, S), np.float32); ropeS = np.zeros((96, S), np.float32)
    ropeC[64:80] = np.cos(ang.astype(np.float64)); ropeC[80:96] = ropeC[64:80]
    ropeS[64:80] = np.sin(ang.astype(np.float64)); ropeS[80:96] = ropeS[64:80]
    rel = _rel_grid()
    cnt = (np.abs(rel) <= 64).astype(np.int64) + ((rel % 4 == 0) & (np.abs(rel) <= 256)) + ((rel % 16 == 0) & (np.abs(rel) <= 1024))
    logc = np.where(cnt > 0, np.log(np.maximum(cnt, 1).astype(np.float64)), NEG).astype(np.float32)
    return cst, ropeC, ropeS, logc


def _pack_small(inp):
    pk = np.zeros((128, L, NPK), np.float32)
    for l in range(L):
        for i, nm in enumerate(("ffn1_norm", "mix_norm", "ffn2_norm")):
            pk[:, l, i * 8:(i + 1) * 8] = inp[nm][l].reshape(8, 128).T
        pk[0:64, l, 24] = np.tile(inp["diff_q_norm"][l], 2)
        pk[0:64, l, 25] = np.tile(inp["diff_k_norm"][l], 2)
        pk[0:128, l, 26] = np.tile(inp["dil_q_norm"][l], 2)
        pk[0:128, l, 27] = np.tile(inp["dil_k_norm"][l], 2)
        pk[:, l, 28:30] = inp["mla_q_norm"][l].reshape(2, 128).T
        pk[:, l, 30] = inp["mla_kv_norm"][l]
        pk[0:96, l, 31] = inp["mla_qn"][l]
        pk[0:96, l, 32] = inp["mla_kn"][l]
        pk[1:65, l, 33] = inp["diff_subln"][l]
        pk[0:32, l, 34:38] = inp["diff_lambda"][l].T
    return pk


WEIGHT_SPECS = [
    ("ffn1_wg", [L, D, DFF]), ("ffn1_wu", [L, D, DFF]), ("ffn1_wd", [L, DFF, D]),
    ("ffn2_wg", [L, D, DFF]), ("ffn2_wu", [L, D, DFF]), ("ffn2_wd", [L, DFF, D]),
    ("w_in", [L, D, INW]), ("w_o", [L, D, D]),
    ("mla_q_up", [L, 256, 576]), ("mla_kv_up", [L, 128, 768]),
]


def build(stages, dbg=None):
    nc = bass.Bass("TRN2", target_bir_lowering=False)
    P = {}
    P["xT"] = nc.dram_tensor("xT", [D, S], F32, kind="ExternalInput").ap()
    for nm, shp in WEIGHT_SPECS:
        P[nm] = nc.dram_tensor(nm, shp, F32, kind="ExternalInput").ap()
    P["pk"] = nc.dram_tensor("pk", [128, L, NPK], F32, kind="ExternalInput").ap()
    P["cst"] = nc.dram_tensor("cst", [128, NCST], F32, kind="ExternalInput").ap()
    P["ropeC"] = nc.dram_tensor("ropeC", [96, S], F32, kind="ExternalInput").ap()
    P["ropeS"] = nc.dram_tensor("ropeS", [96, S], F32, kind="ExternalInput").ap()
    P["logc"] = nc.dram_tensor("logc", [128, GW], F32, kind="ExternalInput").ap()
    P["gbias"] = nc.dram_tensor("gbias", [10, 128, GW], F32, kind="ExternalInput").ap()
    yT = nc.dram_tensor("yT", [D, S], F32, kind="ExternalOutput").ap()

    with contextlib.ExitStack() as es:
        k = K(nc, es)
        ar = Arena(nc, 16512, 229376)
        glob = Scope(ar)

        x_sb = glob("x_sb", [128, NC8, S], F32)
        RX = [[Res("x%d_%d" % (c, t)) for t in range(NTB)] for c in range(NC8)]
        cb = glob("cb", [128, NCST], BF16); Rcb = Res("cb")
        cf = glob("cf", [128, 128], F32); Rcf = Res("cf")
        pk = glob("pk_sb", [128, L, NPK], F32); Rpk = Res("pk")
        epsb = glob("epsb", [128, 1], F32); Reps = Res("eps")
        ps = [es.enter_context(nc.psum_tensor("ps%d" % i, [128, 512], F32)) for i in range(8)]
        RP = [Res("ps%d" % i) for i in range(8)]

        def mm(out, lhsT, rhs, start, stop, reads, writes, inc=True):
            return k.op("pe", lambda: nc.tensor.matmul(out, lhsT=lhsT, rhs=rhs, start=start, stop=stop), reads, writes, inc)

        def warm(n=32):
            for i in range(n):
                mm(ps[7][:, :], cb[:, C_ID:C_ID + 128], cb[:, 0:512], True, True, [Rcb], [RP[7]], inc=(i == n - 1))

        for c in range(NC8):
            k.dma("sp", x_sb[:, c, :], P["xT"][c * 128:(c + 1) * 128, :], writes=RX[c])
        k.dma("pool", cb[:], P["cst"][:, :], writes=[Rcb])
        k.dma("sp", cf[:], P["cst"][:, C_ONES:C_ONES + 128], writes=[Rcf])
        k.dma("sp", pk[:], P["pk"][:, :, :], writes=[Rpk])
        k.op("dve", lambda: nc.vector.memset(epsb[:], EPS), writes=[Reps])
        for l in range(L):
            lam_init = 0.8 - 0.6 * math.exp(-0.3 * (l + 1))
            for col, sc in ((24, 32 ** -0.5), (26, 64 ** -0.5), (31, 96 ** -0.5), (33, 1.0 - lam_init)):
                k.op("dve", lambda: nc.vector.tensor_scalar(
                    out=pk[:, l, col:col + 1], in0=pk[:, l, col:col + 1], scalar1=float(sc), scalar2=None,
                    op0=ALU.mult), reads=[Rpk], writes=[Rpk])

        ones_bf = cb[:, C_ONES:C_ONES + 128]
        ident_bf = cb[:, C_ID:C_ID + 128]

        def rmsnorm_h(h, RH, gcol, stat_bank):
            sc = Scope(ar)
            sq = [sc("nsq%d" % i, [128, TB], BF16) for i in range(4)]
            Rsq = [Res("nsq%d" % i) for i in range(4)]
            lnv = sc("nlnv", [128, TB], F32); Rln = Res("nlnv")
            rstd = [sc("nrstd%d" % i, [128, TB], F32) for i in range(2)]
            Rrs = [Res("nrstd%d" % i) for i in range(2)]
            n = 0
            for tb in range(NTB):
                tsl = slice(tb * TB, (tb + 1) * TB)
                for c in range(NC8):
                    i = n % 4; n += 1
                    k.op("act", lambda: nc.scalar.activation(out=sq[i][:], in_=x_sb[:, c, tsl], func=AF.Square),
                         reads=[RX[c][tb]], writes=[Rsq[i]])
                    mm(ps[stat_bank][:], ones_bf, sq[i][:], c == 0, c == NC8 - 1, [Rsq[i], Rcb], [RP[stat_bank]])
                r = tb % 2
                k.op("act", lambda: nc.scalar.activation(out=lnv[:], in_=ps[stat_bank][:], func=AF.Ln, scale=1.0 / D, bias=epsb[:, 0:1]),
                     reads=[RP[stat_bank], Reps], writes=[Rln])
                k.op("act", lambda: nc.scalar.activation(out=rstd[r][:], in_=lnv[:], func=AF.Exp, scale=-0.5),
                     reads=[Rln], writes=[Rrs[r]])
                for c in range(NC8):
                    k.op("dve", lambda: nc.vector.scalar_tensor_tensor(
                        out=h[:, c, tsl], in0=x_sb[:, c, tsl], scalar=gcol[:, c:c + 1], in1=rstd[r][:],
                        op0=ALU.mult, op1=ALU.mult), reads=[RX[c][tb], Rrs[r], Rpk], writes=[RH[c][tb]])
            k.barrier()
            sc.close()

        def ffn(l, which):
            wg = P["ffn%d_wg" % which]; wu = P["ffn%d_wu" % which]; wd = P["ffn%d_wd" % which]
            gcol = pk[:, l, (0 if which == 1 else 16):(8 if which == 1 else 24)]
            st = Scope(ar)
            h = st("f_h", [128, NC8, S], BF16)
            RH = [[Res("h%d_%d" % (c, t)) for t in range(NTB)] for c in range(NC8)]
            act = st("f_act", [128, 12, S], BF16)
            RA = [[Res("a%d_%d" % (c, t)) for t in range(NTB)] for c in range(12)]
            wgu = [st("f_wgu%d" % i, [128, 2, NC8, 256], BF16) for i in range(2)]
            Rwgu = [Res("wgu%d" % i) for i in range(2)]
            wdt = [st("f_wd%d" % i, [128, 12, 256], BF16) for i in range(2)]
            Rwd = [Res("wd%d" % i) for i in range(2)]
            sg = [st("f_sg%d" % i, [128, TB], F32) for i in range(2)]
            Rsg = [Res("sg%d" % i) for i in range(2)]
            halves = [(0, 6), (6, 11)]
            seq = []
            for hi, (g0, g1) in enumerate(halves):
                for g in range(g0, g1): seq.append(("up", hi, g))
                for dg in range(4): seq.append(("dn", hi, dg))
            cnt = {"up": 0, "dn": 0, "gu": 0, "d": 0}
            slot_of = {}

            def load(item):
                kind, hi, g = item
                g0, g1 = halves[hi]
                if kind == "up":
                    s = cnt["up"] % 2; cnt["up"] += 1
                    slot_of[item] = s
                    for j, w in enumerate((wg, wu)):
                        k.dma("pool", wgu[s][:, j, :, :], w[l, :, g * 256:(g + 1) * 256].rearrange("(c p) f -> p c f", p=128),
                              writes=[Rwgu[s]])
                else:
                    s = cnt["dn"] % 2; cnt["dn"] += 1
                    slot_of[item] = s
                    nf = (g1 - g0) * 2
                    k.dma("pool", wdt[s][:, 0:nf, :], wd[l, g0 * 256:g1 * 256, g * 256:(g + 1) * 256].rearrange("(c p) d -> p c d", p=128),
                          writes=[Rwd[s]])

            def compute(item):
                kind, hi, g = item
                g0, g1 = halves[hi]
                s = slot_of[item]
                if kind == "up":
                    for j in range(2):
                        fi = (g - g0) * 2 + j
                        for tb in range(NTB):
                            tsl = slice(tb * TB, (tb + 1) * TB)
                            n = cnt["gu"] % 2; cnt["gu"] += 1
                            for wi, bank in ((0, n), (1, 2 + n)):
                                for c in range(NC8):
                                    mm(ps[bank][:], wgu[s][:, wi, c, j * 128:(j + 1) * 128], h[:, c, tsl], c == 0, c == NC8 - 1,
                                       [Rwgu[s], RH[c][tb]], [RP[bank]], inc=(c == NC8 - 1))
                            k.op("act", lambda: nc.scalar.activation(out=sg[n][:], in_=ps[n][:], func=AF.Silu),
                                 reads=[RP[n]], writes=[Rsg[n]])
                            k.op("dve", lambda: nc.vector.tensor_tensor(
                                out=act[:, fi, tsl], in0=sg[n][:], in1=ps[2 + n][:], op=ALU.mult),
                                reads=[Rsg[n], RP[2 + n]], writes=[RA[fi][tb]])
                else:
                    nf = (g1 - g0) * 2
                    for dj in range(2):
                        dc = g * 2 + dj
                        for tb in range(NTB):
                            tsl = slice(tb * TB, (tb + 1) * TB)
                            bank = 4 + cnt["d"] % 2; cnt["d"] += 1
                            for fi in range(nf):
                                mm(ps[bank][:], wdt[s][:, fi, dj * 128:(dj + 1) * 128], act[:, fi, tsl], fi == 0, fi == nf - 1,
                                   [Rwd[s], RA[fi][tb]], [RP[bank]], inc=(fi == nf - 1))
                            k.op("dve", lambda: nc.vector.scalar_tensor_tensor(
                                out=x_sb[:, dc, tsl], in0=ps[bank][:], scalar=0.5, in1=x_sb[:, dc, tsl],
                                op0=ALU.mult, op1=ALU.add), reads=[RP[bank], RX[dc][tb]], writes=[RX[dc][tb]])

            load(seq[0])
            rmsnorm_h(h, RH, gcol, 7)
            for i, item in enumerate(seq):
                if i + 1 < len(seq): load(seq[i + 1])
                compute(item)
            k.barrier()
            st.close()

        def mixer(l):
            w_in = P["w_in"]; w_o = P["w_o"]
            lam_init = 0.8 - 0.6 * math.exp(-0.3 * (l + 1))
            com = Scope(ar)
            hs = Scope(ar)
            h = hs("m_h", [128, NC8, S], BF16)
            RH = [[Res("mh%d_%d" % (c, t)) for t in range(NTB)] for c in range(NC8)]
            rmsnorm_h(h, RH, pk[:, l, 8:16], 7)
            warm()

            Et = [com("Et%d" % i, [128, TB], BF16) for i in range(4)]; REt = [Res("Et%d" % i) for i in range(4)]
            NUF = 4
            Uf = [com("Uf%d" % i, [65, TB], F32) for i in range(NUF)]; RU = [Res("Uf%d" % i) for i in range(NUF)]
            deferred = []

            def advance_deferred():
                for item in list(deferred):
                    item.pop(0)()
                    if not item: deferred.remove(item)

            def flush_deferred():
                while deferred:
                    advance_deferred()
            rec = [com("rec%d" % i, [1, TB], F32) for i in range(2)]; Rrec = [Res("rec%d" % i) for i in range(2)]
            sqb = [com("sqb%d" % i, [128, TB], BF16) for i in range(2)]; Rsqb = [Res("sqb%d" % i) for i in range(2)]
            rsd = [com("rsd%d" % i, [128, TB], F32) for i in range(2)]; Rrsd = [Res("rsd%d" % i) for i in range(2)]
            wsl = [com("wsl%d" % i, [128, NC8, 384], BF16) for i in range(2)]; Rwsl = [Res("wsl%d" % i) for i in range(2)]
            vaug = com("vaug", [128, 16, 6, 65], BF16); RV = [Res("vaug%d" % i) for i in range(16)]
            woa = com("woa", [65, 6, D], BF16); Rwo = Res("woa")
            lamt = com("lamt", [65, 8], F32); Rlam = Res("lamt")
            ctr = {"w": 0, "pj": 0, "st": 0, "s": 0, "e": 0, "v": 0, "o": 0, "u": 0, "b": 0, "wo": 0, "r": 0}

            k.op("dve", lambda: nc.vector.tensor_tensor(out=lamt[0:32, 0:2], in0=pk[0:32, l, 34:38:2], in1=pk[0:32, l, 35:39:2], op=ALU.mult),
                 reads=[Rpk], writes=[Rlam])
            mm(ps[7][0:65, 0:2], cf[0:32, 0:65], lamt[0:32, 0:2], True, True, [Rcf, Rlam], [RP[7]])
            k.op("act", lambda: nc.scalar.activation(out=lamt[0:65, 2:4], in_=ps[7][0:65, 0:2], func=AF.Exp), reads=[RP[7]], writes=[Rlam])
            k.op("dve", lambda: nc.vector.tensor_tensor(out=lamt[0:65, 4:5], in0=lamt[0:65, 3:4], in1=lamt[0:65, 2:3], op=ALU.subtract),
                 reads=[Rlam], writes=[Rlam])
            k.op("dve", lambda: nc.vector.tensor_scalar(out=lamt[0:65, 5:6], in0=lamt[0:65, 4:5], scalar1=float(-lam_init), scalar2=None, op0=ALU.add),
                 reads=[Rlam], writes=[Rlam])
            neglam = lamt[0:65, 5:6]

            def load_win(c0, ncol):
                s = ctr["w"] % 2; ctr["w"] += 1
                k.dma("pool", wsl[s][:, :, 0:ncol], w_in[l, :, c0:c0 + ncol].rearrange("(c p) f -> p c f", p=128), writes=[Rwsl[s]])
                return s

            def stat_rstd(src_ap, M, gmat, inv_n, src_res):
                i = ctr["st"] % 2; ctr["st"] += 1
                if src_res[0] in RP:
                    k.op("act", lambda: nc.scalar.activation(out=sqb[i][0:M, :], in_=src_ap, func=AF.Square), reads=src_res, writes=[Rsqb[i]])
                else:
                    k.op("dve", lambda: nc.vector.tensor_tensor(out=sqb[i][0:M, :], in0=src_ap, in1=src_ap, op=ALU.mult),
                         reads=src_res, writes=[Rsqb[i]])
                mm(ps[7][0:M, :], gmat, sqb[i][0:M, :], True, True, [Rcb, Rsqb[i]], [RP[7]])
                k.op("act", lambda: nc.scalar.activation(out=rsd[i][0:M, :], in_=ps[7][0:M, :], func=AF.Ln, scale=float(inv_n), bias=epsb[0:M, 0:1]),
                     reads=[RP[7], Reps], writes=[Rrsd[i]])
                k.op("act", lambda: nc.scalar.activation(out=rsd[i][0:M, :], in_=rsd[i][0:M, :], func=AF.Exp, scale=-0.5),
                     reads=[Rrsd[i]], writes=[Rrsd[i]])
                return i

            def proj_qk(slot, col0, M, tb, gmat, inv_n, gcol, dst_ap, dst_res, src_h=None):
                tsl = slice(tb * TB, (tb + 1) * TB)
                bank = 5 + ctr["pj"] % 2; ctr["pj"] += 1
                for c in range(NC8):
                    mm(ps[bank][0:M, :], wsl[slot][:, c, col0:col0 + M], h[:, c, tsl], c == 0, c == NC8 - 1,
                       [Rwsl[slot], RH[c][tb]], [RP[bank]], inc=(c == NC8 - 1))
                i = stat_rstd(ps[bank][0:M, :], M, gmat, inv_n, [RP[bank]])
                k.op("dve", lambda: nc.vector.scalar_tensor_tensor(out=dst_ap, in0=ps[bank][0:M, :], scalar=gcol, in1=rsd[i][0:M, :],
                                                                   op0=ALU.mult, op1=ALU.mult),
                     reads=[RP[bank], Rrsd[i], Rpk], writes=dst_res)

            def proj_v(lhs_fn, lhs_res_fn, rhs_ap, rhs_res, nh):
                for tcn in range(16):
                    bank = ctr["v"] % 3; ctr["v"] += 1
                    lst = lhs_fn(tcn)
                    for ci, (lap, lres) in enumerate(lst):
                        mm(ps[bank][:, 0:nh * 64], lap, rhs_ap(ci), ci == 0, ci == len(lst) - 1, [lres] + rhs_res, [RP[bank]],
                           inc=(ci == len(lst) - 1))
                    k.op("dve", lambda: nc.vector.tensor_copy(out=vaug[:, tcn, 0:nh, 1:65],
                                                              in_=ps[bank][:, 0:nh * 64].rearrange("p (h d) -> p h d", h=nh)),
                         reads=[RP[bank]], writes=[RV[tcn]])

            def init_vaug():
                k.op("dve", lambda: nc.vector.memset(vaug[:, :, :, 0:1], 1.0), writes=RV)

            def load_wo(h0, nh):
                k.op("dve", lambda: nc.vector.memset(woa[0:1, :, :], 0.0), writes=[Rwo])
                k.dma("pool", woa[1:65, 0:nh, :], w_o[l, h0 * 64:(h0 + nh) * 64, :].rearrange("(h d) o -> d h o", d=64), writes=[Rwo])

            def attn_pass(q_ap, q_res, k_fn, k_res, hl, bias_fn, bias_res, kcs, obank):
                n = len(kcs)
                sb_of = {}; e_of = {}

                def qk(i):
                    kc = kcs[i]
                    sbk = ctr["s"] % 3; ctr["s"] += 1
                    e = ctr["e"] % 4; ctr["e"] += 1
                    sb_of[i] = sbk; e_of[i] = e
                    mm(ps[sbk][:, :], k_fn(kc), q_ap, True, bias_fn is None, k_res + q_res, [RP[sbk]], inc=(bias_fn is None))
                    if bias_fn is not None:
                        mm(ps[sbk][:, :], ident_bf, bias_fn(kc), False, True, [Rcb] + bias_res, [RP[sbk]])
                    k.op("act", lambda: nc.scalar.activation(out=Et[e][:], in_=ps[sbk][:, :], func=AF.Exp), reads=[RP[sbk]], writes=[REt[e]])

                def av(i):
                    kc = kcs[i]; e = e_of[i]
                    mm(ps[obank][0:65, :], vaug[:, kc, hl, 0:65], Et[e][:], i == 0, i == n - 1, [RV[kc], REt[e]], [RP[obank]], inc=(i == n - 1))

                for i in range(min(2, n)): qk(i)
                for i in range(n):
                    if i + 2 < n: qk(i + 2)
                    av(i)
                    if i > 0 and i % 4 == 0: advance_deferred()

            def epi_head(obank):
                u = ctr["u"] % NUF; ctr["u"] += 1
                r = ctr["r"] % 2; ctr["r"] += 1
                k.op("dve", lambda: nc.vector.tensor_copy(out=Uf[u][:], in_=ps[obank][0:65, :]), reads=[RP[obank]], writes=[RU[u]])
                k.op("act", lambda: nc.scalar.activation(out=rec[r][:], in_=Uf[u][0:1, :], func=AF.Ln), reads=[RU[u]], writes=[Rrec[r]])
                k.op("act", lambda: nc.scalar.activation(out=rec[r][:], in_=rec[r][:], func=AF.Exp, scale=-1.0), reads=[Rrec[r]], writes=[Rrec[r]])
                return u, r

            def finish_plain(obank, dst_ap, dst_res, after=None):
                u, r = epi_head(obank)

                def st1(u=u, r=r, dst_ap=dst_ap, dst_res=dst_res):
                    mm(ps[7][0:65, :], cf[0:1, 0:65], rec[r][:], True, True, [Rcf, Rrec[r]], [RP[7]])
                    k.op("dve", lambda: nc.vector.tensor_tensor(out=dst_ap, in0=Uf[u][:], in1=ps[7][0:65, :], op=ALU.mult),
                         reads=[RU[u], RP[7]], writes=dst_res)
                deferred.append([st1] + ([after] if after is not None else []))
                return u

            def wo_apply(mix_ap_fn, mix_res, nh, qb):
                qsl = slice(qb * TB, (qb + 1) * TB)
                hl = list(range(nh)) if isinstance(nh, int) else nh
                for dc in range(NC8):
                    bank = 5 + ctr["wo"] % 2; ctr["wo"] += 1
                    for hi_, hh in enumerate(hl):
                        mm(ps[bank][:, :], woa[0:65, hh, dc * 128:(dc + 1) * 128], mix_ap_fn(hh), hi_ == 0, hi_ == len(hl) - 1,
                           [Rwo] + mix_res, [RP[bank]], inc=(hi_ == len(hl) - 1))
                    k.op("dve", lambda: nc.vector.tensor_tensor(out=x_sb[:, dc, qsl], in0=ps[bank][:, :], in1=x_sb[:, dc, qsl], op=ALU.add),
                         reads=[RP[bank], RX[dc][qb]], writes=[RX[dc][qb]])

            def mixer_diff():
                sc = Scope(ar)
                dq = [sc("dq%d" % i, [64, S], BF16) for i in range(4)]; Rdq = [[Res("dq%d_%d" % (i, t)) for t in range(NTB)] for i in range(4)]
                dk = [sc("dk%d" % i, [64, S], BF16) for i in range(4)]; Rdk = [[Res("dk%d_%d" % (i, t)) for t in range(NTB)] for i in range(4)]
                bwin = [sc("bwin%d" % i, [128, WIN], BF16) for i in range(2)]; Rbw = [Res("bwin%d" % i) for i in range(2)]
                mixT = [sc("mixT%d" % i, [65, 4, TB], BF16) for i in range(2)]; Rmx = [Res("mixT%d" % i) for i in range(2)]
                init_vaug()
                load_wo(0, 4)
                g32 = cb[0:64, C_G32:C_G32 + 64]
                s0 = load_win(0, 256)
                s1 = load_win(256, 256)
                for hd in range(4):
                    for tb in range(NTB):
                        tsl = slice(tb * TB, (tb + 1) * TB)
                        proj_qk(s0, hd * 64, 64, tb, g32, 1.0 / 32, pk[0:64, l, 24:25], dq[hd][:, tsl], [Rdq[hd][tb]])
                s2 = load_win(512, 256)
                for hd in range(4):
                    for tb in range(NTB):
                        tsl = slice(tb * TB, (tb + 1) * TB)
                        proj_qk(s1, hd * 64, 64, tb, g32, 1.0 / 32, pk[0:64, l, 25:26], dk[hd][:, tsl], [Rdk[hd][tb]])
                proj_v(lambda tcn: [(h[:, c, tcn * 128:(tcn + 1) * 128], RH[c][tcn // 4]) for c in range(NC8)], None,
                       lambda ci: wsl[s2][:, ci, 0:256], [Rwsl[s2]], 4)
                kcs = list(range(16))
                m65 = cb[0:65, C_M65:C_M65 + 65]
                warm()

                def load_bias(hd, qb):
                    s = ctr["b"] % 2; ctr["b"] += 1
                    k.dma("pool", bwin[s][:, :], P["gbias"][hd, :, qb * TB:qb * TB + WIN], writes=[Rbw[s]])
                    return s
                order = [(qb, hd) for qb in range(NTB) for hd in range(4)]
                bs = {order[0]: load_bias(order[0][1], order[0][0])}
                for oi, (qb, hd) in enumerate(order):
                    if oi + 1 < len(order):
                        nq, nh_ = order[oi + 1]
                        bs[order[oi + 1]] = load_bias(nh_, nq)
                    s = bs[(qb, hd)]
                    qsl = slice(qb * TB, (qb + 1) * TB)
                    m = qb % 2
                    kres = [Rdk[hd][t] for t in range(NTB)]
                    us = []
                    for c2 in range(2):
                        rows = slice(c2 * 32, (c2 + 1) * 32)
                        ob = 3 + c2
                        attn_pass(dq[hd][rows, qsl], [Rdq[hd][qb]], lambda kc: dk[hd][rows, kc * 128:(kc + 1) * 128], kres, hd,
                                  lambda kc: bwin[s][:, 1920 - kc * 128:1920 - kc * 128 + TB], [Rbw[s]], kcs, ob)
                        u, r = epi_head(ob)
                        us.append((u, r))
                    (u0, r0), (u1, r1) = us

                    stt = {}

                    def st1(u0=u0, r0=r0, u1=u1, r1=r1, stt=stt):
                        mm(ps[7][0:65, :], cf[0:1, 0:65], rec[r0][:], True, True, [Rcf, Rrec[r0]], [RP[7]])
                        mm(ps[6][0:65, :], cf[0:1, 0:65], rec[r1][:], True, True, [Rcf, Rrec[r1]], [RP[6]])
                        k.op("dve", lambda: nc.vector.tensor_tensor(out=Uf[u0][:], in0=Uf[u0][:], in1=ps[7][0:65, :], op=ALU.mult),
                             reads=[RU[u0], RP[7]], writes=[RU[u0]])
                        k.op("dve", lambda: nc.vector.scalar_tensor_tensor(out=Uf[u1][:], in0=Uf[u1][:], scalar=neglam, in1=ps[6][0:65, :],
                                                                           op0=ALU.mult, op1=ALU.mult),
                             reads=[RU[u1], RP[6], Rlam], writes=[RU[u1]])
                        k.op("dve", lambda: nc.vector.tensor_tensor(out=Uf[u1][:], in0=Uf[u1][:], in1=Uf[u0][:], op=ALU.add),
                             reads=[RU[u0], RU[u1]], writes=[RU[u1]])
                        i = ctr["st"] % 2; ctr["st"] += 1; stt["i"] = i
                        k.op("dve", lambda: nc.vector.tensor_tensor(out=sqb[i][0:65, :], in0=Uf[u1][:], in1=Uf[u1][:], op=ALU.mult),
                             reads=[RU[u1]], writes=[Rsqb[i]])

                    def st2(u1=u1, m=m, hd=hd, stt=stt):
                        i = stt["i"]
                        mm(ps[7][0:65, :], m65, sqb[i][0:65, :], True, True, [Rcb, Rsqb[i]], [RP[7]])
                        k.op("act", lambda: nc.scalar.activation(out=rsd[i][0:65, :], in_=ps[7][0:65, :], func=AF.Ln, scale=1.0 / 64, bias=epsb[0:65, 0:1]),
                             reads=[RP[7], Reps], writes=[Rrsd[i]])
                        k.op("act", lambda: nc.scalar.activation(out=rsd[i][0:65, :], in_=rsd[i][0:65, :], func=AF.Exp, scale=-0.5),
                             reads=[Rrsd[i]], writes=[Rrsd[i]])
                        k.op("dve", lambda: nc.vector.scalar_tensor_tensor(out=mixT[m][:, hd, :], in0=Uf[u1][:], scalar=pk[0:65, l, 33:34], in1=rsd[i][0:65, :],
                                                                           op0=ALU.mult, op1=ALU.mult),
                             reads=[RU[u1], Rrsd[i], Rpk], writes=[Rmx[m]])

                    def st3(m=m, qb=qb):
                        wo_apply(lambda hh: mixT[m][:, hh, :], [Rmx[m]], 4, qb)
                    deferred.append([st1, st2] + ([st3] if hd == 3 else []))
                flush_deferred()
                k.barrier()
                sc.close()

            def mixer_dil_proj():
                sc = Scope(ar)
                lq = [sc("lq%d" % i, [128, S], BF16) for i in range(3)]; Rlq = [[Res("lq%d_%d" % (i, t)) for t in range(NTB)] for i in range(3)]
                lk = [sc("lk%d" % i, [128, S], BF16) for i in range(3)]; Rlk = [[Res("lk%d_%d" % (i, t)) for t in range(NTB)] for i in range(3)]
                g64 = cb[:, C_G64:C_G64 + 128]
                init_vaug()
                warm()
                s0 = load_win(768, 384)
                s1 = load_win(1152, 384)
                for pr in range(3):
                    for tb in range(NTB):
                        tsl = slice(tb * TB, (tb + 1) * TB)
                        proj_qk(s0, pr * 128, 128, tb, g64, 1.0 / 64, pk[:, l, 26:27], lq[pr][:, tsl], [Rlq[pr][tb]])
                s2 = load_win(1536, 384)
                for pr in range(3):
                    for tb in range(NTB):
                        tsl = slice(tb * TB, (tb + 1) * TB)
                        proj_qk(s1, pr * 128, 128, tb, g64, 1.0 / 64, pk[:, l, 27:28], lk[pr][:, tsl], [Rlk[pr][tb]])
                proj_v(lambda tcn: [(h[:, c, tcn * 128:(tcn + 1) * 128], RH[c][tcn // 4]) for c in range(NC8)], None,
                       lambda ci: wsl[s2][:, ci, 0:384], [Rwsl[s2]], 6)
                return sc, lq, Rlq, lk, Rlk

            def mixer_dil_attn(sc, lq, Rlq, lk, Rlk):
                bwin = [sc("bwin%d" % i, [128, WIN], BF16) for i in range(2)]; Rbw = [Res("bwin%d" % i) for i in range(2)]
                lcw = sc("lcw", [128, WIN], BF16); Rlc = Res("lcw")
                mixT = [sc("mixT%d" % i, [65, 6, TB], BF16) for i in range(2)]; Rmx = [Res("mixT%d" % i) for i in range(2)]
                load_wo(4, 6)
                warm()

                def load_bias(hd, qb):
                    s = ctr["b"] % 2; ctr["b"] += 1
                    k.dma("pool", bwin[s][:, :], P["gbias"][4 + hd, :, qb * TB:qb * TB + WIN], writes=[Rbw[s]])
                    return s
                order = [(qb, hd) for qb in range(NTB) for hd in range(6)]
                bs = {order[0]: load_bias(order[0][1], order[0][0])}
                for oi, (qb, hd) in enumerate(order):
                    if hd == 0:
                        k.dma("pool", lcw[:, :], P["logc"][:, qb * TB:qb * TB + WIN], writes=[Rlc])
                    if oi + 1 < len(order):
                        nq, nh_ = order[oi + 1]
                        bs[order[oi + 1]] = load_bias(nh_, nq)
                    s = bs[(qb, hd)]
                    k.op("dve", lambda: nc.vector.tensor_tensor(out=bwin[s][:, :], in0=bwin[s][:, :], in1=lcw[:, :], op=ALU.add),
                         reads=[Rbw[s], Rlc], writes=[Rbw[s]])
                    qsl = slice(qb * TB, (qb + 1) * TB)
                    m = qb % 2
                    pr = hd // 2; rows = slice((hd % 2) * 64, (hd % 2) * 64 + 64)
                    kcs = [kc for kc in range(16) if -1151 <= kc * 128 - qb * TB <= 1535]
                    ob = 3 + ctr["o"] % 2; ctr["o"] += 1
                    attn_pass(lq[pr][rows, qsl], [Rlq[pr][qb]], lambda kc: lk[pr][rows, kc * 128:(kc + 1) * 128], [Rlk[pr][t] for t in range(NTB)], hd,
                              lambda kc: bwin[s][:, 1920 - kc * 128:1920 - kc * 128 + TB], [Rbw[s]], kcs, ob)
                    aft = (lambda m=m, qb=qb: wo_apply(lambda hh: mixT[m][:, hh, :], [Rmx[m]], 6, qb)) if hd == 5 else None
                    finish_plain(ob, mixT[m][:, hd, :], [Rmx[m]], after=aft)
                flush_deferred()
                k.barrier()
                sc.close()

            def mixer_mla_latents():
                sc = Scope(ar)
                qd = sc("qd", [128, 2, S], BF16); Rqd = [Res("qd%d" % t) for t in range(NTB)]
                ckv = sc("ckv", [128, S], BF16); Rckv = [Res("ckv%d" % t) for t in range(NTB)]
                krp = sc("krp", [32, S], BF16); Rkrp = [Res("krp%d" % t) for t in range(NTB)]
                s0 = load_win(1920, 256)
                s1 = load_win(2176, 160)
                for tb in range(NTB):
                    tsl = slice(tb * TB, (tb + 1) * TB)
                    for j in range(2):
                        for c in range(NC8):
                            mm(ps[5 + j][:, :], wsl[s0][:, c, j * 128:(j + 1) * 128], h[:, c, tsl], c == 0, c == NC8 - 1,
                               [Rwsl[s0], RH[c][tb]], [RP[5 + j]], inc=(c == NC8 - 1))
                    for j in range(2):
                        k.op("act", lambda: nc.scalar.activation(out=sqb[j][:, :], in_=ps[5 + j][:, :], func=AF.Square),
                             reads=[RP[5 + j]], writes=[Rsqb[j]])
                        mm(ps[7][:, :], ones_bf, sqb[j][:, :], j == 0, j == 1, [Rcb, Rsqb[j]], [RP[7]])
                    k.op("act", lambda: nc.scalar.activation(out=rsd[0][:, :], in_=ps[7][:, :], func=AF.Ln, scale=1.0 / 256, bias=epsb[:, 0:1]),
                         reads=[RP[7], Reps], writes=[Rrsd[0]])
                    k.op("act", lambda: nc.scalar.activation(out=rsd[0][:, :], in_=rsd[0][:, :], func=AF.Exp, scale=-0.5), reads=[Rrsd[0]], writes=[Rrsd[0]])
                    for j in range(2):
                        k.op("dve", lambda: nc.vector.scalar_tensor_tensor(out=qd[:, j, tsl], in0=ps[5 + j][:, :], scalar=pk[:, l, 28 + j:29 + j], in1=rsd[0][:, :],
                                                                           op0=ALU.mult, op1=ALU.mult),
                             reads=[RP[5 + j], Rrsd[0], Rpk], writes=[Rqd[tb]])
                    for c in range(NC8):
                        mm(ps[5][:, :], wsl[s1][:, c, 0:128], h[:, c, tsl], c == 0, c == NC8 - 1, [Rwsl[s1], RH[c][tb]], [RP[5]], inc=(c == NC8 - 1))
                    for c in range(NC8):
                        mm(ps[6][0:32, :], wsl[s1][:, c, 128:160], h[:, c, tsl], c == 0, c == NC8 - 1, [Rwsl[s1], RH[c][tb]], [RP[6]], inc=(c == NC8 - 1))
                    i = stat_rstd(ps[5][:, :], 128, ones_bf, 1.0 / 128, [RP[5]])
                    k.op("dve", lambda: nc.vector.scalar_tensor_tensor(out=ckv[:, tsl], in0=ps[5][:, :], scalar=pk[:, l, 30:31], in1=rsd[i][:, :],
                                                                       op0=ALU.mult, op1=ALU.mult),
                         reads=[RP[5], Rrsd[i], Rpk], writes=[Rckv[tb]])
                    k.op("dve", lambda: nc.vector.tensor_copy(out=krp[:, tsl], in_=ps[6][0:32, :]), reads=[RP[6]], writes=[Rkrp[tb]])
                return sc, qd, Rqd, ckv, Rckv, krp, Rkrp

            def mixer_mla_attn(sc, qd, Rqd, ckv, Rckv, krp, Rkrp):
                qup = sc("qup", [128, 2, 576], BF16); Rqup = Res("qup")
                knw = sc("knw", [128, 6, 96], BF16); Rknw = Res("knw")
                kvw = sc("kvw", [128, 6, 64], BF16); Rkvw = Res("kvw")
                rC = sc("ropeC", [96, S], F32); rS = sc("ropeS", [96, S], F32); Rrope = Res("rope")
                mqh = [sc("mqh%d" % i, [96, S], BF16) for i in range(2)]; Rmq = [[Res("mq%d_%d" % (i, t)) for t in range(NTB)] for i in range(2)]
                mkh = [sc("mkh%d" % i, [96, S], BF16) for i in range(2)]; Rmk = [[Res("mk%d_%d" % (i, t)) for t in range(NTB)] for i in range(2)]
                qn = [sc("qn%d" % i, [96, TB], BF16) for i in range(2)]; Rqn = [Res("qn%d" % i) for i in range(2)]
                t1 = sc("t1", [96, TB], F32); Rt1 = Res("t1")
                t2 = sc("t2", [96, TB], F32); Rt2 = Res("t2")
                mixT = [sc("mixTc%d" % i, [65, TB], BF16) for i in range(2)]; Rmx = [Res("mxc%d" % i) for i in range(2)]
                k.dma("pool", qup[:, :, :], P["mla_q_up"][l].rearrange("(c p) n -> p c n", p=128), writes=[Rqup])
                k.op("dve", lambda: nc.vector.memset(knw[:, :, :], 0.0), writes=[Rknw])
                kvv = P["mla_kv_up"][l].rearrange("p (h t d) -> p h t d", h=6, t=2)
                k.dma("pool", knw[:, :, 0:64], kvv[:, :, 0, :], writes=[Rknw])
                k.dma("pool", kvw[:, :, :], kvv[:, :, 1, :], writes=[Rkvw])
                k.dma("sp", rC[:, :], P["ropeC"][:, :], writes=[Rrope])
                k.dma("sp", rS[:, :], P["ropeS"][:, :], writes=[Rrope])
                init_vaug()
                load_wo(10, 6)
                warm()
                proj_v(lambda tcn: [(ckv[:, tcn * 128:(tcn + 1) * 128], Rckv[tcn // 4])], None,
                       lambda ci: kvw[:, :, :].rearrange("p h d -> p (h d)"), [Rkvw], 6)
                o96 = cb[0:96, C_ONES:C_ONES + 96]
                rt = cb[0:96, C_RT:C_RT + 96]
                sel = cb[0:32, C_SEL:C_SEL + 96]
                nn = {"q": 0}

                def norm_rope(bank, gcol, dst_ap, dst_res, tsl):
                    i = stat_rstd(ps[bank][0:96, :], 96, o96, 1.0 / 96, [RP[bank]])
                    j = nn["q"] % 2; nn["q"] += 1
                    k.op("dve", lambda: nc.vector.scalar_tensor_tensor(out=qn[j][:, :], in0=ps[bank][0:96, :], scalar=gcol, in1=rsd[i][0:96, :],
                                                                       op0=ALU.mult, op1=ALU.mult),
                         reads=[RP[bank], Rrsd[i], Rpk], writes=[Rqn[j]])
                    rb = 3 + nn["q"] % 2
                    mm(ps[rb][0:96, :], rt, qn[j][:, :], True, True, [Rcb, Rqn[j]], [RP[rb]])
                    k.op("dve", lambda: nc.vector.tensor_tensor(out=t1[:, :], in0=qn[j][:, :], in1=rC[:, tsl], op=ALU.mult),
                         reads=[Rqn[j], Rrope], writes=[Rt1])
                    k.op("dve", lambda: nc.vector.tensor_tensor(out=t2[:, :], in0=ps[rb][0:96, :], in1=rS[:, tsl], op=ALU.mult),
                         reads=[RP[rb], Rrope], writes=[Rt2])
                    k.op("dve", lambda: nc.vector.tensor_tensor(out=dst_ap, in0=t1[:, :], in1=t2[:, :], op=ALU.add),
                         reads=[Rt1, Rt2], writes=dst_res)

                kcs = list(range(16))
                warm()
                for hd in range(6):
                    sl = hd % 2
                    for tb in range(NTB):
                        tsl = slice(tb * TB, (tb + 1) * TB)
                        bank = 5 + ctr["pj"] % 2; ctr["pj"] += 1
                        for j in range(2):
                            mm(ps[bank][0:96, :], qup[:, j, hd * 96:(hd + 1) * 96], qd[:, j, tsl], j == 0, j == 1, [Rqup, Rqd[tb]], [RP[bank]], inc=(j == 1))
                        norm_rope(bank, pk[0:96, l, 31:32], mqh[sl][:, tsl], [Rmq[sl][tb]], tsl)
                        bank = 5 + ctr["pj"] % 2; ctr["pj"] += 1
                        mm(ps[bank][0:96, :], knw[:, hd, :], ckv[:, tsl], True, False, [Rknw, Rckv[tb]], [RP[bank]], inc=False)
                        mm(ps[bank][0:96, :], sel, krp[:, tsl], False, True, [Rcb, Rkrp[tb]], [RP[bank]])
                        norm_rope(bank, pk[0:96, l, 32:33], mkh[sl][:, tsl], [Rmk[sl][tb]], tsl)
                    for qb in range(NTB):
                        qsl = slice(qb * TB, (qb + 1) * TB)
                        ob = 3 + ctr["o"] % 2; ctr["o"] += 1
                        attn_pass(mqh[sl][:, qsl], [Rmq[sl][qb]], lambda kc: mkh[sl][:, kc * 128:(kc + 1) * 128], [Rmk[sl][t] for t in range(NTB)], hd,
                                  None, [], kcs, ob)
                        m = ctr["o"] % 2
                        finish_plain(ob, mixT[m][:, :], [Rmx[m]],
                                     after=(lambda m=m, hd=hd, qb=qb: wo_apply(lambda hh: mixT[m][:, :], [Rmx[m]], [hd], qb)))
                flush_deferred()
                k.barrier()
                sc.close()

            mixer_diff()
            dsc = mixer_dil_proj()
            msc = mixer_mla_latents()
            k.barrier()
            hs.close()
            mixer_dil_attn(*dsc)
            mixer_mla_attn(*msc)
            k.barrier()
            com.close()

        for stg in stages:
            kind, l = stg
            if kind == "ffn1": ffn(l, 1)
            elif kind == "ffn2": ffn(l, 2)
            elif kind == "mix": mixer(l)

        for c in range(NC8):
            k.dma("sp", yT[c * 128:(c + 1) * 128, :], x_sb[:, c, :], reads=RX[c])
        k.barrier()
        build.stats = (k.ninst, k.nwait, ar.peak)
    return nc


ALL_STAGES = [("ffn1", 0), ("mix", 0), ("ffn2", 0), ("ffn1", 1), ("mix", 1), ("ffn2", 1)]


def _gather_bias(rel_bias):
    idx = _t5_bucket(_rel_grid())
    return np.ascontiguousarray(np.transpose(rel_bias[idx], (2, 0, 1))).astype(np.float32)


def make_in_maps(inp, n_cores=8):
    cst, ropeC, ropeS, logc = _host_consts()
    pk = _pack_small(inp)
    gb = _gather_bias(np.asarray(inp["rel_bias"], np.float32))
    shared = {nm: np.ascontiguousarray(np.asarray(inp[nm], np.float32)) for nm, _ in WEIGHT_SPECS}
    shared.update({"pk": pk, "cst": cst, "ropeC": ropeC, "ropeS": ropeS, "logc": logc, "gbias": gb})
    maps = []
    for b in range(n_cores):
        m = dict(shared)
        m["xT"] = np.ascontiguousarray(np.asarray(inp["x"][b], np.float32).T)
        maps.append(m)
    return maps


def kernel(**inp):
    nc = build(ALL_STAGES)
    maps = make_in_maps(inp, 8)
    res = run_bass_kernel_spmd(nc, maps, core_ids=list(range(8)))
    out = np.stack([np.ascontiguousarray(r["yT"].T) for r in res.results], axis=0)
    return out.astype(np.float32)
```

```python
import math
import contextlib
import numpy as np
import concourse.bass as bass
import concourse.mybir as mybir
from concourse.bass_utils import run_bass_kernel_spmd

F32 = mybir.dt.float32
BF16 = mybir.dt.bfloat16
AF = mybir.ActivationFunctionType
ALU = mybir.AluOpType

D = 1024; S = 2048; DFF = 2816; L = 2; NC8 = 8; TB = 512; NTB = 4
EPS = 1e-6
INW = 2336
NEG = -30000.0
GW = 3968
WIN = 2432
C_ID = 0; C_ONES = 128; C_G32 = 256; C_G64 = 384; C_M65 = 512; C_RT = 640; C_SEL = 768; NCST = 896
NPK = 40


class Res:
    __slots__ = ("name", "w", "r")

    def __init__(self, name):
        self.name = name; self.w = None; self.r = {}


class K:
    def __init__(self, nc, es, n_dma_sems=8):
        self.nc = nc
        self.eng = {"pe": nc.tensor, "dve": nc.vector, "act": nc.scalar, "pool": nc.gpsimd, "sp": nc.sync}
        self.sem = {}; self.cnt = {}
        for e in self.eng:
            self.sem[e] = es.enter_context(nc.semaphore("s_" + e)); self.cnt[e] = 0
        self.seen = {e: {} for e in self.eng}
        self.pend = {e: ([], []) for e in self.eng}
        self.dq = {}
        for q in ("sp", "pool"):
            lst = []
            for i in range(n_dma_sems):
                nm = "d_%s%d" % (q, i)
                self.sem[nm] = es.enter_context(nc.semaphore(nm)); self.cnt[nm] = 0
                lst.append(nm)
            self.dq[q] = [lst, 0]
        self.nwait = 0; self.ninst = 0
        self.es = es; self.slot_sem = {}

    def _sem_for(self, q, writes):
        if not writes:
            lst, i = self.dq[q]
            self.dq[q][1] = i + 1
            return lst[i % len(lst)]
        key = writes[0].name
        if key not in self.slot_sem:
            nm = "ds_%d" % len(self.slot_sem)
            self.sem[nm] = self.es.enter_context(self.nc.semaphore(nm)); self.cnt[nm] = 0
            self.slot_sem[key] = nm
        return self.slot_sem[key]

    def _wait(self, e, f, c):
        if c <= 0 or self.seen[e].get(f, 0) >= c:
            return
        self.eng[e].wait_ge(self.sem[f], c)
        self.seen[e][f] = c; self.nwait += 1

    def op(self, e, fn, reads=(), writes=(), inc=True):
        need = {}
        for r in reads:
            if r.w is not None:
                f, c = r.w
                if f == e and e == "pe":
                    continue
                if c > need.get(f, 0): need[f] = c
        for w in writes:
            if w.w is not None:
                f, c = w.w
                if f != e and c > need.get(f, 0): need[f] = c
            for f, c in w.r.items():
                if f != e and c > need.get(f, 0): need[f] = c
        for f, c in need.items():
            self._wait(e, f, c)
        ins = fn()
        self.ninst += 1
        pr, pw = self.pend[e]
        pr.extend(reads); pw.extend(writes)
        if inc:
            ins.then_inc(self.sem[e], 1)
            self.cnt[e] += 1
            c = self.cnt[e]
            for r in pr: r.r[e] = c
            for w in pw:
                w.w = (e, c); w.r = {}
            self.pend[e] = ([], [])
        return ins

    def dma(self, q, out, in_, reads=(), writes=()):
        s = self._sem_for(q, list(writes))
        need = {}
        for r in reads:
            if r.w is not None:
                f, c = r.w
                if c > need.get(f, 0): need[f] = c
        for w in writes:
            if w.w is not None:
                f, c = w.w
                if c > need.get(f, 0): need[f] = c
            for f, c in w.r.items():
                if c > need.get(f, 0): need[f] = c
        for f, c in need.items():
            self._wait(q, f, c)
        self._wait(q, s, self.cnt[s])
        self.eng[q].dma_start(out=out, in_=in_).then_inc(self.sem[s], 16)
        self.cnt[s] += 16
        c = self.cnt[s]
        for r in reads: r.r[s] = c
        for w in writes:
            w.w = (s, c); w.r = {}
        self.ninst += 1

    def barrier(self):
        for e in self.eng:
            for f in self.cnt:
                if f != e: self._wait(e, f, self.cnt[f])


class Arena:
    def __init__(self, nc, lo, hi):
        self.nc = nc; self.free = [(lo, hi)]; self.used = {}; self.n = 0; self.peak = 0; self.hi = hi

    def alloc(self, name, shape, dt):
        esz = 4 if dt == F32 else 2
        size = esz
        for d in shape[1:]: size *= d
        size = (size + 63) // 64 * 64
        for i, (a, b) in enumerate(self.free):
            if b - a >= size:
                self.free[i] = (a + size, b)
                if self.free[i][0] == self.free[i][1]: self.free.pop(i)
                self.n += 1
                t = self.nc.alloc_sbuf_tensor_at("%s_%d" % (name, self.n), list(shape), dt, offset=a)
                self.used[id(t)] = (a, a + size, t)
                self.peak = max(self.peak, a + size)
                return t
        raise RuntimeError("SBUF arena full allocating %s %s (free=%s)" % (name, shape, self.free))

    def release(self, ts):
        for t in ts:
            a, b, _ = self.used.pop(id(t))
            self.free.append((a, b))
        self.free.sort()
        m = []
        for a, b in self.free:
            if m and m[-1][1] == a: m[-1] = (m[-1][0], b)
            else: m.append((a, b))
        self.free = m


class Scope:
    def __init__(self, ar): self.ar = ar; self.ts = []
    def __call__(self, name, shape, dt):
        t = self.ar.alloc(name, shape, dt); self.ts.append(t); return t
    def close(self):
        self.ar.release(self.ts); self.ts = []


def _t5_bucket(rel):
    rel = np.asarray(rel, dtype=np.int64)
    n = np.abs(rel)
    nf = np.maximum(n, 1).astype(np.float64)
    large = 8 + np.floor(np.log(nf / 8.0) / math.log(128.0) * 8.0 + 1e-9).astype(np.int64)
    large = np.where(n < 8, 0, large)
    large = np.minimum(large, 15)
    return np.where(rel > 0, 16, 0) + np.where(n < 8, n, large)


def _rel_grid():
    kp = np.arange(128)[:, None]; j = np.arange(GW)[None, :]
    return kp - j + 1920


def _host_consts():
    cst = np.zeros((128, NCST), np.float32)
    cst[:, C_ID:C_ID + 128] = np.eye(128, dtype=np.float32)
    cst[:, C_ONES:C_ONES + 128] = 1.0
    for g in range(4):
        cst[g * 32:(g + 1) * 32, C_G32 + g * 32:C_G32 + (g + 1) * 32] = 1.0
    for g in range(2):
        cst[g * 64:(g + 1) * 64, C_G64 + g * 64:C_G64 + (g + 1) * 64] = 1.0
    cst[1:65, C_M65:C_M65 + 65] = 1.0
    for m in range(64, 80):
        cst[m + 16, C_RT + m] = -1.0
    for m in range(80, 96):
        cst[m - 16, C_RT + m] = 1.0
    for j in range(32):
        cst[j, C_SEL + 64 + j] = 1.0
    inv = (10000.0 ** (-(np.arange(16, dtype=np.float32) / np.float32(16)))).astype(np.float32)
    ang = (np.arange(S, dtype=np.float32)[None, :] * inv[:, None]).astype(np.float32)
    ropeC = np.ones((96, S), np.float32); ropeS = np.zeros((96, S), np.float32)
    ropeC[64:80] = np.cos(ang.astype(np.float64)); ropeC[80:96] = ropeC[64:80]
    ropeS[64:80] = np.sin(ang.astype(np.float64)); ropeS[80:96] = ropeS[64:80]
    rel = _rel_grid()
    cnt = (np.abs(rel) <= 64).astype(np.int64) + ((rel % 4 == 0) & (np.abs(rel) <= 256)) + ((rel % 16 == 0) & (np.abs(rel) <= 1024))
    logc = np.where(cnt > 0, np.log(np.maximum(cnt, 1).astype(np.float64)), NEG).astype(np.float32)
    return cst, ropeC, ropeS, logc


def _pack_small(inp):
    pk = np.zeros((128, L, NPK), np.float32)
    for l in range(L):
        for i, nm in enumerate(("ffn1_norm", "mix_norm", "ffn2_norm")):
            pk[:, l, i * 8:(i + 1) * 8] = inp[nm][l].reshape(8, 128).T
        pk[0:64, l, 24] = np.tile(inp["diff_q_norm"][l], 2)
        pk[0:64, l, 25] = np.tile(inp["diff_k_norm"][l], 2)
        pk[0:128, l, 26] = np.tile(inp["dil_q_norm"][l], 2)
        pk[0:128, l, 27] = np.tile(inp["dil_k_norm"][l], 2)
        pk[:, l, 28:30] = inp["mla_q_norm"][l].reshape(2, 128).T
        pk[:, l, 30] = inp["mla_kv_norm"][l]
        pk[0:96, l, 31] = inp["mla_qn"][l]
        pk[0:96, l, 32] = inp["mla_kn"][l]
        pk[1:65, l, 33] = inp["diff_subln"][l]
        pk[0:32, l, 34:38] = inp["diff_lambda"][l].T
    return pk


WEIGHT_SPECS = [
    ("ffn1_wg", [L, D, DFF]), ("ffn1_wu", [L, D, DFF]), ("ffn1_wd", [L, DFF, D]),
    ("ffn2_wg", [L, D, DFF]), ("ffn2_wu", [L, D, DFF]), ("ffn2_wd", [L, DFF, D]),
    ("w_in", [L, D, INW]), ("w_o", [L, D, D]),
    ("mla_q_up", [L, 256, 576]), ("mla_kv_up", [L, 128, 768]),
]


def build(stages, dbg=None):
    nc = bass.Bass("TRN2", target_bir_lowering=False)
    P = {}
    P["xT"] = nc.dram_tensor("xT", [D, S], F32, kind="ExternalInput").ap()
    for nm, shp in WEIGHT_SPECS:
        P[nm] = nc.dram_tensor(nm, shp, F32, kind="ExternalInput").ap()
    P["pk"] = nc.dram_tensor("pk", [128, L, NPK], F32, kind="ExternalInput").ap()
    P["cst"] = nc.dram_tensor("cst", [128, NCST], F32, kind="ExternalInput").ap()
    P["ropeC"] = nc.dram_tensor("ropeC", [96, S], F32, kind="ExternalInput").ap()
    P["ropeS"] = nc.dram_tensor("ropeS", [96, S], F32, kind="ExternalInput").ap()
    P["logc"] = nc.dram_tensor("logc", [128, GW], F32, kind="ExternalInput").ap()
    P["gbias"] = nc.dram_tensor("gbias", [10, 128, GW], F32, kind="ExternalInput").ap()
    yT = nc.dram_tensor("yT", [D, S], F32, kind="ExternalOutput").ap()

    with contextlib.ExitStack() as es:
        k = K(nc, es)
        ar = Arena(nc, 16512, 229376)
        glob = Scope(ar)

        x_sb = glob("x_sb", [128, NC8, S], F32)
        RX = [[Res("x%d_%d" % (c, t)) for t in range(NTB)] for c in range(NC8)]
        cb = glob("cb", [128, NCST], BF16); Rcb = Res("cb")
        cf = glob("cf", [128, 128], F32); Rcf = Res("cf")
        pk = glob("pk_sb", [128, L, NPK], F32); Rpk = Res("pk")
        epsb = glob("epsb", [128, 1], F32); Reps = Res("eps")
        ps = [es.enter_context(nc.psum_tensor("ps%d" % i, [128, 512], F32)) for i in range(8)]
        RP = [Res("ps%d" % i) for i in range(8)]

        def mm(out, lhsT, rhs, start, stop, reads, writes, inc=True):
            return k.op("pe", lambda: nc.tensor.matmul(out, lhsT=lhsT, rhs=rhs, start=start, stop=stop), reads, writes, inc)

        def warm(n=32):
            for i in range(n):
                mm(ps[7][:, :], cb[:, C_ID:C_ID + 128], cb[:, 0:512], True, True, [Rcb], [RP[7]], inc=(i == n - 1))

        for c in range(NC8):
            k.dma("sp", x_sb[:, c, :], P["xT"][c * 128:(c + 1) * 128, :], writes=RX[c])
        k.dma("pool", cb[:], P["cst"][:, :], writes=[Rcb])
        k.dma("sp", cf[:], P["cst"][:, C_ONES:C_ONES + 128], writes=[Rcf])
        k.dma("sp", pk[:], P["pk"][:, :, :], writes=[Rpk])
        k.op("dve", lambda: nc.vector.memset(epsb[:], EPS), writes=[Reps])
        for l in range(L):
            lam_init = 0.8 - 0.6 * math.exp(-0.3 * (l + 1))
            for col, sc in ((24, 32 ** -0.5), (26, 64 ** -0.5), (31, 96 ** -0.5), (33, 1.0 - lam_init)):
                k.op("dve", lambda: nc.vector.tensor_scalar(
                    out=pk[:, l, col:col + 1], in0=pk[:, l, col:col + 1], scalar1=float(sc), scalar2=None,
                    op0=ALU.mult), reads=[Rpk], writes=[Rpk])

        ones_bf = cb[:, C_ONES:C_ONES + 128]
        ident_bf = cb[:, C_ID:C_ID + 128]

        def rmsnorm_h(h, RH, gcol, stat_bank):
            sc = Scope(ar)
            sq = [sc("nsq%d" % i, [128, TB], BF16) for i in range(4)]
            Rsq = [Res("nsq%d" % i) for i in range(4)]
            lnv = sc("nlnv", [128, TB], F32); Rln = Res("nlnv")
            rstd = [sc("nrstd%d" % i, [128, TB], F32) for i in range(2)]
            Rrs = [Res("nrstd%d" % i) for i in range(2)]
            n = 0
            for tb in range(NTB):
                tsl = slice(tb * TB, (tb + 1) * TB)
                for c in range(NC8):
                    i = n % 4; n += 1
                    k.op("act", lambda: nc.scalar.activation(out=sq[i][:], in_=x_sb[:, c, tsl], func=AF.Square),
                         reads=[RX[c][tb]], writes=[Rsq[i]])
                    mm(ps[stat_bank][:], ones_bf, sq[i][:], c == 0, c == NC8 - 1, [Rsq[i], Rcb], [RP[stat_bank]])
                r = tb % 2
                k.op("act", lambda: nc.scalar.activation(out=lnv[:], in_=ps[stat_bank][:], func=AF.Ln, scale=1.0 / D, bias=epsb[:, 0:1]),
                     reads=[RP[stat_bank], Reps], writes=[Rln])
                k.op("act", lambda: nc.scalar.activation(out=rstd[r][:], in_=lnv[:], func=AF.Exp, scale=-0.5),
                     reads=[Rln], writes=[Rrs[r]])
                for c in range(NC8):
                    k.op("dve", lambda: nc.vector.scalar_tensor_tensor(
                        out=h[:, c, tsl], in0=x_sb[:, c, tsl], scalar=gcol[:, c:c + 1], in1=rstd[r][:],
                        op0=ALU.mult, op1=ALU.mult), reads=[RX[c][tb], Rrs[r], Rpk], writes=[RH[c][tb]])
            k.barrier()
            sc.close()

        def ffn(l, which):
            wg = P["ffn%d_wg" % which]; wu = P["ffn%d_wu" % which]; wd = P["ffn%d_wd" % which]
            gcol = pk[:, l, (0 if which == 1 else 16):(8 if which == 1 else 24)]
            st = Scope(ar)
            h = st("f_h", [128, NC8, S], BF16)
            RH = [[Res("h%d_%d" % (c, t)) for t in range(NTB)] for c in range(NC8)]
            act = st("f_act", [128, 12, S], BF16)
            RA = [[Res("a%d_%d" % (c, t)) for t in range(NTB)] for c in range(12)]
            wgu = [st("f_wgu%d" % i, [128, 2, NC8, 256], BF16) for i in range(2)]
            Rwgu = [Res("wgu%d" % i) for i in range(2)]
            wdt = [st("f_wd%d" % i, [128, 12, 256], BF16) for i in range(2)]
            Rwd = [Res("wd%d" % i) for i in range(2)]
            sg = [st("f_sg%d" % i, [128, TB], F32) for i in range(2)]
            Rsg = [Res("sg%d" % i) for i in range(2)]
            halves = [(0, 6), (6, 11)]
            seq = []
            for hi, (g0, g1) in enumerate(halves):
                for g in range(g0, g1): seq.append(("up", hi, g))
                for dg in range(4): seq.append(("dn", hi, dg))
            cnt = {"up": 0, "dn": 0, "gu": 0, "d": 0}
            slot_of = {}

            def load(item):
                kind, hi, g = item
                g0, g1 = halves[hi]
                if kind == "up":
                    s = cnt["up"] % 2; cnt["up"] += 1
                    slot_of[item] = s
                    for j, w in enumerate((wg, wu)):
                        k.dma("pool", wgu[s][:, j, :, :], w[l, :, g * 256:(g + 1) * 256].rearrange("(c p) f -> p c f", p=128),
                              writes=[Rwgu[s]])
                else:
                    s = cnt["dn"] % 2; cnt["dn"] += 1
                    slot_of[item] = s
                    nf = (g1 - g0) * 2
                    k.dma("pool", wdt[s][:, 0:nf, :], wd[l, g0 * 256:g1 * 256, g * 256:(g + 1) * 256].rearrange("(c p) d -> p c d", p=128),
                          writes=[Rwd[s]])

            def compute(item):
                kind, hi, g = item
                g0, g1 = halves[hi]
                s = slot_of[item]
                if kind == "up":
                    for j in range(2):
                        fi = (g - g0) * 2 + j
                        for tb in range(NTB):
                            tsl = slice(tb * TB, (tb + 1) * TB)
                            n = cnt["gu"] % 2; cnt["gu"] += 1
                            for wi, bank in ((0, n), (1, 2 + n)):
                                for c in range(NC8):
                                    mm(ps[bank][:], wgu[s][:, wi, c, j * 128:(j + 1) * 128], h[:, c, tsl], c == 0, c == NC8 - 1,
                                       [Rwgu[s], RH[c][tb]], [RP[bank]], inc=(c == NC8 - 1))
                            k.op("act", lambda: nc.scalar.activation(out=sg[n][:], in_=ps[n][:], func=AF.Silu),
                                 reads=[RP[n]], writes=[Rsg[n]])
                            k.op("dve", lambda: nc.vector.tensor_tensor(
                                out=act[:, fi, tsl], in0=sg[n][:], in1=ps[2 + n][:], op=ALU.mult),
                                reads=[Rsg[n], RP[2 + n]], writes=[RA[fi][tb]])
                else:
                    nf = (g1 - g0) * 2
                    for dj in range(2):
                        dc = g * 2 + dj
                        for tb in range(NTB):
                            tsl = slice(tb * TB, (tb + 1) * TB)
                            bank = 4 + cnt["d"] % 2; cnt["d"] += 1
                            for fi in range(nf):
                                mm(ps[bank][:], wdt[s][:, fi, dj * 128:(dj + 1) * 128], act[:, fi, tsl], fi == 0, fi == nf - 1,
                                   [Rwd[s], RA[fi][tb]], [RP[bank]], inc=(fi == nf - 1))
                            k.op("dve", lambda: nc.vector.scalar_tensor_tensor(
                                out=x_sb[:, dc, tsl], in0=ps[bank][:], scalar=0.5, in1=x_sb[:, dc, tsl],
                                op0=ALU.mult, op1=ALU.add), reads=[RP[bank], RX[dc][tb]], writes=[RX[dc][tb]])

            load(seq[0])
            rmsnorm_h(h, RH, gcol, 7)
            for i, item in enumerate(seq):
                if i + 1 < len(seq): load(seq[i + 1])
                compute(item)
            k.barrier()
            st.close()

        def mixer(l):
            w_in = P["w_in"]; w_o = P["w_o"]
            lam_init = 0.8 - 0.6 * math.exp(-0.3 * (l + 1))
            com = Scope(ar)
            hs = Scope(ar)
            h = hs("m_h", [128, NC8, S], BF16)
            RH = [[Res("mh%d_%d" % (c, t)) for t in range(NTB)] for c in range(NC8)]
            rmsnorm_h(h, RH, pk[:, l, 8:16], 7)
            warm()

            Et = [com("Et%d" % i, [128, TB], BF16) for i in range(4)]; REt = [Res("Et%d" % i) for i in range(4)]
            NUF = 4
            Uf = [com("Uf%d" % i, [65, TB], F32) for i in range(NUF)]; RU = [Res("Uf%d" % i) for i in range(NUF)]
            deferred = []

            def advance_deferred():
                for item in list(deferred):
                    item.pop(0)()
                    if not item: deferred.remove(item)

            def flush_deferred():
                while deferred:
                    advance_deferred()
            rec = [com("rec%d" % i, [1, TB], F32) for i in range(2)]; Rrec = [Res("rec%d" % i) for i in range(2)]
            sqb = [com("sqb%d" % i, [128, TB], BF16) for i in range(2)]; Rsqb = [Res("sqb%d" % i) for i in range(2)]
            rsd = [com("rsd%d" % i, [128, TB], F32) for i in range(2)]; Rrsd = [Res("rsd%d" % i) for i in range(2)]
            wsl = [com("wsl%d" % i, [128, NC8, 384], BF16) for i in range(2)]; Rwsl = [Res("wsl%d" % i) for i in range(2)]
            vaug = com("vaug", [128, 16, 6, 65], BF16); RV = [Res("vaug%d" % i) for i in range(16)]
            woa = com("woa", [65, 6, D], BF16); Rwo = Res("woa")
            lamt = com("lamt", [65, 8], F32); Rlam = Res("lamt")
            ctr = {"w": 0, "pj": 0, "st": 0, "s": 0, "e": 0, "v": 0, "o": 0, "u": 0, "b": 0, "wo": 0, "r": 0}

            k.op("dve", lambda: nc.vector.tensor_tensor(out=lamt[0:32, 0:2], in0=pk[0:32, l, 34:38:2], in1=pk[0:32, l, 35:39:2], op=ALU.mult),
                 reads=[Rpk], writes=[Rlam])
            mm(ps[7][0:65, 0:2], cf[0:32, 0:65], lamt[0:32, 0:2], True, True, [Rcf, Rlam], [RP[7]])
            k.op("act", lambda: nc.scalar.activation(out=lamt[0:65, 2:4], in_=ps[7][0:65, 0:2], func=AF.Exp), reads=[RP[7]], writes=[Rlam])
            k.op("dve", lambda: nc.vector.tensor_tensor(out=lamt[0:65, 4:5], in0=lamt[0:65, 3:4], in1=lamt[0:65, 2:3], op=ALU.subtract),
                 reads=[Rlam], writes=[Rlam])
            k.op("dve", lambda: nc.vector.tensor_scalar(out=lamt[0:65, 5:6], in0=lamt[0:65, 4:5], scalar1=float(-lam_init), scalar2=None, op0=ALU.add),
                 reads=[Rlam], writes=[Rlam])
            neglam = lamt[0:65, 5:6]

            def load_win(c0, ncol):
                s = ctr["w"] % 2; ctr["w"] += 1
                k.dma("pool", wsl[s][:, :, 0:ncol], w_in[l, :, c0:c0 + ncol].rearrange("(c p) f -> p c f", p=128), writes=[Rwsl[s]])
                return s

            def stat_rstd(src_ap, M, gmat, inv_n, src_res):
                i = ctr["st"] % 2; ctr["st"] += 1
                if src_res[0] in RP:
                    k.op("act", lambda: nc.scalar.activation(out=sqb[i][0:M, :], in_=src_ap, func=AF.Square), reads=src_res, writes=[Rsqb[i]])
                else:
                    k.op("dve", lambda: nc.vector.tensor_tensor(out=sqb[i][0:M, :], in0=src_ap, in1=src_ap, op=ALU.mult),
                         reads=src_res, writes=[Rsqb[i]])
                mm(ps[7][0:M, :], gmat, sqb[i][0:M, :], True, True, [Rcb, Rsqb[i]], [RP[7]])
                k.op("act", lambda: nc.scalar.activation(out=rsd[i][0:M, :], in_=ps[7][0:M, :], func=AF.Ln, scale=float(inv_n), bias=epsb[0:M, 0:1]),
                     reads=[RP[7], Reps], writes=[Rrsd[i]])
                k.op("act", lambda: nc.scalar.activation(out=rsd[i][0:M, :], in_=rsd[i][0:M, :], func=AF.Exp, scale=-0.5),
                     reads=[Rrsd[i]], writes=[Rrsd[i]])
                return i

            def proj_qk(slot, col0, M, tb, gmat, inv_n, gcol, dst_ap, dst_res, src_h=None):
                tsl = slice(tb * TB, (tb + 1) * TB)
                bank = 5 + ctr["pj"] % 2; ctr["pj"] += 1
                for c in range(NC8):
                    mm(ps[bank][0:M, :], wsl[slot][:, c, col0:col0 + M], h[:, c, tsl], c == 0, c == NC8 - 1,
                       [Rwsl[slot], RH[c][tb]], [RP[bank]], inc=(c == NC8 - 1))
                i = stat_rstd(ps[bank][0:M, :], M, gmat, inv_n, [RP[bank]])
                k.op("dve", lambda: nc.vector.scalar_tensor_tensor(out=dst_ap, in0=ps[bank][0:M, :], scalar=gcol, in1=rsd[i][0:M, :],
                                                                   op0=ALU.mult, op1=ALU.mult),
                     reads=[RP[bank], Rrsd[i], Rpk], writes=dst_res)

            def proj_v(lhs_fn, lhs_res_fn, rhs_ap, rhs_res, nh):
                for tcn in range(16):
                    bank = ctr["v"] % 3; ctr["v"] += 1
                    lst = lhs_fn(tcn)
                    for ci, (lap, lres) in enumerate(lst):
                        mm(ps[bank][:, 0:nh * 64], lap, rhs_ap(ci), ci == 0, ci == len(lst) - 1, [lres] + rhs_res, [RP[bank]],
                           inc=(ci == len(lst) - 1))
                    k.op("dve", lambda: nc.vector.tensor_copy(out=vaug[:, tcn, 0:nh, 1:65],
                                                              in_=ps[bank][:, 0:nh * 64].rearrange("p (h d) -> p h d", h=nh)),
                         reads=[RP[bank]], writes=[RV[tcn]])

            def init_vaug():
                k.op("dve", lambda: nc.vector.memset(vaug[:, :, :, 0:1], 1.0), writes=RV)

            def load_wo(h0, nh):
                k.op("dve", lambda: nc.vector.memset(woa[0:1, :, :], 0.0), writes=[Rwo])
                k.dma("pool", woa[1:65, 0:nh, :], w_o[l, h0 * 64:(h0 + nh) * 64, :].rearrange("(h d) o -> d h o", d=64), writes=[Rwo])

            def attn_pass(q_ap, q_res, k_fn, k_res, hl, bias_fn, bias_res, kcs, obank):
                n = len(kcs)
                sb_of = {}; e_of = {}

                def qk(i):
                    kc = kcs[i]
                    sbk = ctr["s"] % 3; ctr["s"] += 1
                    e = ctr["e"] % 4; ctr["e"] += 1
                    sb_of[i] = sbk; e_of[i] = e
                    mm(ps[sbk][:, :], k_fn(kc), q_ap, True, True, k_res + q_res, [RP[sbk]])
                    k.op("act", lambda: nc.scalar.activation(out=Et[e][:], in_=ps[sbk][:, :], func=AF.Exp), reads=[RP[sbk]], writes=[REt[e]])
                    if bias_fn is not None:
                        k.op("dve", lambda: nc.vector.tensor_tensor(out=Et[e][:], in0=Et[e][:], in1=bias_fn(kc), op=ALU.mult),
                             reads=[REt[e]] + bias_res, writes=[REt[e]])

                def av(i):
                    kc = kcs[i]; e = e_of[i]
                    mm(ps[obank][0:65, :], vaug[:, kc, hl, 0:65], Et[e][:], i == 0, i == n - 1, [RV[kc], REt[e]], [RP[obank]], inc=(i == n - 1))

                for i in range(min(2, n)): qk(i)
                for i in range(n):
                    if i + 2 < n: qk(i + 2)
                    av(i)
                    if i > 0 and i % 4 == 0: advance_deferred()

            def epi_head(obank):
                u = ctr["u"] % NUF; ctr["u"] += 1
                r = ctr["r"] % 2; ctr["r"] += 1
                k.op("dve", lambda: nc.vector.tensor_copy(out=Uf[u][:], in_=ps[obank][0:65, :]), reads=[RP[obank]], writes=[RU[u]])
                k.op("act", lambda: nc.scalar.activation(out=rec[r][:], in_=Uf[u][0:1, :], func=AF.Ln), reads=[RU[u]], writes=[Rrec[r]])
                k.op("act", lambda: nc.scalar.activation(out=rec[r][:], in_=rec[r][:], func=AF.Exp, scale=-1.0), reads=[Rrec[r]], writes=[Rrec[r]])
                return u, r

            def finish_plain(obank, dst_ap, dst_res, after=None):
                u, r = epi_head(obank)

                def st1(u=u, r=r, dst_ap=dst_ap, dst_res=dst_res):
                    mm(ps[7][0:65, :], cf[0:1, 0:65], rec[r][:], True, True, [Rcf, Rrec[r]], [RP[7]])
                    k.op("dve", lambda: nc.vector.tensor_tensor(out=dst_ap, in0=Uf[u][:], in1=ps[7][0:65, :], op=ALU.mult),
                         reads=[RU[u], RP[7]], writes=dst_res)
                deferred.append([st1] + ([after] if after is not None else []))
                return u

            def wo_apply(mix_ap_fn, mix_res, nh, qb):
                qsl = slice(qb * TB, (qb + 1) * TB)
                hl = list(range(nh)) if isinstance(nh, int) else nh
                for dc in range(NC8):
                    bank = 5 + ctr["wo"] % 2; ctr["wo"] += 1
                    for hi_, hh in enumerate(hl):
                        mm(ps[bank][:, :], woa[0:65, hh, dc * 128:(dc + 1) * 128], mix_ap_fn(hh), hi_ == 0, hi_ == len(hl) - 1,
                           [Rwo] + mix_res, [RP[bank]], inc=(hi_ == len(hl) - 1))
                    k.op("dve", lambda: nc.vector.tensor_tensor(out=x_sb[:, dc, qsl], in0=ps[bank][:, :], in1=x_sb[:, dc, qsl], op=ALU.add),
                         reads=[RP[bank], RX[dc][qb]], writes=[RX[dc][qb]])

            def mixer_diff():
                sc = Scope(ar)
                dq = [sc("dq%d" % i, [64, S], BF16) for i in range(4)]; Rdq = [[Res("dq%d_%d" % (i, t)) for t in range(NTB)] for i in range(4)]
                dk = [sc("dk%d" % i, [64, S], BF16) for i in range(4)]; Rdk = [[Res("dk%d_%d" % (i, t)) for t in range(NTB)] for i in range(4)]
                bwin = [sc("bwin%d" % i, [128, WIN], BF16) for i in range(2)]; Rbw = [Res("bwin%d" % i) for i in range(2)]
                mixT = [sc("mixT%d" % i, [65, 4, TB], BF16) for i in range(2)]; Rmx = [Res("mixT%d" % i) for i in range(2)]
                init_vaug()
                load_wo(0, 4)
                g32 = cb[0:64, C_G32:C_G32 + 64]
                s0 = load_win(0, 256)
                s1 = load_win(256, 256)
                for hd in range(4):
                    for tb in range(NTB):
                        tsl = slice(tb * TB, (tb + 1) * TB)
                        proj_qk(s0, hd * 64, 64, tb, g32, 1.0 / 32, pk[0:64, l, 24:25], dq[hd][:, tsl], [Rdq[hd][tb]])
                s2 = load_win(512, 256)
                for hd in range(4):
                    for tb in range(NTB):
                        tsl = slice(tb * TB, (tb + 1) * TB)
                        proj_qk(s1, hd * 64, 64, tb, g32, 1.0 / 32, pk[0:64, l, 25:26], dk[hd][:, tsl], [Rdk[hd][tb]])
                proj_v(lambda tcn: [(h[:, c, tcn * 128:(tcn + 1) * 128], RH[c][tcn // 4]) for c in range(NC8)], None,
                       lambda ci: wsl[s2][:, ci, 0:256], [Rwsl[s2]], 4)
                kcs = list(range(16))
                m65 = cb[0:65, C_M65:C_M65 + 65]
                warm()

                def load_bias(hd, qb):
                    s = ctr["b"] % 2; ctr["b"] += 1
                    k.dma("pool", bwin[s][:, :], P["gbias"][hd, :, qb * TB:qb * TB + WIN], writes=[Rbw[s]])
                    return s
                order = [(qb, hd) for qb in range(NTB) for hd in range(4)]
                bs = {order[0]: load_bias(order[0][1], order[0][0])}
                for oi, (qb, hd) in enumerate(order):
                    if oi + 1 < len(order):
                        nq, nh_ = order[oi + 1]
                        bs[order[oi + 1]] = load_bias(nh_, nq)
                    s = bs[(qb, hd)]
                    k.op("act", lambda: nc.scalar.activation(out=bwin[s][:, :], in_=bwin[s][:, :], func=AF.Exp), reads=[Rbw[s]], writes=[Rbw[s]])
                    qsl = slice(qb * TB, (qb + 1) * TB)
                    m = qb % 2
                    kres = [Rdk[hd][t] for t in range(NTB)]
                    us = []
                    for c2 in range(2):
                        rows = slice(c2 * 32, (c2 + 1) * 32)
                        ob = 3 + c2
                        attn_pass(dq[hd][rows, qsl], [Rdq[hd][qb]], lambda kc: dk[hd][rows, kc * 128:(kc + 1) * 128], kres, hd,
                                  lambda kc: bwin[s][:, 1920 - kc * 128:1920 - kc * 128 + TB], [Rbw[s]], kcs, ob)
                        u, r = epi_head(ob)
                        us.append((u, r))
                    (u0, r0), (u1, r1) = us

                    stt = {}

                    def st1(u0=u0, r0=r0, u1=u1, r1=r1, stt=stt):
                        mm(ps[7][0:65, :], cf[0:1, 0:65], rec[r0][:], True, True, [Rcf, Rrec[r0]], [RP[7]])
                        mm(ps[6][0:65, :], cf[0:1, 0:65], rec[r1][:], True, True, [Rcf, Rrec[r1]], [RP[6]])
                        k.op("dve", lambda: nc.vector.tensor_tensor(out=Uf[u0][:], in0=Uf[u0][:], in1=ps[7][0:65, :], op=ALU.mult),
                             reads=[RU[u0], RP[7]], writes=[RU[u0]])
                        k.op("dve", lambda: nc.vector.scalar_tensor_tensor(out=Uf[u1][:], in0=Uf[u1][:], scalar=neglam, in1=ps[6][0:65, :],
                                                                           op0=ALU.mult, op1=ALU.mult),
                             reads=[RU[u1], RP[6], Rlam], writes=[RU[u1]])
                        k.op("dve", lambda: nc.vector.tensor_tensor(out=Uf[u1][:], in0=Uf[u1][:], in1=Uf[u0][:], op=ALU.add),
                             reads=[RU[u0], RU[u1]], writes=[RU[u1]])
                        i = ctr["st"] % 2; ctr["st"] += 1; stt["i"] = i
                        k.op("dve", lambda: nc.vector.tensor_tensor(out=sqb[i][0:65, :], in0=Uf[u1][:], in1=Uf[u1][:], op=ALU.mult),
                             reads=[RU[u1]], writes=[Rsqb[i]])

                    def st2(u1=u1, m=m, hd=hd, stt=stt):
                        i = stt["i"]
                        mm(ps[7][0:65, :], m65, sqb[i][0:65, :], True, True, [Rcb, Rsqb[i]], [RP[7]])
                        k.op("act", lambda: nc.scalar.activation(out=rsd[i][0:65, :], in_=ps[7][0:65, :], func=AF.Ln, scale=1.0 / 64, bias=epsb[0:65, 0:1]),
                             reads=[RP[7], Reps], writes=[Rrsd[i]])
                        k.op("act", lambda: nc.scalar.activation(out=rsd[i][0:65, :], in_=rsd[i][0:65, :], func=AF.Exp, scale=-0.5),
                             reads=[Rrsd[i]], writes=[Rrsd[i]])
                        k.op("dve", lambda: nc.vector.scalar_tensor_tensor(out=mixT[m][:, hd, :], in0=Uf[u1][:], scalar=pk[0:65, l, 33:34], in1=rsd[i][0:65, :],
                                                                           op0=ALU.mult, op1=ALU.mult),
                             reads=[RU[u1], Rrsd[i], Rpk], writes=[Rmx[m]])

                    def st3(m=m, qb=qb):
                        wo_apply(lambda hh: mixT[m][:, hh, :], [Rmx[m]], 4, qb)
                    deferred.append([st1, st2] + ([st3] if hd == 3 else []))
                flush_deferred()
                k.barrier()
                sc.close()

            def mixer_dil_proj():
                sc = Scope(ar)
                lq = [sc("lq%d" % i, [128, S], BF16) for i in range(3)]; Rlq = [[Res("lq%d_%d" % (i, t)) for t in range(NTB)] for i in range(3)]
                lk = [sc("lk%d" % i, [128, S], BF16) for i in range(3)]; Rlk = [[Res("lk%d_%d" % (i, t)) for t in range(NTB)] for i in range(3)]
                g64 = cb[:, C_G64:C_G64 + 128]
                init_vaug()
                warm()
                s0 = load_win(768, 384)
                s1 = load_win(1152, 384)
                for pr in range(3):
                    for tb in range(NTB):
                        tsl = slice(tb * TB, (tb + 1) * TB)
                        proj_qk(s0, pr * 128, 128, tb, g64, 1.0 / 64, pk[:, l, 26:27], lq[pr][:, tsl], [Rlq[pr][tb]])
                s2 = load_win(1536, 384)
                for pr in range(3):
                    for tb in range(NTB):
                        tsl = slice(tb * TB, (tb + 1) * TB)
                        proj_qk(s1, pr * 128, 128, tb, g64, 1.0 / 64, pk[:, l, 27:28], lk[pr][:, tsl], [Rlk[pr][tb]])
                proj_v(lambda tcn: [(h[:, c, tcn * 128:(tcn + 1) * 128], RH[c][tcn // 4]) for c in range(NC8)], None,
                       lambda ci: wsl[s2][:, ci, 0:384], [Rwsl[s2]], 6)
                return sc, lq, Rlq, lk, Rlk

            def mixer_dil_attn(sc, lq, Rlq, lk, Rlk):
                bwin = [sc("bwin%d" % i, [128, WIN], BF16) for i in range(2)]; Rbw = [Res("bwin%d" % i) for i in range(2)]
                lcw = sc("lcw", [128, WIN], BF16); Rlc = Res("lcw")
                mixT = [sc("mixT%d" % i, [65, 6, TB], BF16) for i in range(2)]; Rmx = [Res("mixT%d" % i) for i in range(2)]
                load_wo(4, 6)
                warm()

                def load_bias(hd, qb):
                    s = ctr["b"] % 2; ctr["b"] += 1
                    k.dma("pool", bwin[s][:, :], P["gbias"][4 + hd, :, qb * TB:qb * TB + WIN], writes=[Rbw[s]])
                    return s
                order = [(qb, hd) for qb in range(NTB) for hd in range(6)]
                bs = {order[0]: load_bias(order[0][1], order[0][0])}
                for oi, (qb, hd) in enumerate(order):
                    if hd == 0:
                        k.dma("pool", lcw[:, :], P["logc"][:, qb * TB:qb * TB + WIN], writes=[Rlc])
                    if oi + 1 < len(order):
                        nq, nh_ = order[oi + 1]
                        bs[order[oi + 1]] = load_bias(nh_, nq)
                    s = bs[(qb, hd)]
                    k.op("dve", lambda: nc.vector.tensor_tensor(out=bwin[s][:, :], in0=bwin[s][:, :], in1=lcw[:, :], op=ALU.add),
                         reads=[Rbw[s], Rlc], writes=[Rbw[s]])
                    k.op("act", lambda: nc.scalar.activation(out=bwin[s][:, :], in_=bwin[s][:, :], func=AF.Exp), reads=[Rbw[s]], writes=[Rbw[s]])
                    qsl = slice(qb * TB, (qb + 1) * TB)
                    m = qb % 2
                    pr = hd // 2; rows = slice((hd % 2) * 64, (hd % 2) * 64 + 64)
                    kcs = [kc for kc in range(16) if -1151 <= kc * 128 - qb * TB <= 1535]
                    ob = 3 + ctr["o"] % 2; ctr["o"] += 1
                    attn_pass(lq[pr][rows, qsl], [Rlq[pr][qb]], lambda kc: lk[pr][rows, kc * 128:(kc + 1) * 128], [Rlk[pr][t] for t in range(NTB)], hd,
                              lambda kc: bwin[s][:, 1920 - kc * 128:1920 - kc * 128 + TB], [Rbw[s]], kcs, ob)
                    aft = (lambda m=m, qb=qb: wo_apply(lambda hh: mixT[m][:, hh, :], [Rmx[m]], 6, qb)) if hd == 5 else None
                    finish_plain(ob, mixT[m][:, hd, :], [Rmx[m]], after=aft)
                flush_deferred()
                k.barrier()
                sc.close()

            def mixer_mla_latents():
                sc = Scope(ar)
                qd = sc("qd", [128, 2, S], BF16); Rqd = [Res("qd%d" % t) for t in range(NTB)]
                ckv = sc("ckv", [128, S], BF16); Rckv = [Res("ckv%d" % t) for t in range(NTB)]
                krp = sc("krp", [32, S], BF16); Rkrp = [Res("krp%d" % t) for t in range(NTB)]
                s0 = load_win(1920, 256)
                s1 = load_win(2176, 160)
                for tb in range(NTB):
                    tsl = slice(tb * TB, (tb + 1) * TB)
                    for j in range(2):
                        for c in range(NC8):
                            mm(ps[5 + j][:, :], wsl[s0][:, c, j * 128:(j + 1) * 128], h[:, c, tsl], c == 0, c == NC8 - 1,
                               [Rwsl[s0], RH[c][tb]], [RP[5 + j]], inc=(c == NC8 - 1))
                    for j in range(2):
                        k.op("act", lambda: nc.scalar.activation(out=sqb[j][:, :], in_=ps[5 + j][:, :], func=AF.Square),
                             reads=[RP[5 + j]], writes=[Rsqb[j]])
                        mm(ps[7][:, :], ones_bf, sqb[j][:, :], j == 0, j == 1, [Rcb, Rsqb[j]], [RP[7]])
                    k.op("act", lambda: nc.scalar.activation(out=rsd[0][:, :], in_=ps[7][:, :], func=AF.Ln, scale=1.0 / 256, bias=epsb[:, 0:1]),
                         reads=[RP[7], Reps], writes=[Rrsd[0]])
                    k.op("act", lambda: nc.scalar.activation(out=rsd[0][:, :], in_=rsd[0][:, :], func=AF.Exp, scale=-0.5), reads=[Rrsd[0]], writes=[Rrsd[0]])
                    for j in range(2):
                        k.op("dve", lambda: nc.vector.scalar_tensor_tensor(out=qd[:, j, tsl], in0=ps[5 + j][:, :], scalar=pk[:, l, 28 + j:29 + j], in1=rsd[0][:, :],
                                                                           op0=ALU.mult, op1=ALU.mult),
                             reads=[RP[5 + j], Rrsd[0], Rpk], writes=[Rqd[tb]])
                    for c in range(NC8):
                        mm(ps[5][:, :], wsl[s1][:, c, 0:128], h[:, c, tsl], c == 0, c == NC8 - 1, [Rwsl[s1], RH[c][tb]], [RP[5]], inc=(c == NC8 - 1))
                    for c in range(NC8):
                        mm(ps[6][0:32, :], wsl[s1][:, c, 128:160], h[:, c, tsl], c == 0, c == NC8 - 1, [Rwsl[s1], RH[c][tb]], [RP[6]], inc=(c == NC8 - 1))
                    i = stat_rstd(ps[5][:, :], 128, ones_bf, 1.0 / 128, [RP[5]])
                    k.op("dve", lambda: nc.vector.scalar_tensor_tensor(out=ckv[:, tsl], in0=ps[5][:, :], scalar=pk[:, l, 30:31], in1=rsd[i][:, :],
                                                                       op0=ALU.mult, op1=ALU.mult),
                         reads=[RP[5], Rrsd[i], Rpk], writes=[Rckv[tb]])
                    k.op("dve", lambda: nc.vector.tensor_copy(out=krp[:, tsl], in_=ps[6][0:32, :]), reads=[RP[6]], writes=[Rkrp[tb]])
                return sc, qd, Rqd, ckv, Rckv, krp, Rkrp

            def mixer_mla_attn(sc, qd, Rqd, ckv, Rckv, krp, Rkrp):
                qup = sc("qup", [128, 2, 576], BF16); Rqup = Res("qup")
                knw = sc("knw", [128, 6, 96], BF16); Rknw = Res("knw")
                kvw = sc("kvw", [128, 6, 64], BF16); Rkvw = Res("kvw")
                rC = sc("ropeC", [96, S], F32); rS = sc("ropeS", [96, S], F32); Rrope = Res("rope")
                mqh = [sc("mqh%d" % i, [96, S], BF16) for i in range(2)]; Rmq = [[Res("mq%d_%d" % (i, t)) for t in range(NTB)] for i in range(2)]
                mkh = [sc("mkh%d" % i, [96, S], BF16) for i in range(2)]; Rmk = [[Res("mk%d_%d" % (i, t)) for t in range(NTB)] for i in range(2)]
                qn = [sc("qn%d" % i, [96, TB], BF16) for i in range(2)]; Rqn = [Res("qn%d" % i) for i in range(2)]
                t1 = sc("t1", [96, TB], F32); Rt1 = Res("t1")
                t2 = sc("t2", [96, TB], F32); Rt2 = Res("t2")
                mixT = [sc("mixTc%d" % i, [65, TB], BF16) for i in range(2)]; Rmx = [Res("mxc%d" % i) for i in range(2)]
                k.dma("pool", qup[:, :, :], P["mla_q_up"][l].rearrange("(c p) n -> p c n", p=128), writes=[Rqup])
                k.op("dve", lambda: nc.vector.memset(knw[:, :, :], 0.0), writes=[Rknw])
                kvv = P["mla_kv_up"][l].rearrange("p (h t d) -> p h t d", h=6, t=2)
                k.dma("pool", knw[:, :, 0:64], kvv[:, :, 0, :], writes=[Rknw])
                k.dma("pool", kvw[:, :, :], kvv[:, :, 1, :], writes=[Rkvw])
                k.dma("sp", rC[:, :], P["ropeC"][:, :], writes=[Rrope])
                k.dma("sp", rS[:, :], P["ropeS"][:, :], writes=[Rrope])
                init_vaug()
                load_wo(10, 6)
                warm()
                proj_v(lambda tcn: [(ckv[:, tcn * 128:(tcn + 1) * 128], Rckv[tcn // 4])], None,
                       lambda ci: kvw[:, :, :].rearrange("p h d -> p (h d)"), [Rkvw], 6)
                o96 = cb[0:96, C_ONES:C_ONES + 96]
                rt = cb[0:96, C_RT:C_RT + 96]
                sel = cb[0:32, C_SEL:C_SEL + 96]
                nn = {"q": 0}

                def norm_rope(bank, gcol, dst_ap, dst_res, tsl):
                    i = stat_rstd(ps[bank][0:96, :], 96, o96, 1.0 / 96, [RP[bank]])
                    j = nn["q"] % 2; nn["q"] += 1
                    k.op("dve", lambda: nc.vector.scalar_tensor_tensor(out=qn[j][:, :], in0=ps[bank][0:96, :], scalar=gcol, in1=rsd[i][0:96, :],
                                                                       op0=ALU.mult, op1=ALU.mult),
                         reads=[RP[bank], Rrsd[i], Rpk], writes=[Rqn[j]])
                    rb = 3 + nn["q"] % 2
                    mm(ps[rb][0:96, :], rt, qn[j][:, :], True, True, [Rcb, Rqn[j]], [RP[rb]])
                    k.op("dve", lambda: nc.vector.tensor_tensor(out=t1[:, :], in0=qn[j][:, :], in1=rC[:, tsl], op=ALU.mult),
                         reads=[Rqn[j], Rrope], writes=[Rt1])
                    k.op("dve", lambda: nc.vector.tensor_tensor(out=t2[:, :], in0=ps[rb][0:96, :], in1=rS[:, tsl], op=ALU.mult),
                         reads=[RP[rb], Rrope], writes=[Rt2])
                    k.op("dve", lambda: nc.vector.tensor_tensor(out=dst_ap, in0=t1[:, :], in1=t2[:, :], op=ALU.add),
                         reads=[Rt1, Rt2], writes=dst_res)

                kcs = list(range(16))
                warm()
                for hd in range(6):
                    sl = hd % 2
                    for tb in range(NTB):
                        tsl = slice(tb * TB, (tb + 1) * TB)
                        bank = 5 + ctr["pj"] % 2; ctr["pj"] += 1
                        for j in range(2):
                            mm(ps[bank][0:96, :], qup[:, j, hd * 96:(hd + 1) * 96], qd[:, j, tsl], j == 0, j == 1, [Rqup, Rqd[tb]], [RP[bank]], inc=(j == 1))
                        norm_rope(bank, pk[0:96, l, 31:32], mqh[sl][:, tsl], [Rmq[sl][tb]], tsl)
                        bank = 5 + ctr["pj"] % 2; ctr["pj"] += 1
                        mm(ps[bank][0:96, :], knw[:, hd, :], ckv[:, tsl], True, False, [Rknw, Rckv[tb]], [RP[bank]], inc=False)
                        mm(ps[bank][0:96, :], sel, krp[:, tsl], False, True, [Rcb, Rkrp[tb]], [RP[bank]])
                        norm_rope(bank, pk[0:96, l, 32:33], mkh[sl][:, tsl], [Rmk[sl][tb]], tsl)
                    for qb in range(NTB):
                        qsl = slice(qb * TB, (qb + 1) * TB)
                        ob = 3 + ctr["o"] % 2; ctr["o"] += 1
                        attn_pass(mqh[sl][:, qsl], [Rmq[sl][qb]], lambda kc: mkh[sl][:, kc * 128:(kc + 1) * 128], [Rmk[sl][t] for t in range(NTB)], hd,
                                  None, [], kcs, ob)
                        m = ctr["o"] % 2
                        finish_plain(ob, mixT[m][:, :], [Rmx[m]],
                                     after=(lambda m=m, hd=hd, qb=qb: wo_apply(lambda hh: mixT[m][:, :], [Rmx[m]], [hd], qb)))
                flush_deferred()
                k.barrier()
                sc.close()

            mixer_diff()
            dsc = mixer_dil_proj()
            msc = mixer_mla_latents()
            k.barrier()
            hs.close()
            mixer_dil_attn(*dsc)
            mixer_mla_attn(*msc)
            k.barrier()
            com.close()

        for stg in stages:
            kind, l = stg
            if kind == "ffn1": ffn(l, 1)
            elif kind == "ffn2": ffn(l, 2)
            elif kind == "mix": mixer(l)

        for c in range(NC8):
            k.dma("sp", yT[c * 128:(c + 1) * 128, :], x_sb[:, c, :], reads=RX[c])
        k.barrier()
        build.stats = (k.ninst, k.nwait, ar.peak)
    return nc


ALL_STAGES = [("ffn1", 0), ("mix", 0), ("ffn2", 0), ("ffn1", 1), ("mix", 1), ("ffn2", 1)]


def _gather_bias(rel_bias):
    idx = _t5_bucket(_rel_grid())
    return np.ascontiguousarray(np.transpose(rel_bias[idx], (2, 0, 1))).astype(np.float32)


def make_in_maps(inp, n_cores=8):
    cst, ropeC, ropeS, logc = _host_consts()
    pk = _pack_small(inp)
    gb = _gather_bias(np.asarray(inp["rel_bias"], np.float32))
    shared = {nm: np.ascontiguousarray(np.asarray(inp[nm], np.float32)) for nm, _ in WEIGHT_SPECS}
    shared.update({"pk": pk, "cst": cst, "ropeC": ropeC, "ropeS": ropeS, "logc": logc, "gbias": gb})
    maps = []
    for b in range(n_cores):
        m = dict(shared)
        m["xT"] = np.ascontiguousarray(np.asarray(inp["x"][b], np.float32).T)
        maps.append(m)
    return maps


def kernel(**inp):
    nc = build(ALL_STAGES)
    maps = make_in_maps(inp, 8)
    res = run_bass_kernel_spmd(nc, maps, core_ids=list(range(8)))
    out = np.stack([np.ascontiguousarray(r["yT"].T) for r in res.results], axis=0)
    return out.astype(np.float32)
```

```python
import math
import contextlib
import numpy as np
import concourse.bass as bass
import concourse.mybir as mybir
from concourse.bass_utils import run_bass_kernel_spmd

F32 = mybir.dt.float32
BF16 = mybir.dt.bfloat16
AF = mybir.ActivationFunctionType
ALU = mybir.AluOpType

D = 1024; S = 2048; DFF = 2816; L = 2; NC8 = 8; TB = 512; NTB = 4
EPS = 1e-6
INW = 2336
NEG = -30000.0
GW = 3968
WIN = 2432
C_ID = 0; C_ONES = 128; C_G32 = 256; C_G64 = 384; C_M65 = 512; C_RT = 640; C_SEL = 768; NCST = 896
NPK = 40


class Res:
    __slots__ = ("name", "w", "r")

    def __init__(self, name):
        self.name = name; self.w = None; self.r = {}


class K:
    def __init__(self, nc, es, n_dma_sems=8):
        self.nc = nc
        self.eng = {"pe": nc.tensor, "dve": nc.vector, "act": nc.scalar, "pool": nc.gpsimd, "sp": nc.sync}
        self.sem = {}; self.cnt = {}
        for e in self.eng:
            self.sem[e] = es.enter_context(nc.semaphore("s_" + e)); self.cnt[e] = 0
        self.seen = {e: {} for e in self.eng}
        self.pend = {e: ([], []) for e in self.eng}
        self.dq = {}
        for q in ("sp", "pool"):
            lst = []
            for i in range(n_dma_sems):
                nm = "d_%s%d" % (q, i)
                self.sem[nm] = es.enter_context(nc.semaphore(nm)); self.cnt[nm] = 0
                lst.append(nm)
            self.dq[q] = [lst, 0]
        self.nwait = 0; self.ninst = 0
        self.es = es; self.slot_sem = {}

    def _sem_for(self, q, writes):
        if not writes:
            lst, i = self.dq[q]
            self.dq[q][1] = i + 1
            return lst[i % len(lst)]
        key = writes[0].name
        if key not in self.slot_sem:
            nm = "ds_%d" % len(self.slot_sem)
            self.sem[nm] = self.es.enter_context(self.nc.semaphore(nm)); self.cnt[nm] = 0
            self.slot_sem[key] = nm
        return self.slot_sem[key]

    def _wait(self, e, f, c):
        if c <= 0 or self.seen[e].get(f, 0) >= c:
            return
        self.eng[e].wait_ge(self.sem[f], c)
        self.seen[e][f] = c; self.nwait += 1

    def op(self, e, fn, reads=(), writes=(), inc=True):
        need = {}
        for r in reads:
            if r.w is not None:
                f, c = r.w
                if f == e and e == "pe":
                    continue
                if c > need.get(f, 0): need[f] = c
        for w in writes:
            if w.w is not None:
                f, c = w.w
                if f != e and c > need.get(f, 0): need[f] = c
            for f, c in w.r.items():
                if f != e and c > need.get(f, 0): need[f] = c
        for f, c in need.items():
            self._wait(e, f, c)
        ins = fn()
        self.ninst += 1
        pr, pw = self.pend[e]
        pr.extend(reads); pw.extend(writes)
        if inc:
            ins.then_inc(self.sem[e], 1)
            self.cnt[e] += 1
            c = self.cnt[e]
            for r in pr: r.r[e] = c
            for w in pw:
                w.w = (e, c); w.r = {}
            self.pend[e] = ([], [])
        return ins

    def dma(self, q, out, in_, reads=(), writes=()):
        s = self._sem_for(q, list(writes))
        need = {}
        for r in reads:
            if r.w is not None:
                f, c = r.w
                if c > need.get(f, 0): need[f] = c
        for w in writes:
            if w.w is not None:
                f, c = w.w
                if c > need.get(f, 0): need[f] = c
            for f, c in w.r.items():
                if c > need.get(f, 0): need[f] = c
        for f, c in need.items():
            self._wait(q, f, c)
        self._wait(q, s, self.cnt[s])
        self.eng[q].dma_start(out=out, in_=in_).then_inc(self.sem[s], 16)
        self.cnt[s] += 16
        c = self.cnt[s]
        for r in reads: r.r[s] = c
        for w in writes:
            w.w = (s, c); w.r = {}
        self.ninst += 1

    def barrier(self):
        for e in self.eng:
            for f in self.cnt:
                if f != e: self._wait(e, f, self.cnt[f])


class Arena:
    def __init__(self, nc, lo, hi):
        self.nc = nc; self.free = [(lo, hi)]; self.used = {}; self.n = 0; self.peak = 0; self.hi = hi

    def alloc(self, name, shape, dt):
        esz = 4 if dt == F32 else 2
        size = esz
        for d in shape[1:]: size *= d
        size = (size + 63) // 64 * 64
        for i, (a, b) in enumerate(self.free):
            if b - a >= size:
                self.free[i] = (a + size, b)
                if self.free[i][0] == self.free[i][1]: self.free.pop(i)
                self.n += 1
                t = self.nc.alloc_sbuf_tensor_at("%s_%d" % (name, self.n), list(shape), dt, offset=a)
                self.used[id(t)] = (a, a + size, t)
                self.peak = max(self.peak, a + size)
                return t
        raise RuntimeError("SBUF arena full allocating %s %s (free=%s)" % (name, shape, self.free))

    def release(self, ts):
        for t in ts:
            a, b, _ = self.used.pop(id(t))
            self.free.append((a, b))
        self.free.sort()
        m = []
        for a, b in self.free:
            if m and m[-1][1] == a: m[-1] = (m[-1][0], b)
            else: m.append((a, b))
        self.free = m


class Scope:
    def __init__(self, ar): self.ar = ar; self.ts = []
    def __call__(self, name, shape, dt):
        t = self.ar.alloc(name, shape, dt); self.ts.append(t); return t
    def close(self):
        self.ar.release(self.ts); self.ts = []


def _t5_bucket(rel):
    rel = np.asarray(rel, dtype=np.int64)
    n = np.abs(rel)
    nf = np.maximum(n, 1).astype(np.float64)
    large = 8 + np.floor(np.log(nf / 8.0) / math.log(128.0) * 8.0 + 1e-9).astype(np.int64)
    large = np.where(n < 8, 0, large)
    large = np.minimum(large, 15)
    return np.where(rel > 0, 16, 0) + np.where(n < 8, n, large)


def _rel_grid():
    kp = np.arange(128)[:, None]; j = np.arange(GW)[None, :]
    return kp - j + 1920


def _host_consts():
    cst = np.zeros((128, NCST), np.float32)
    cst[:, C_ID:C_ID + 128] = np.eye(128, dtype=np.float32)
    cst[:, C_ONES:C_ONES + 128] = 1.0
    for g in range(4):
        cst[g * 32:(g + 1) * 32, C_G32 + g * 32:C_G32 + (g + 1) * 32] = 1.0
    for g in range(2):
        cst[g * 64:(g + 1) * 64, C_G64 + g * 64:C_G64 + (g + 1) * 64] = 1.0
    cst[1:65, C_M65:C_M65 + 65] = 1.0
    for m in range(64, 80):
        cst[m + 16, C_RT + m] = -1.0
    for m in range(80, 96):
        cst[m - 16, C_RT + m] = 1.0
    for j in range(32):
        cst[j, C_SEL + 64 + j] = 1.0
    inv = (10000.0 ** (-(np.arange(16, dtype=np.float32) / np.float32(16)))).astype(np.float32)
    ang = (np.arange(S, dtype=np.float32)[None, :] * inv[:, None]).astype(np.float32)
    ropeC = np.ones((96, S), np.float32); ropeS = np.zeros((96, S), np.float32)
    ropeC[64:80] = np.cos(ang.astype(np.float64)); ropeC[80:96] = ropeC[64:80]
    ropeS[64:80] = np.sin(ang.astype(np.float64)); ropeS[80:96] = ropeS[64:80]
    rel = _rel_grid()
    cnt = (np.abs(rel) <= 64).astype(np.int64) + ((rel % 4 == 0) & (np.abs(rel) <= 256)) + ((rel % 16 == 0) & (np.abs(rel) <= 1024))
    logc = np.where(cnt > 0, np.log(np.maximum(cnt, 1).astype(np.float64)), NEG).astype(np.float32)
    return cst, ropeC, ropeS, logc


def _pack_small(inp):
    pk = np.zeros((128, L, NPK), np.float32)
    for l in range(L):
        for i, nm in enumerate(("ffn1_norm", "mix_norm", "ffn2_norm")):
            pk[:, l, i * 8:(i + 1) * 8] = inp[nm][l].reshape(8, 128).T
        pk[0:64, l, 24] = np.tile(inp["diff_q_norm"][l], 2)
        pk[0:64, l, 25] = np.tile(inp["diff_k_norm"][l], 2)
        pk[0:128, l, 26] = np.tile(inp["dil_q_norm"][l], 2)
        pk[0:128, l, 27] = np.tile(inp["dil_k_norm"][l], 2)
        pk[:, l, 28:30] = inp["mla_q_norm"][l].reshape(2, 128).T
        pk[:, l, 30] = inp["mla_kv_norm"][l]
        pk[0:96, l, 31] = inp["mla_qn"][l]
        pk[0:96, l, 32] = inp["mla_kn"][l]
        pk[1:65, l, 33] = inp["diff_subln"][l]
        pk[0:32, l, 34:38] = inp["diff_lambda"][l].T
    return pk


WEIGHT_SPECS = [
    ("ffn1_wg", [L, D, DFF]), ("ffn1_wu", [L, D, DFF]), ("ffn1_wd", [L, DFF, D]),
    ("ffn2_wg", [L, D, DFF]), ("ffn2_wu", [L, D, DFF]), ("ffn2_wd", [L, DFF, D]),
    ("w_in", [L, D, INW]), ("w_o", [L, D, D]),
    ("mla_q_up", [L, 256, 576]), ("mla_kv_up", [L, 128, 768]),
]


def build(stages, dbg=None):
    nc = bass.Bass("TRN2", target_bir_lowering=False)
    P = {}
    P["xT"] = nc.dram_tensor("xT", [D, S], F32, kind="ExternalInput").ap()
    for nm, shp in WEIGHT_SPECS:
        P[nm] = nc.dram_tensor(nm, shp, F32, kind="ExternalInput").ap()
    P["pk"] = nc.dram_tensor("pk", [128, L, NPK], F32, kind="ExternalInput").ap()
    P["cst"] = nc.dram_tensor("cst", [128, NCST], F32, kind="ExternalInput").ap()
    P["ropeC"] = nc.dram_tensor("ropeC", [96, S], F32, kind="ExternalInput").ap()
    P["ropeS"] = nc.dram_tensor("ropeS", [96, S], F32, kind="ExternalInput").ap()
    P["logc"] = nc.dram_tensor("logc", [128, GW], F32, kind="ExternalInput").ap()
    P["gbias"] = nc.dram_tensor("gbias", [10, 128, GW], F32, kind="ExternalInput").ap()
    yT = nc.dram_tensor("yT", [D, S], F32, kind="ExternalOutput").ap()

    with contextlib.ExitStack() as es:
        k = K(nc, es)
        ar = Arena(nc, 16512, 229376)
        glob = Scope(ar)

        x_sb = glob("x_sb", [128, NC8, S], F32)
        RX = [[Res("x%d_%d" % (c, t)) for t in range(NTB)] for c in range(NC8)]
        cb = glob("cb", [128, NCST], BF16); Rcb = Res("cb")
        cf = glob("cf", [128, 128], F32); Rcf = Res("cf")
        pk = glob("pk_sb", [128, L, NPK], F32); Rpk = Res("pk")
        epsb = glob("epsb", [128, 1], F32); Reps = Res("eps")
        ps = [es.enter_context(nc.psum_tensor("ps%d" % i, [128, 512], F32)) for i in range(8)]
        RP = [Res("ps%d" % i) for i in range(8)]

        def mm(out, lhsT, rhs, start, stop, reads, writes, inc=True):
            return k.op("pe", lambda: nc.tensor.matmul(out, lhsT=lhsT, rhs=rhs, start=start, stop=stop), reads, writes, inc)

        def warm(n=32):
            for i in range(n):
                mm(ps[7][:, :], cb[:, C_ID:C_ID + 128], cb[:, 0:512], True, True, [Rcb], [RP[7]], inc=(i == n - 1))

        for c in range(NC8):
            k.dma("sp", x_sb[:, c, :], P["xT"][c * 128:(c + 1) * 128, :], writes=RX[c])
        k.dma("pool", cb[:], P["cst"][:, :], writes=[Rcb])
        k.dma("sp", cf[:], P["cst"][:, C_ONES:C_ONES + 128], writes=[Rcf])
        k.dma("sp", pk[:], P["pk"][:, :, :], writes=[Rpk])
        k.op("dve", lambda: nc.vector.memset(epsb[:], EPS), writes=[Reps])
        for l in range(L):
            lam_init = 0.8 - 0.6 * math.exp(-0.3 * (l + 1))
            for col, sc in ((24, 32 ** -0.5), (26, 64 ** -0.5), (31, 96 ** -0.5), (33, 1.0 - lam_init)):
                k.op("dve", lambda: nc.vector.tensor_scalar(
                    out=pk[:, l, col:col + 1], in0=pk[:, l, col:col + 1], scalar1=float(sc), scalar2=None,
                    op0=ALU.mult), reads=[Rpk], writes=[Rpk])

        ones_bf = cb[:, C_ONES:C_ONES + 128]
        ident_bf = cb[:, C_ID:C_ID + 128]

        def rmsnorm_h(h, RH, gcol, stat_bank):
            sc = Scope(ar)
            sq = [sc("nsq%d" % i, [128, TB], BF16) for i in range(4)]
            Rsq = [Res("nsq%d" % i) for i in range(4)]
            lnv = sc("nlnv", [128, TB], F32); Rln = Res("nlnv")
            rstd = [sc("nrstd%d" % i, [128, TB], F32) for i in range(2)]
            Rrs = [Res("nrstd%d" % i) for i in range(2)]
            n = 0
            for tb in range(NTB):
                tsl = slice(tb * TB, (tb + 1) * TB)
                for c in range(NC8):
                    i = n % 4; n += 1
                    k.op("act", lambda: nc.scalar.activation(out=sq[i][:], in_=x_sb[:, c, tsl], func=AF.Square),
                         reads=[RX[c][tb]], writes=[Rsq[i]])
                    mm(ps[stat_bank][:], ones_bf, sq[i][:], c == 0, c == NC8 - 1, [Rsq[i], Rcb], [RP[stat_bank]])
                r = tb % 2
                k.op("act", lambda: nc.scalar.activation(out=lnv[:], in_=ps[stat_bank][:], func=AF.Ln, scale=1.0 / D, bias=epsb[:, 0:1]),
                     reads=[RP[stat_bank], Reps], writes=[Rln])
                k.op("act", lambda: nc.scalar.activation(out=rstd[r][:], in_=lnv[:], func=AF.Exp, scale=-0.5),
                     reads=[Rln], writes=[Rrs[r]])
                for c in range(NC8):
                    k.op("dve", lambda: nc.vector.scalar_tensor_tensor(
                        out=h[:, c, tsl], in0=x_sb[:, c, tsl], scalar=gcol[:, c:c + 1], in1=rstd[r][:],
                        op0=ALU.mult, op1=ALU.mult), reads=[RX[c][tb], Rrs[r], Rpk], writes=[RH[c][tb]])
            k.barrier()
            sc.close()

        def ffn(l, which):
            wg = P["ffn%d_wg" % which]; wu = P["ffn%d_wu" % which]; wd = P["ffn%d_wd" % which]
            gcol = pk[:, l, (0 if which == 1 else 16):(8 if which == 1 else 24)]
            st = Scope(ar)
            h = st("f_h", [128, NC8, S], BF16)
            RH = [[Res("h%d_%d" % (c, t)) for t in range(NTB)] for c in range(NC8)]
            act = st("f_act", [128, 12, S], BF16)
            RA = [[Res("a%d_%d" % (c, t)) for t in range(NTB)] for c in range(12)]
            wgu = [st("f_wgu%d" % i, [128, 2, NC8, 256], BF16) for i in range(2)]
            Rwgu = [Res("wgu%d" % i) for i in range(2)]
            wdt = [st("f_wd%d" % i, [128, 12, 256], BF16) for i in range(2)]
            Rwd = [Res("wd%d" % i) for i in range(2)]
            sg = [st("f_sg%d" % i, [128, TB], F32) for i in range(2)]
            Rsg = [Res("sg%d" % i) for i in range(2)]
            halves = [(0, 6), (6, 11)]
            seq = []
            for hi, (g0, g1) in enumerate(halves):
                for g in range(g0, g1): seq.append(("up", hi, g))
                for dg in range(4): seq.append(("dn", hi, dg))
            cnt = {"up": 0, "dn": 0, "gu": 0, "d": 0}
            slot_of = {}

            def load(item):
                kind, hi, g = item
                g0, g1 = halves[hi]
                if kind == "up":
                    s = cnt["up"] % 2; cnt["up"] += 1
                    slot_of[item] = s
                    for j, w in enumerate((wg, wu)):
                        k.dma("pool", wgu[s][:, j, :, :], w[l, :, g * 256:(g + 1) * 256].rearrange("(c p) f -> p c f", p=128),
                              writes=[Rwgu[s]])
                else:
                    s = cnt["dn"] % 2; cnt["dn"] += 1
                    slot_of[item] = s
                    nf = (g1 - g0) * 2
                    k.dma("pool", wdt[s][:, 0:nf, :], wd[l, g0 * 256:g1 * 256, g * 256:(g + 1) * 256].rearrange("(c p) d -> p c d", p=128),
                          writes=[Rwd[s]])

            def compute(item):
                kind, hi, g = item
                g0, g1 = halves[hi]
                s = slot_of[item]
                if kind == "up":
                    for j in range(2):
                        fi = (g - g0) * 2 + j
                        for tb in range(NTB):
                            tsl = slice(tb * TB, (tb + 1) * TB)
                            n = cnt["gu"] % 2; cnt["gu"] += 1
                            for wi, bank in ((0, n), (1, 2 + n)):
                                for c in range(NC8):
                                    mm(ps[bank][:], wgu[s][:, wi, c, j * 128:(j + 1) * 128], h[:, c, tsl], c == 0, c == NC8 - 1,
                                       [Rwgu[s], RH[c][tb]], [RP[bank]], inc=(c == NC8 - 1))
                            k.op("act", lambda: nc.scalar.activation(out=sg[n][:], in_=ps[n][:], func=AF.Silu),
                                 reads=[RP[n]], writes=[Rsg[n]])
                            k.op("dve", lambda: nc.vector.tensor_tensor(
                                out=act[:, fi, tsl], in0=sg[n][:], in1=ps[2 + n][:], op=ALU.mult),
                                reads=[Rsg[n], RP[2 + n]], writes=[RA[fi][tb]])
                else:
                    nf = (g1 - g0) * 2
                    for dj in range(2):
                        dc = g * 2 + dj
                        for tb in range(NTB):
                            tsl = slice(tb * TB, (tb + 1) * TB)
                            bank = 4 + cnt["d"] % 2; cnt["d"] += 1
                            for fi in range(nf):
                                mm(ps[bank][:], wdt[s][:, fi, dj * 128:(dj + 1) * 128], act[:, fi, tsl], fi == 0, fi == nf - 1,
                                   [Rwd[s], RA[fi][tb]], [RP[bank]], inc=(fi == nf - 1))
                            k.op("dve", lambda: nc.vector.scalar_tensor_tensor(
                                out=x_sb[:, dc, tsl], in0=ps[bank][:], scalar=0.5, in1=x_sb[:, dc, tsl],
                                op0=ALU.mult, op1=ALU.add), reads=[RP[bank], RX[dc][tb]], writes=[RX[dc][tb]])

            load(seq[0])
            rmsnorm_h(h, RH, gcol, 7)
            for i, item in enumerate(seq):
                if i + 1 < len(seq): load(seq[i + 1])
                compute(item)
            k.barrier()
            st.close()

        def mixer(l):
            w_in = P["w_in"]; w_o = P["w_o"]
            lam_init = 0.8 - 0.6 * math.exp(-0.3 * (l + 1))
            com = Scope(ar)
            hs = Scope(ar)
            h = hs("m_h", [128, NC8, S], BF16)
            RH = [[Res("mh%d_%d" % (c, t)) for t in range(NTB)] for c in range(NC8)]
            rmsnorm_h(h, RH, pk[:, l, 8:16], 7)
            warm()

            Et = [com("Et%d" % i, [128, TB], BF16) for i in range(4)]; REt = [Res("Et%d" % i) for i in range(4)]
            NUF = 4
            Uf = [com("Uf%d" % i, [65, TB], F32) for i in range(NUF)]; RU = [Res("Uf%d" % i) for i in range(NUF)]
            deferred = []

            def advance_deferred():
                for item in list(deferred):
                    item.pop(0)()
                    if not item: deferred.remove(item)

            def flush_deferred():
                while deferred:
                    advance_deferred()
            rec = [com("rec%d" % i, [1, TB], F32) for i in range(2)]; Rrec = [Res("rec%d" % i) for i in range(2)]
            sqb = [com("sqb%d" % i, [128, TB], BF16) for i in range(2)]; Rsqb = [Res("sqb%d" % i) for i in range(2)]
            rsd = [com("rsd%d" % i, [128, TB], F32) for i in range(2)]; Rrsd = [Res("rsd%d" % i) for i in range(2)]
            wsl = [com("wsl%d" % i, [128, NC8, 384], BF16) for i in range(2)]; Rwsl = [Res("wsl%d" % i) for i in range(2)]
            vaug = com("vaug", [128, 16, 6, 65], BF16); RV = [Res("vaug%d" % i) for i in range(16)]
            woa = com("woa", [65, 6, D], BF16); Rwo = Res("woa")
            lamt = com("lamt", [65, 8], F32); Rlam = Res("lamt")
            ctr = {"w": 0, "pj": 0, "st": 0, "s": 0, "e": 0, "v": 0, "o": 0, "u": 0, "b": 0, "wo": 0, "r": 0}

            k.op("dve", lambda: nc.vector.tensor_tensor(out=lamt[0:32, 0:2], in0=pk[0:32, l, 34:38:2], in1=pk[0:32, l, 35:39:2], op=ALU.mult),
                 reads=[Rpk], writes=[Rlam])
            mm(ps[7][0:65, 0:2], cf[0:32, 0:65], lamt[0:32, 0:2], True, True, [Rcf, Rlam], [RP[7]])
            k.op("act", lambda: nc.scalar.activation(out=lamt[0:65, 2:4], in_=ps[7][0:65, 0:2], func=AF.Exp), reads=[RP[7]], writes=[Rlam])
            k.op("dve", lambda: nc.vector.tensor_tensor(out=lamt[0:65, 4:5], in0=lamt[0:65, 3:4], in1=lamt[0:65, 2:3], op=ALU.subtract),
                 reads=[Rlam], writes=[Rlam])
            k.op("dve", lambda: nc.vector.tensor_scalar(out=lamt[0:65, 5:6], in0=lamt[0:65, 4:5], scalar1=float(-lam_init), scalar2=None, op0=ALU.add),
                 reads=[Rlam], writes=[Rlam])
            neglam = lamt[0:65, 5:6]

            def load_win(c0, ncol):
                s = ctr["w"] % 2; ctr["w"] += 1
                k.dma("pool", wsl[s][:, :, 0:ncol], w_in[l, :, c0:c0 + ncol].rearrange("(c p) f -> p c f", p=128), writes=[Rwsl[s]])
                return s

            def stat_rstd(src_ap, M, gmat, inv_n, src_res):
                i = ctr["st"] % 2; ctr["st"] += 1
                if src_res[0] in RP:
                    k.op("act", lambda: nc.scalar.activation(out=sqb[i][0:M, :], in_=src_ap, func=AF.Square), reads=src_res, writes=[Rsqb[i]])
                else:
                    k.op("dve", lambda: nc.vector.tensor_tensor(out=sqb[i][0:M, :], in0=src_ap, in1=src_ap, op=ALU.mult),
                         reads=src_res, writes=[Rsqb[i]])
                mm(ps[7][0:M, :], gmat, sqb[i][0:M, :], True, True, [Rcb, Rsqb[i]], [RP[7]])
                k.op("act", lambda: nc.scalar.activation(out=rsd[i][0:M, :], in_=ps[7][0:M, :], func=AF.Ln, scale=float(inv_n), bias=epsb[0:M, 0:1]),
                     reads=[RP[7], Reps], writes=[Rrsd[i]])
                k.op("act", lambda: nc.scalar.activation(out=rsd[i][0:M, :], in_=rsd[i][0:M, :], func=AF.Exp, scale=-0.5),
                     reads=[Rrsd[i]], writes=[Rrsd[i]])
                return i

            def proj_qk(slot, col0, M, tb, gmat, inv_n, gcol, dst_ap, dst_res, src_h=None):
                tsl = slice(tb * TB, (tb + 1) * TB)
                bank = 5 + ctr["pj"] % 2; ctr["pj"] += 1
                for c in range(NC8):
                    mm(ps[bank][0:M, :], wsl[slot][:, c, col0:col0 + M], h[:, c, tsl], c == 0, c == NC8 - 1,
                       [Rwsl[slot], RH[c][tb]], [RP[bank]], inc=(c == NC8 - 1))
                i = stat_rstd(ps[bank][0:M, :], M, gmat, inv_n, [RP[bank]])
                k.op("dve", lambda: nc.vector.scalar_tensor_tensor(out=dst_ap, in0=ps[bank][0:M, :], scalar=gcol, in1=rsd[i][0:M, :],
                                                                   op0=ALU.mult, op1=ALU.mult),
                     reads=[RP[bank], Rrsd[i], Rpk], writes=dst_res)

            def proj_v(lhs_fn, lhs_res_fn, rhs_ap, rhs_res, nh):
                for tcn in range(16):
                    bank = ctr["v"] % 3; ctr["v"] += 1
                    lst = lhs_fn(tcn)
                    for ci, (lap, lres) in enumerate(lst):
                        mm(ps[bank][:, 0:nh * 64], lap, rhs_ap(ci), ci == 0, ci == len(lst) - 1, [lres] + rhs_res, [RP[bank]],
                           inc=(ci == len(lst) - 1))
                    k.op("dve", lambda: nc.vector.tensor_copy(out=vaug[:, tcn, 0:nh, 1:65],
                                                              in_=ps[bank][:, 0:nh * 64].rearrange("p (h d) -> p h d", h=nh)),
                         reads=[RP[bank]], writes=[RV[tcn]])

            def init_vaug():
                k.op("dve", lambda: nc.vector.memset(vaug[:, :, :, 0:1], 1.0), writes=RV)

            def load_wo(h0, nh):
                k.op("dve", lambda: nc.vector.memset(woa[0:1, :, :], 0.0), writes=[Rwo])
                k.dma("pool", woa[1:65, 0:nh, :], w_o[l, h0 * 64:(h0 + nh) * 64, :].rearrange("(h d) o -> d h o", d=64), writes=[Rwo])

            def attn_pass(q_ap, q_res, k_fn, k_res, hl, bias_fn, bias_res, kcs, obank):
                n = len(kcs)
                sb_of = {}; e_of = {}

                def qk(i):
                    kc = kcs[i]
                    sbk = ctr["s"] % 3; ctr["s"] += 1
                    e = ctr["e"] % 4; ctr["e"] += 1
                    sb_of[i] = sbk; e_of[i] = e
                    mm(ps[sbk][:, :], k_fn(kc), q_ap, True, True, k_res + q_res, [RP[sbk]])
                    k.op("act", lambda: nc.scalar.activation(out=Et[e][:], in_=ps[sbk][:, :], func=AF.Exp), reads=[RP[sbk]], writes=[REt[e]])
                    if bias_fn is not None:
                        k.op("dve", lambda: nc.vector.tensor_tensor(out=Et[e][:], in0=Et[e][:], in1=bias_fn(kc), op=ALU.mult),
                             reads=[REt[e]] + bias_res, writes=[REt[e]])

                def av(i):
                    kc = kcs[i]; e = e_of[i]
                    mm(ps[obank][0:65, :], vaug[:, kc, hl, 0:65], Et[e][:], i == 0, i == n - 1, [RV[kc], REt[e]], [RP[obank]], inc=(i == n - 1))

                for i in range(min(2, n)): qk(i)
                for i in range(n):
                    if i + 2 < n: qk(i + 2)
                    av(i)
                    if i > 0 and i % 4 == 0: advance_deferred()

            def epi_head(obank):
                u = ctr["u"] % NUF; ctr["u"] += 1
                r = ctr["r"] % 2; ctr["r"] += 1
                k.op("dve", lambda: nc.vector.tensor_copy(out=Uf[u][:], in_=ps[obank][0:65, :]), reads=[RP[obank]], writes=[RU[u]])
                k.op("act", lambda: nc.scalar.activation(out=rec[r][:], in_=Uf[u][0:1, :], func=AF.Ln), reads=[RU[u]], writes=[Rrec[r]])
                k.op("act", lambda: nc.scalar.activation(out=rec[r][:], in_=rec[r][:], func=AF.Exp, scale=-1.0), reads=[Rrec[r]], writes=[Rrec[r]])
                return u, r

            def finish_plain(obank, dst_ap, dst_res, after=None):
                u, r = epi_head(obank)

                def st1(u=u, r=r, dst_ap=dst_ap, dst_res=dst_res):
                    mm(ps[7][0:65, :], cf[0:1, 0:65], rec[r][:], True, True, [Rcf, Rrec[r]], [RP[7]])
                    k.op("dve", lambda: nc.vector.tensor_tensor(out=dst_ap, in0=Uf[u][:], in1=ps[7][0:65, :], op=ALU.mult),
                         reads=[RU[u], RP[7]], writes=dst_res)
                deferred.append([st1] + ([after] if after is not None else []))
                return u

            def wo_apply(mix_ap_fn, mix_res, nh, qb):
                qsl = slice(qb * TB, (qb + 1) * TB)
                hl = list(range(nh)) if isinstance(nh, int) else nh
                for dc in range(NC8):
                    bank = 5 + ctr["wo"] % 2; ctr["wo"] += 1
                    for hi_, hh in enumerate(hl):
                        mm(ps[bank][:, :], woa[0:65, hh, dc * 128:(dc + 1) * 128], mix_ap_fn(hh), hi_ == 0, hi_ == len(hl) - 1,
                           [Rwo] + mix_res, [RP[bank]], inc=(hi_ == len(hl) - 1))
                    k.op("dve", lambda: nc.vector.tensor_tensor(out=x_sb[:, dc, qsl], in0=ps[bank][:, :], in1=x_sb[:, dc, qsl], op=ALU.add),
                         reads=[RP[bank], RX[dc][qb]], writes=[RX[dc][qb]])

            def mixer_diff():
                sc = Scope(ar)
                dq = [sc("dq%d" % i, [64, S], BF16) for i in range(4)]; Rdq = [[Res("dq%d_%d" % (i, t)) for t in range(NTB)] for i in range(4)]
                dk = [sc("dk%d" % i, [64, S], BF16) for i in range(4)]; Rdk = [[Res("dk%d_%d" % (i, t)) for t in range(NTB)] for i in range(4)]
                bwin = [sc("bwin%d" % i, [128, WIN], BF16) for i in range(2)]; Rbw = [Res("bwin%d" % i) for i in range(2)]
                mixT = [sc("mixT%d" % i, [65, 4, TB], BF16) for i in range(2)]; Rmx = [Res("mixT%d" % i) for i in range(2)]
                init_vaug()
                load_wo(0, 4)
                g32 = cb[0:64, C_G32:C_G32 + 64]
                s0 = load_win(0, 256)
                s1 = load_win(256, 256)
                for hd in range(4):
                    for tb in range(NTB):
                        tsl = slice(tb * TB, (tb + 1) * TB)
                        proj_qk(s0, hd * 64, 64, tb, g32, 1.0 / 32, pk[0:64, l, 24:25], dq[hd][:, tsl], [Rdq[hd][tb]])
                s2 = load_win(512, 256)
                for hd in range(4):
                    for tb in range(NTB):
                        tsl = slice(tb * TB, (tb + 1) * TB)
                        proj_qk(s1, hd * 64, 64, tb, g32, 1.0 / 32, pk[0:64, l, 25:26], dk[hd][:, tsl], [Rdk[hd][tb]])
                proj_v(lambda tcn: [(h[:, c, tcn * 128:(tcn + 1) * 128], RH[c][tcn // 4]) for c in range(NC8)], None,
                       lambda ci: wsl[s2][:, ci, 0:256], [Rwsl[s2]], 4)
                kcs = list(range(16))
                m65 = cb[0:65, C_M65:C_M65 + 65]
                warm()

                def load_bias(hd, qb):
                    s = ctr["b"] % 2; ctr["b"] += 1
                    k.dma("pool", bwin[s][:, :], P["gbias"][hd, :, qb * TB:qb * TB + WIN], writes=[Rbw[s]])
                    return s
                order = [(qb, hd) for qb in range(NTB) for hd in range(4)]
                bs = {order[0]: load_bias(order[0][1], order[0][0])}
                for oi, (qb, hd) in enumerate(order):
                    if oi + 1 < len(order):
                        nq, nh_ = order[oi + 1]
                        bs[order[oi + 1]] = load_bias(nh_, nq)
                    s = bs[(qb, hd)]
                    k.op("act", lambda: nc.scalar.activation(out=bwin[s][:, :], in_=bwin[s][:, :], func=AF.Exp), reads=[Rbw[s]], writes=[Rbw[s]])
                    qsl = slice(qb * TB, (qb + 1) * TB)
                    m = qb % 2
                    kres = [Rdk[hd][t] for t in range(NTB)]
                    us = []
                    for c2 in range(2):
                        rows = slice(c2 * 32, (c2 + 1) * 32)
                        ob = 3 + c2
                        attn_pass(dq[hd][rows, qsl], [Rdq[hd][qb]], lambda kc: dk[hd][rows, kc * 128:(kc + 1) * 128], kres, hd,
                                  lambda kc: bwin[s][:, 1920 - kc * 128:1920 - kc * 128 + TB], [Rbw[s]], kcs, ob)
                        u, r = epi_head(ob)
                        us.append((u, r))
                    (u0, r0), (u1, r1) = us

                    stt = {}

                    def st1(u0=u0, r0=r0, u1=u1, r1=r1, stt=stt):
                        mm(ps[7][0:65, :], cf[0:1, 0:65], rec[r0][:], True, True, [Rcf, Rrec[r0]], [RP[7]])
                        mm(ps[6][0:65, :], cf[0:1, 0:65], rec[r1][:], True, True, [Rcf, Rrec[r1]], [RP[6]])
                        k.op("dve", lambda: nc.vector.tensor_tensor(out=Uf[u0][:], in0=Uf[u0][:], in1=ps[7][0:65, :], op=ALU.mult),
                             reads=[RU[u0], RP[7]], writes=[RU[u0]])
                        k.op("dve", lambda: nc.vector.scalar_tensor_tensor(out=Uf[u1][:], in0=Uf[u1][:], scalar=neglam, in1=ps[6][0:65, :],
                                                                           op0=ALU.mult, op1=ALU.mult),
                             reads=[RU[u1], RP[6], Rlam], writes=[RU[u1]])
                        k.op("dve", lambda: nc.vector.tensor_tensor(out=Uf[u1][:], in0=Uf[u1][:], in1=Uf[u0][:], op=ALU.add),
                             reads=[RU[u0], RU[u1]], writes=[RU[u1]])
                        i = ctr["st"] % 2; ctr["st"] += 1; stt["i"] = i
                        k.op("dve", lambda: nc.vector.tensor_tensor(out=sqb[i][0:65, :], in0=Uf[u1][:], in1=Uf[u1][:], op=ALU.mult),
                             reads=[RU[u1]], writes=[Rsqb[i]])

                    def st2(u1=u1, m=m, hd=hd, stt=stt):
                        i = stt["i"]
                        mm(ps[7][0:65, :], m65, sqb[i][0:65, :], True, True, [Rcb, Rsqb[i]], [RP[7]])
                        k.op("act", lambda: nc.scalar.activation(out=rsd[i][0:65, :], in_=ps[7][0:65, :], func=AF.Ln, scale=1.0 / 64, bias=epsb[0:65, 0:1]),
                             reads=[RP[7], Reps], writes=[Rrsd[i]])
                        k.op("act", lambda: nc.scalar.activation(out=rsd[i][0:65, :], in_=rsd[i][0:65, :], func=AF.Exp, scale=-0.5),
                             reads=[Rrsd[i]], writes=[Rrsd[i]])
                        k.op("dve", lambda: nc.vector.scalar_tensor_tensor(out=mixT[m][:, hd, :], in0=Uf[u1][:], scalar=pk[0:65, l, 33:34], in1=rsd[i][0:65, :],
                                                                           op0=ALU.mult, op1=ALU.mult),
                             reads=[RU[u1], Rrsd[i], Rpk], writes=[Rmx[m]])

                    def st3(m=m, qb=qb):
                        wo_apply(lambda hh: mixT[m][:, hh, :], [Rmx[m]], 4, qb)
                    deferred.append([st1, st2] + ([st3] if hd == 3 else []))
                flush_deferred()
                k.barrier()
                sc.close()

            def mixer_dil_proj():
                sc = Scope(ar)
                lq = [sc("lq%d" % i, [128, S], BF16) for i in range(3)]; Rlq = [[Res("lq%d_%d" % (i, t)) for t in range(NTB)] for i in range(3)]
                lk = [sc("lk%d" % i, [128, S], BF16) for i in range(3)]; Rlk = [[Res("lk%d_%d" % (i, t)) for t in range(NTB)] for i in range(3)]
                g64 = cb[:, C_G64:C_G64 + 128]
                init_vaug()
                warm()
                s0 = load_win(768, 384)
                s1 = load_win(1152, 384)
                for pr in range(3):
                    for tb in range(NTB):
                        tsl = slice(tb * TB, (tb + 1) * TB)
                        proj_qk(s0, pr * 128, 128, tb, g64, 1.0 / 64, pk[:, l, 26:27], lq[pr][:, tsl], [Rlq[pr][tb]])
                s2 = load_win(1536, 384)
                for pr in range(3):
                    for tb in range(NTB):
                        tsl = slice(tb * TB, (tb + 1) * TB)
                        proj_qk(s1, pr * 128, 128, tb, g64, 1.0 / 64, pk[:, l, 27:28], lk[pr][:, tsl], [Rlk[pr][tb]])
                proj_v(lambda tcn: [(h[:, c, tcn * 128:(tcn + 1) * 128], RH[c][tcn // 4]) for c in range(NC8)], None,
                       lambda ci: wsl[s2][:, ci, 0:384], [Rwsl[s2]], 6)
                return sc, lq, Rlq, lk, Rlk

            def mixer_dil_attn(sc, lq, Rlq, lk, Rlk):
                bwin = [sc("bwin%d" % i, [128, WIN], BF16) for i in range(2)]; Rbw = [Res("bwin%d" % i) for i in range(2)]
                lcw = sc("lcw", [128, WIN], BF16); Rlc = Res("lcw")
                mixT = [sc("mixT%d" % i, [65, 6, TB], BF16) for i in range(2)]; Rmx = [Res("mixT%d" % i) for i in range(2)]
                load_wo(4, 6)
                warm()

                def load_bias(hd, qb):
                    s = ctr["b"] % 2; ctr["b"] += 1
                    k.dma("pool", bwin[s][:, :], P["gbias"][4 + hd, :, qb * TB:qb * TB + WIN], writes=[Rbw[s]])
                    return s
                order = [(qb, hd) for qb in range(NTB) for hd in range(6)]
                bs = {order[0]: load_bias(order[0][1], order[0][0])}
                for oi, (qb, hd) in enumerate(order):
                    if hd == 0:
                        k.dma("pool", lcw[:, :], P["logc"][:, qb * TB:qb * TB + WIN], writes=[Rlc])
                    if oi + 1 < len(order):
                        nq, nh_ = order[oi + 1]
                        bs[order[oi + 1]] = load_bias(nh_, nq)
                    s = bs[(qb, hd)]
                    k.op("dve", lambda: nc.vector.tensor_tensor(out=bwin[s][:, :], in0=bwin[s][:, :], in1=lcw[:, :], op=ALU.add),
                         reads=[Rbw[s], Rlc], writes=[Rbw[s]])
                    k.op("act", lambda: nc.scalar.activation(out=bwin[s][:, :], in_=bwin[s][:, :], func=AF.Exp), reads=[Rbw[s]], writes=[Rbw[s]])
                    qsl = slice(qb * TB, (qb + 1) * TB)
                    m = qb % 2
                    pr = hd // 2; rows = slice((hd % 2) * 64, (hd % 2) * 64 + 64)
                    kcs = [kc for kc in range(16) if -1151 <= kc * 128 - qb * TB <= 1535]
                    ob = 3 + ctr["o"] % 2; ctr["o"] += 1
                    attn_pass(lq[pr][rows, qsl], [Rlq[pr][qb]], lambda kc: lk[pr][rows, kc * 128:(kc + 1) * 128], [Rlk[pr][t] for t in range(NTB)], hd,
                              lambda kc: bwin[s][:, 1920 - kc * 128:1920 - kc * 128 + TB], [Rbw[s]], kcs, ob)
                    aft = (lambda m=m, qb=qb: wo_apply(lambda hh: mixT[m][:, hh, :], [Rmx[m]], 6, qb)) if hd == 5 else None
                    finish_plain(ob, mixT[m][:, hd, :], [Rmx[m]], after=aft)
                flush_deferred()
                k.barrier()
                sc.close()

            def mixer_mla_latents():
                sc = Scope(ar)
                qd = sc("qd", [128, 2, S], BF16); Rqd = [Res("qd%d" % t) for t in range(NTB)]
                ckv = sc("ckv", [128, S], BF16); Rckv = [Res("ckv%d" % t) for t in range(NTB)]
                krp = sc("krp", [32, S], BF16); Rkrp = [Res("krp%d" % t) for t in range(NTB)]
                s0 = load_win(1920, 256)
                s1 = load_win(2176, 160)
                for tb in range(NTB):
                    tsl = slice(tb * TB, (tb + 1) * TB)
                    for j in range(2):
                        for c in range(NC8):
                            mm(ps[5 + j][:, :], wsl[s0][:, c, j * 128:(j + 1) * 128], h[:, c, tsl], c == 0, c == NC8 - 1,
                               [Rwsl[s0], RH[c][tb]], [RP[5 + j]], inc=(c == NC8 - 1))
                    for j in range(2):
                        k.op("act", lambda: nc.scalar.activation(out=sqb[j][:, :], in_=ps[5 + j][:, :], func=AF.Square),
                             reads=[RP[5 + j]], writes=[Rsqb[j]])
                        mm(ps[7][:, :], ones_bf, sqb[j][:, :], j == 0, j == 1, [Rcb, Rsqb[j]], [RP[7]])
                    k.op("act", lambda: nc.scalar.activation(out=rsd[0][:, :], in_=ps[7][:, :], func=AF.Ln, scale=1.0 / 256, bias=epsb[:, 0:1]),
                         reads=[RP[7], Reps], writes=[Rrsd[0]])
                    k.op("act", lambda: nc.scalar.activation(out=rsd[0][:, :], in_=rsd[0][:, :], func=AF.Exp, scale=-0.5), reads=[Rrsd[0]], writes=[Rrsd[0]])
                    for j in range(2):
                        k.op("dve", lambda: nc.vector.scalar_tensor_tensor(out=qd[:, j, tsl], in0=ps[5 + j][:, :], scalar=pk[:, l, 28 + j:29 + j], in1=rsd[0][:, :],
                                                                           op0=ALU.mult, op1=ALU.mult),
                             reads=[RP[5 + j], Rrsd[0], Rpk], writes=[Rqd[tb]])
                    for c in range(NC8):
                        mm(ps[5][:, :], wsl[s1][:, c, 0:128], h[:, c, tsl], c == 0, c == NC8 - 1, [Rwsl[s1], RH[c][tb]], [RP[5]], inc=(c == NC8 - 1))
                    for c in range(NC8):
                        mm(ps[6][0:32, :], wsl[s1][:, c, 128:160], h[:, c, tsl], c == 0, c == NC8 - 1, [Rwsl[s1], RH[c][tb]], [RP[6]], inc=(c == NC8 - 1))
                    i = stat_rstd(ps[5][:, :], 128, ones_bf, 1.0 / 128, [RP[5]])
                    k.op("dve", lambda: nc.vector.scalar_tensor_tensor(out=ckv[:, tsl], in0=ps[5][:, :], scalar=pk[:, l, 30:31], in1=rsd[i][:, :],
                                                                       op0=ALU.mult, op1=ALU.mult),
                         reads=[RP[5], Rrsd[i], Rpk], writes=[Rckv[tb]])
                    k.op("dve", lambda: nc.vector.tensor_copy(out=krp[:, tsl], in_=ps[6][0:32, :]), reads=[RP[6]], writes=[Rkrp[tb]])
                return sc, qd, Rqd, ckv, Rckv, krp, Rkrp

            def mixer_mla_attn(sc, qd, Rqd, ckv, Rckv, krp, Rkrp):
                qup = sc("qup", [128, 2, 576], BF16); Rqup = Res("qup")
                knw = sc("knw", [128, 6, 96], BF16); Rknw = Res("knw")
                kvw = sc("kvw", [128, 6, 64], BF16); Rkvw = Res("kvw")
                rC = sc("ropeC", [96, S], F32); rS = sc("ropeS", [96, S], F32); Rrope = Res("rope")
                mqh = [sc("mqh%d" % i, [96, S], BF16) for i in range(2)]; Rmq = [[Res("mq%d_%d" % (i, t)) for t in range(NTB)] for i in range(2)]
                mkh = [sc("mkh%d" % i, [96, S], BF16) for i in range(2)]; Rmk = [[Res("mk%d_%d" % (i, t)) for t in range(NTB)] for i in range(2)]
                qn = [sc("qn%d" % i, [96, TB], BF16) for i in range(2)]; Rqn = [Res("qn%d" % i) for i in range(2)]
                t1 = sc("t1", [96, TB], F32); Rt1 = Res("t1")
                t2 = sc("t2", [96, TB], F32); Rt2 = Res("t2")
                mixT = [sc("mixTc%d" % i, [65, TB], BF16) for i in range(2)]; Rmx = [Res("mxc%d" % i) for i in range(2)]
                k.dma("pool", qup[:, :, :], P["mla_q_up"][l].rearrange("(c p) n -> p c n", p=128), writes=[Rqup])
                k.op("dve", lambda: nc.vector.memset(knw[:, :, :], 0.0), writes=[Rknw])
                kvv = P["mla_kv_up"][l].rearrange("p (h t d) -> p h t d", h=6, t=2)
                k.dma("pool", knw[:, :, 0:64], kvv[:, :, 0, :], writes=[Rknw])
                k.dma("pool", kvw[:, :, :], kvv[:, :, 1, :], writes=[Rkvw])
                k.dma("sp", rC[:, :], P["ropeC"][:, :], writes=[Rrope])
                k.dma("sp", rS[:, :], P["ropeS"][:, :], writes=[Rrope])
                init_vaug()
                load_wo(10, 6)
                warm()
                proj_v(lambda tcn: [(ckv[:, tcn * 128:(tcn + 1) * 128], Rckv[tcn // 4])], None,
                       lambda ci: kvw[:, :, :].rearrange("p h d -> p (h d)"), [Rkvw], 6)
                o96 = cb[0:96, C_ONES:C_ONES + 96]
                rt = cb[0:96, C_RT:C_RT + 96]
                sel = cb[0:32, C_SEL:C_SEL + 96]
                nn = {"q": 0}

                def norm_rope(bank, gcol, dst_ap, dst_res, tsl):
                    i = stat_rstd(ps[bank][0:96, :], 96, o96, 1.0 / 96, [RP[bank]])
                    j = nn["q"] % 2; nn["q"] += 1
                    k.op("dve", lambda: nc.vector.scalar_tensor_tensor(out=qn[j][:, :], in0=ps[bank][0:96, :], scalar=gcol, in1=rsd[i][0:96, :],
                                                                       op0=ALU.mult, op1=ALU.mult),
                         reads=[RP[bank], Rrsd[i], Rpk], writes=[Rqn[j]])
                    rb = 3 + nn["q"] % 2
                    mm(ps[rb][0:96, :], rt, qn[j][:, :], True, True, [Rcb, Rqn[j]], [RP[rb]])
                    k.op("dve", lambda: nc.vector.tensor_tensor(out=t1[:, :], in0=qn[j][:, :], in1=rC[:, tsl], op=ALU.mult),
                         reads=[Rqn[j], Rrope], writes=[Rt1])
                    k.op("dve", lambda: nc.vector.tensor_tensor(out=t2[:, :], in0=ps[rb][0:96, :], in1=rS[:, tsl], op=ALU.mult),
                         reads=[RP[rb], Rrope], writes=[Rt2])
                    k.op("dve", lambda: nc.vector.tensor_tensor(out=dst_ap, in0=t1[:, :], in1=t2[:, :], op=ALU.add),
                         reads=[Rt1, Rt2], writes=dst_res)

                def unit_stages(hd, sl, tb, which):
                    tsl = slice(tb * TB, (tb + 1) * TB); stt = {}
                    gcol = pk[0:96, l, 31:32] if which == "q" else pk[0:96, l, 32:33]
                    dst_ap = (mqh if which == "q" else mkh)[sl][:, tsl]
                    dst_res = [(Rmq if which == "q" else Rmk)[sl][tb]]

                    def A():
                        bank = 5 + ctr["pj"] % 2; ctr["pj"] += 1; stt["bank"] = bank
                        if which == "q":
                            for j in range(2):
                                mm(ps[bank][0:96, :], qup[:, j, hd * 96:(hd + 1) * 96], qd[:, j, tsl], j == 0, j == 1, [Rqup, Rqd[tb]], [RP[bank]], inc=(j == 1))
                        else:
                            mm(ps[bank][0:96, :], knw[:, hd, :], ckv[:, tsl], True, False, [Rknw, Rckv[tb]], [RP[bank]], inc=False)
                            mm(ps[bank][0:96, :], sel, krp[:, tsl], False, True, [Rcb, Rkrp[tb]], [RP[bank]])
                        i = ctr["st"] % 2; ctr["st"] += 1; stt["i"] = i
                        k.op("act", lambda: nc.scalar.activation(out=sqb[i][0:96, :], in_=ps[bank][0:96, :], func=AF.Square), reads=[RP[bank]], writes=[Rsqb[i]])

                    def B():
                        bank, i = stt["bank"], stt["i"]
                        mm(ps[7][0:96, :], o96, sqb[i][0:96, :], True, True, [Rcb, Rsqb[i]], [RP[7]])
                        k.op("act", lambda: nc.scalar.activation(out=rsd[i][0:96, :], in_=ps[7][0:96, :], func=AF.Ln, scale=1.0 / 96, bias=epsb[0:96, 0:1]),
                             reads=[RP[7], Reps], writes=[Rrsd[i]])
                        k.op("act", lambda: nc.scalar.activation(out=rsd[i][0:96, :], in_=rsd[i][0:96, :], func=AF.Exp, scale=-0.5), reads=[Rrsd[i]], writes=[Rrsd[i]])
                        j = nn["q"] % 2; nn["q"] += 1; stt["j"] = j
                        k.op("dve", lambda: nc.vector.scalar_tensor_tensor(out=qn[j][:, :], in0=ps[bank][0:96, :], scalar=gcol, in1=rsd[i][0:96, :],
                                                                           op0=ALU.mult, op1=ALU.mult),
                             reads=[RP[bank], Rrsd[i], Rpk], writes=[Rqn[j]])

                    def C():
                        j = stt["j"]; rb = 3 + j
                        mm(ps[rb][0:96, :], rt, qn[j][:, :], True, True, [Rcb, Rqn[j]], [RP[rb]])
                        k.op("dve", lambda: nc.vector.tensor_tensor(out=t1[:, :], in0=qn[j][:, :], in1=rC[:, tsl], op=ALU.mult), reads=[Rqn[j], Rrope], writes=[Rt1])
                        k.op("dve", lambda: nc.vector.tensor_tensor(out=t2[:, :], in0=ps[rb][0:96, :], in1=rS[:, tsl], op=ALU.mult), reads=[RP[rb], Rrope], writes=[Rt2])
                        k.op("dve", lambda: nc.vector.tensor_tensor(out=dst_ap, in0=t1[:, :], in1=t2[:, :], op=ALU.add), reads=[Rt1, Rt2], writes=dst_res)
                    return (A, B, C)

                kcs = list(range(16))
                warm()
                for hd in range(6):
                    sl = hd % 2
                    units = []
                    for tb in range(NTB):
                        for which in ("q", "k"):
                            units.append(unit_stages(hd, sl, tb, which))
                    nu = len(units)
                    for t in range(nu + 2):
                        if t < nu: units[t][0]()
                        if 0 <= t - 1 < nu: units[t - 1][1]()
                        if 0 <= t - 2 < nu: units[t - 2][2]()
                    for qb in range(NTB):
                        qsl = slice(qb * TB, (qb + 1) * TB)
                        ob = 3 + ctr["o"] % 2; ctr["o"] += 1
                        attn_pass(mqh[sl][:, qsl], [Rmq[sl][qb]], lambda kc: mkh[sl][:, kc * 128:(kc + 1) * 128], [Rmk[sl][t] for t in range(NTB)], hd,
                                  None, [], kcs, ob)
                        m = ctr["o"] % 2
                        finish_plain(ob, mixT[m][:, :], [Rmx[m]],
                                     after=(lambda m=m, hd=hd, qb=qb: wo_apply(lambda hh: mixT[m][:, :], [Rmx[m]], [hd], qb)))
                flush_deferred()
                k.barrier()
                sc.close()

            mixer_diff()
            dsc = mixer_dil_proj()
            msc = mixer_mla_latents()
            k.barrier()
            hs.close()
            mixer_dil_attn(*dsc)
            mixer_mla_attn(*msc)
            k.barrier()
            com.close()

        for stg in stages:
            kind, l = stg
            if kind == "ffn1": ffn(l, 1)
            elif kind == "ffn2": ffn(l, 2)
            elif kind == "mix": mixer(l)

        for c in range(NC8):
            k.dma("sp", yT[c * 128:(c + 1) * 128, :], x_sb[:, c, :], reads=RX[c])
        k.barrier()
        build.stats = (k.ninst, k.nwait, ar.peak)
    return nc


ALL_STAGES = [("ffn1", 0), ("mix", 0), ("ffn2", 0), ("ffn1", 1), ("mix", 1), ("ffn2", 1)]


def _gather_bias(rel_bias):
    idx = _t5_bucket(_rel_grid())
    return np.ascontiguousarray(np.transpose(rel_bias[idx], (2, 0, 1))).astype(np.float32)


def make_in_maps(inp, n_cores=8):
    cst, ropeC, ropeS, logc = _host_consts()
    pk = _pack_small(inp)
    gb = _gather_bias(np.asarray(inp["rel_bias"], np.float32))
    shared = {nm: np.ascontiguousarray(np.asarray(inp[nm], np.float32)) for nm, _ in WEIGHT_SPECS}
    shared.update({"pk": pk, "cst": cst, "ropeC": ropeC, "ropeS": ropeS, "logc": logc, "gbias": gb})
    maps = []
    for b in range(n_cores):
        m = dict(shared)
        m["xT"] = np.ascontiguousarray(np.asarray(inp["x"][b], np.float32).T)
        maps.append(m)
    return maps


def kernel(**inp):
    nc = build(ALL_STAGES)
    maps = make_in_maps(inp, 8)
    res = run_bass_kernel_spmd(nc, maps, core_ids=list(range(8)))
    out = np.stack([np.ascontiguousarray(r["yT"].T) for r in res.results], axis=0)
    return out.astype(np.float32)
```
